# Optimizing a Trainium2 kernel written in Bass

```python
import math
import jax, jax.numpy as jnp
from jax import lax
import numpy as np

D_MODEL = 1024
BATCH = 2
SEQ = 8192
DEPTH = 1

D_MIX = D_MODEL
HEAD_DIM = 64
D_ATTN = D_MIX // 2
D_CONV = D_MIX - D_ATTN
N_HEADS = D_ATTN // HEAD_DIM
N_KV_HEADS = 2
GQA_GROUP = N_HEADS // N_KV_HEADS
D_KV = N_KV_HEADS * HEAD_DIM
WINDOW = 128
BLOCK = WINDOW
CONV_WIDTH = 31
N_BUCKETS = 32
MAX_DISTANCE = 128
D_FF = 2816
FFN_CONV_WIDTH = 3
LN_EPS = 1e-5
DEEPNORM_ALPHA = (2.0 * DEPTH) ** 0.25
DEEPNORM_BETA = (8.0 * DEPTH) ** -0.25

Q_END = D_ATTN
K_END = Q_END + D_KV
V_END = K_END + D_KV
A_END = V_END + D_CONV
D_IN = A_END + D_CONV

kernel_name = "hybrid_swa_sink_conformer_convffn_deepnorm"


def layer_norm(x, g, b):
    xf = x.astype(jnp.float32)
    mu = jnp.mean(xf, axis=-1, keepdims=True)
    var = jnp.mean(jnp.square(xf - mu), axis=-1, keepdims=True)
    y = (xf - mu) * lax.rsqrt(var + LN_EPS)
    return (y * g.astype(jnp.float32) + b.astype(jnp.float32)).astype(x.dtype)


def rms_norm(x, g):
    xf = x.astype(jnp.float32)
    y = xf * lax.rsqrt(jnp.mean(jnp.square(xf), axis=-1, keepdims=True) + LN_EPS)
    return (y * g.astype(jnp.float32)).astype(x.dtype)


def causal_depthwise_conv(x, w, b):
    k, c = w.shape
    y = lax.conv_general_dilated(
        x, w[:, None, :].astype(x.dtype), window_strides=(1,), padding=[(k - 1, 0)],
        dimension_numbers=("NWC", "WIO", "NWC"), feature_group_count=c)
    return y + b


def t5_causal_bucket(n):
    max_exact = N_BUCKETS // 2
    nf = jnp.maximum(n, max_exact).astype(jnp.float32)
    large = max_exact + (jnp.log(nf / max_exact) / math.log(MAX_DISTANCE / max_exact)
                         * (N_BUCKETS - max_exact)).astype(jnp.int32)
    large = jnp.minimum(large, N_BUCKETS - 1)
    return jnp.where(n < max_exact, n, large)


def sliding_window_gqa(q, k, v, sinks, rel_bias_table):
    b, s = q.shape[:2]
    nb = s // BLOCK
    qb = q.reshape(b, nb, BLOCK, N_KV_HEADS, GQA_GROUP, HEAD_DIM)

    def band(t):
        t = t.reshape(b, s, N_KV_HEADS, HEAD_DIM)
        tp = jnp.pad(t, ((0, 0), (BLOCK, 0), (0, 0), (0, 0)))
        tp = tp.reshape(b, nb + 1, BLOCK, N_KV_HEADS, HEAD_DIM)
        return jnp.concatenate([tp[:, :-1], tp[:, 1:]], axis=2)

    kb, vb = band(k), band(v)
    scale = HEAD_DIM ** -0.5
    scores = jnp.einsum("bnqhgd,bnkhd->bhgnqk", qb, kb).astype(jnp.float32) * scale

    qi = jnp.arange(BLOCK)[:, None]
    kj = jnp.arange(2 * BLOCK)[None, :]
    dist = qi + BLOCK - kj
    band_ok = (dist >= 0) & (dist < WINDOW)
    bias = rel_bias_table[t5_causal_bucket(jnp.maximum(dist, 0))]
    bias = bias.astype(jnp.float32).transpose(2, 0, 1).reshape(
        N_KV_HEADS, GQA_GROUP, BLOCK, 2 * BLOCK)
    key_pos = jnp.arange(nb)[:, None] * BLOCK - BLOCK + jnp.arange(2 * BLOCK)[None, :]
    mask = band_ok[None] & (key_pos >= 0)[:, None, :]

    scores = jnp.where(mask, scores + bias[:, :, None], -jnp.inf)
    sink = sinks.astype(jnp.float32).reshape(N_KV_HEADS, GQA_GROUP)[None, :, :, None, None, None]
    m = jnp.maximum(jnp.max(scores, axis=-1, keepdims=True), sink)
    p = jnp.exp(scores - m)
    denom = jnp.sum(p, axis=-1, keepdims=True) + jnp.exp(sink - m)
    probs = (p / denom).astype(v.dtype)
    out = jnp.einsum("bhgnqk,bnkhd->bnqhgd", probs, vb)
    return out.reshape(b, s, D_ATTN)


def conformer_conv_group(a, gate, dw_w, dw_b, ln_g, ln_b):
    h = a * jax.nn.sigmoid(gate)
    h = causal_depthwise_conv(h, dw_w, dw_b)
    h = layer_norm(h, ln_g, ln_b)
    return jax.nn.silu(h)


def hybrid_mixer(x, w_in, b_in, sinks, rel_bias_table, conv_dw_w, conv_dw_b,
                 conv_ln_g, conv_ln_b, attn_out_gain, conv_out_gain, w_out, b_out):
    proj = x @ w_in + b_in
    q, k, v, a, gate = jnp.split(proj, [Q_END, K_END, V_END, A_END], axis=-1)
    y_attn = sliding_window_gqa(q, k, v, sinks, rel_bias_table)
    y_conv = conformer_conv_group(a, gate, conv_dw_w, conv_dw_b, conv_ln_g, conv_ln_b)
    y = jnp.concatenate([rms_norm(y_attn, attn_out_gain),
                         rms_norm(y_conv, conv_out_gain)], axis=-1)
    return y @ w_out + b_out


def conv_ffn(x, w_up, dw_w, dw_b, w_down):
    h = causal_depthwise_conv(x @ w_up, dw_w, dw_b)
    g, u = jnp.split(h, 2, axis=-1)
    return (jax.nn.silu(g) * u) @ w_down


def setup_inputs(seed: int = 0) -> dict:
    key = jax.random.key(seed)
    ks = jax.random.split(key, 24)
    f32 = jnp.float32
    L = DEPTH

    def nrm(k, shape, scale):
        return jax.random.normal(k, shape, f32) * scale

    return {
        "x": nrm(ks[0], (BATCH, SEQ, D_MODEL), 1.0),
        "w_in": nrm(ks[1], (L, D_MODEL, D_IN), D_MODEL ** -0.5),
        "b_in": nrm(ks[2], (L, D_IN), 0.02),
        "attn_sinks": nrm(ks[3], (L, N_HEADS), 1.0),
        "rel_bias_table": nrm(ks[4], (N_BUCKETS, N_HEADS), 0.5),
        "conv_dw_w": nrm(ks[5], (L, CONV_WIDTH, D_CONV), CONV_WIDTH ** -0.5),
        "conv_dw_b": nrm(ks[6], (L, D_CONV), 0.02),
        "conv_ln_g": 1.0 + nrm(ks[7], (L, D_CONV), 0.02),
        "conv_ln_b": nrm(ks[8], (L, D_CONV), 0.02),
        "attn_out_gain": 1.0 + nrm(ks[9], (L, D_ATTN), 0.02),
        "conv_out_gain": 1.0 + nrm(ks[10], (L, D_CONV), 0.02),
        "w_out": nrm(ks[11], (L, D_MIX, D_MODEL), D_MIX ** -0.5 * DEEPNORM_BETA),
        "b_out": nrm(ks[12], (L, D_MODEL), 0.02),
        "ln1_g": 1.0 + nrm(ks[13], (L, D_MODEL), 0.02),
        "ln1_b": nrm(ks[14], (L, D_MODEL), 0.02),
        "w_up": nrm(ks[15], (L, D_MODEL, 2 * D_FF), D_MODEL ** -0.5),
        "ffn_dw_w": nrm(ks[16], (L, FFN_CONV_WIDTH, 2 * D_FF), FFN_CONV_WIDTH ** -0.5),
        "ffn_dw_b": nrm(ks[17], (L, 2 * D_FF), 0.02),
        "w_down": nrm(ks[18], (L, D_FF, D_MODEL), D_FF ** -0.5 * DEEPNORM_BETA),
        "ln2_g": 1.0 + nrm(ks[19], (L, D_MODEL), 0.02),
        "ln2_b": nrm(ks[20], (L, D_MODEL), 0.02),
    }


def reference(x, w_in, b_in, attn_sinks, rel_bias_table, conv_dw_w, conv_dw_b,
              conv_ln_g, conv_ln_b, attn_out_gain, conv_out_gain, w_out, b_out,
              ln1_g, ln1_b, w_up, ffn_dw_w, ffn_dw_b, w_down, ln2_g, ln2_b):
    for l in range(DEPTH):
        mix = hybrid_mixer(x, w_in[l], b_in[l], attn_sinks[l], rel_bias_table,
                           conv_dw_w[l], conv_dw_b[l], conv_ln_g[l], conv_ln_b[l],
                           attn_out_gain[l], conv_out_gain[l], w_out[l], b_out[l])
        x = layer_norm(DEEPNORM_ALPHA * x + mix, ln1_g[l], ln1_b[l])
        ffn = conv_ffn(x, w_up[l], ffn_dw_w[l], ffn_dw_b[l], w_down[l])
        x = layer_norm(DEEPNORM_ALPHA * x + ffn, ln2_g[l], ln2_b[l])
    return x
```

```python
import math
import numpy as np
import concourse.bass as bass
import concourse.mybir as mybir
from concourse.bass_utils import run_bass_kernel_spmd

F32 = mybir.dt.float32
BF16 = mybir.dt.bfloat16
AF = mybir.ActivationFunctionType
ALU = mybir.AluOpType


class Res:
    __slots__ = ("name", "lw", "rd", "lws")

    def __init__(self, name):
        self.name = name
        self.lw = None
        self.lws = []
        self.rd = []


class Lane:
    def __init__(self, name, sem, unit):
        self.name, self.sem, self.unit, self.count = name, sem, unit, 0


class Eng:
    def __init__(self, name, lane):
        self.name, self.lane, self.ops, self.known = name, lane, [], {}


class _Rec:
    def __getattr__(self, name):
        return lambda *a, **k: (name, a, k)


class Prog:
    def __init__(self, nc):
        self.nc = nc
        self.eng = {}
        for n in ("pe", "act", "dve", "pool", "sp"):
            self.eng[n] = Eng(n, Lane(n, nc.alloc_semaphore("s_" + n), 1))
        self.dma_lanes = {}
        self.lane_rr = {}
        self.nres = 0

    def res(self, name=None):
        self.nres += 1
        return Res(name or f"r{self.nres}")

    LANES = {"ld_c": 4, "ld_w": 8, "ld_w2": 4, "ld_x": 8, "ld_xt": 3, "st_x1": 2, "ld_wu": 12, "ld_wd": 6,
             "ld_x1b": 3, "ld_xr": 3, "st_out": 2, "ld_c2": 2}

    def lane(self, name):
        k = self.LANES.get(name, 2)
        i = self.lane_rr.get(name, 0)
        self.lane_rr[name] = i + 1
        key = f"{name}{i % k}"
        if key not in self.dma_lanes:
            self.dma_lanes[key] = Lane(key, self.nc.alloc_semaphore("d_" + key), 16)
        return self.dma_lanes[key]

    def _deps(self, e, reads, writes, par=False):
        need = {}

        def add(t):
            if t is not None and need.get(t[0], 0) < t[1]:
                need[t[0]] = t[1]
        for r in reads:
            add(r.lw)
            for t in r.lws:
                add(t)
        for w in writes:
            for t in w.rd:
                add(t)
            if par and not w.rd:
                continue
            add(w.lw)
            for t in w.lws:
                add(t)
        waits = []
        for ln, idx in need.items():
            if ln is e.lane and e.name == "pe":
                continue
            if e.known.get(ln, 0) >= idx:
                continue
            e.known[ln] = idx
            waits.append((ln.sem, idx * ln.unit))
        return waits

    def _mark(self, t, reads, writes, par=False):
        for r in reads:
            r.rd.append(t)
        for w in writes:
            if par:
                if w.rd:
                    w.lw, w.lws, w.rd = None, [], []
                w.lws.append(t)
            else:
                w.lw, w.lws, w.rd = t, [], []

    def op(self, en, fn, reads=(), writes=(), inc=True):
        name, a, k = fn(_Rec())
        fn = (lambda eng, name=name, a=a, k=k: getattr(eng, name)(*a, **k))
        e = self.eng[en]
        waits = self._deps(e, reads, writes)
        ln = e.lane
        if inc:
            ln.count += 1
            idx = ln.count
        else:
            idx = ln.count + 1
        e.ops.append((waits, fn, ln if inc else None, 1))
        self._mark((ln, idx), reads, writes)

    def dma(self, qn, lane_name, out, in_, reads=(), writes=(), par=False):
        e = self.eng[qn]
        ln = self.lane(lane_name)
        waits = self._deps(e, reads, writes, par)
        if ln.count and e.known.get(ln, 0) < ln.count:
            e.known[ln] = ln.count
            waits.append((ln.sem, ln.count * ln.unit))
        ln.count += 1
        e.ops.append((waits, (lambda eng, o=out, i=in_: eng.dma_start(out=o, in_=i)), ln, 16))
        self._mark((ln, ln.count), reads, writes, par)

    def barrier(self):
        lanes = [e.lane for e in self.eng.values()] + list(self.dma_lanes.values())
        for e in self.eng.values():
            waits = []
            for ln in lanes:
                if ln is e.lane or ln.count == 0:
                    continue
                if e.known.get(ln, 0) < ln.count:
                    e.known[ln] = ln.count
                    waits.append((ln.sem, ln.count * ln.unit))
            e.ops.append((waits, None, None, 0))

    def wait_all(self, en):
        e = self.eng[en]
        waits = []
        for ln in self.dma_lanes.values():
            if ln.count and e.known.get(ln, 0) < ln.count:
                e.known[ln] = ln.count
                waits.append((ln.sem, ln.count * ln.unit))
        e.ops.append((waits, None, None, 0))

    def emit(self):
        with self.nc.Block() as block:
            def run(e, eng):
                for waits, fn, ln, amt in e.ops:
                    for sem, val in waits:
                        eng.wait_ge(sem, val)
                    if fn is None:
                        continue
                    ins = fn(eng)
                    if ln is not None:
                        ins.then_inc(ln.sem, amt)

            @block.tensor
            def _(eng):
                run(self.eng["pe"], eng)

            @block.scalar
            def _(eng):
                run(self.eng["act"], eng)

            @block.vector
            def _(eng):
                run(self.eng["dve"], eng)

            @block.gpsimd
            def _(eng):
                run(self.eng["pool"], eng)

            @block.sync
            def _(eng):
                run(self.eng["sp"], eng)


class Arena:
    def __init__(self, nc, words):
        self.t = nc.alloc_sbuf_tensor("arena", [128, words], F32)
        self.words = words
        self.off = 0

    def _take(self, n):
        o = self.off
        self.off += n
        assert self.off <= self.words, f"arena overflow {self.off * 4} > {self.words * 4}"
        return o

    @staticmethod
    def _shape(v, dims):
        if len(dims) == 1:
            return v
        names = " ".join(f"d{i}" for i in range(len(dims)))
        kw = {f"d{i}": d for i, d in enumerate(dims[:-1])}
        return v.rearrange(f"p ({names}) -> p {names}", **kw)

    def f32(self, *dims):
        n = int(np.prod(dims))
        o = self._take(n)
        return self._shape(self.t[:, o:o + n], dims)

    def bf16(self, *dims):
        n = int(np.prod(dims))
        n4 = (n + 1) // 2
        o = self._take(n4)
        v = self.t[:, o:o + n4].bitcast(BF16)[:, 0:n]
        return self._shape(v, dims)


D = 1024
NTOK = 2304
NBLK = 18
OWN = 2048
DFF = 2816
NFF = 22
ALPHA = float(2.0 ** 0.25)
EPS = 1e-5
W_IN_COLS = 14 * 128 + 128
PC_IN, PC_CONV, PC_CW, PC_FW, PC_FB = 0, 14, 30, 154, 286
PC_N = 330

DEBUG = False
KA, KB, KSUB = 99, 99, 9

def build():
    nc = bass.Bass("TRN2", target_bir_lowering=False)
    dt = lambda n, s, k="ExternalInput": nc.dram_tensor(n, s, F32, kind=k).ap()
    xh_d = dt("xh", [NTOK, D])
    xT_d = dt("xT", [D, NTOK])
    flag_d = dt("flag", [128, 1])
    w_in_d = dt("w_in", [D, 1792])
    w_out_d = dt("w_out", [D, D])
    w_up_d = dt("w_up", [D, 2 * DFF])
    w_down_d = dt("w_down", [DFF, D])
    pcol_d = dt("pcol", [128, PC_N])
    bv_d = dt("bv", [1, 128])
    again_d = dt("again", [1, 512])
    bout_d = dt("bout", [1, D])
    ln1g_d = dt("ln1g", [1, D])
    ln1b_d = dt("ln1b", [1, D])
    ln2g_d = dt("ln2g", [1, D])
    ln2b_d = dt("ln2b", [1, D])
    sinks_d = dt("sinks", [1, 8])
    biasg_d = dt("biasg", [128, 2 * 8 * 128])
    maskc_d = dt("maskc", [128, 2 * 128])
    out_d = dt("out", [OWN, D], "ExternalOutput")
    x1s_d = dt("x1s", [NTOK - 128, D], "ExternalOutput" if DEBUG else "Internal")

    P = Prog(nc)
    A = Arena(nc, 53000)
    psum = nc.alloc_psum_tensor("psum", [128, 8, 512], F32)
    R = P.res
    Rps = [R(f"ps{i}") for i in range(8)]

    pcol = A.f32(PC_N); Rpcol = R()
    flag = A.f32(1); Rflag = R()
    ident = A.bf16(128); Rident = R()
    ones = A.f32(128); Rones = R()
    mhalf = A.f32(512); Rmhalf = R()
    halfb = A.f32(8); Rhalfb = R()
    small = A.f32(64); Rsmall = R()
    identf = A.f32(128); Ridentf = R()
    rdt = A.f32(8); Rrdt = R()
    keep_mark = A.off

    P.dma("sp", "ld_c", pcol, pcol_d, writes=[Rpcol])
    P.dma("sp", "ld_c", flag, flag_d, writes=[Rflag])
    P.op("pool", lambda e: e.memset(ident, 0.0), writes=[Rident])
    P.op("pool", lambda e: e.affine_select(out=ident, in_=ident, compare_op=ALU.not_equal, fill=1.0,
                                          base=0, pattern=[[-1, 128]], channel_multiplier=1),
         reads=[Rident], writes=[Rident])
    P.op("pool", lambda e: e.memset(ones, 1.0), writes=[Rones])
    P.op("pool", lambda e: e.tensor_copy(out=identf, in_=ident), reads=[Rident], writes=[Ridentf])
    P.op("pool", lambda e: e.memset(mhalf, -0.5), writes=[Rmhalf])
    P.op("dve", lambda e: e.tensor_scalar(out=halfb[:, 0:4], in0=pcol[:, PC_IN + 6:PC_IN + 10], scalar1=0.5, scalar2=None, op0=ALU.mult),
         reads=[Rpcol], writes=[Rhalfb])
    P.op("dve", lambda e: e.tensor_scalar(out=halfb[:, 4:8], in0=pcol[:, PC_IN + 10:PC_IN + 14], scalar1=0.5, scalar2=None, op0=ALU.mult),
         reads=[Rpcol], writes=[Rhalfb])

    w_in = A.bf16(8, W_IN_COLS); Rw_in = R()
    diag = A.bf16(124, 128); Rdiag = [R() for _ in range(4)]
    xT = [A.bf16(8, 512) for _ in range(2)]; RxT = [R(), R()]
    hT = [A.bf16(4, 30 + 512) for _ in range(2)]; RhT = [R(), R()]
    cf = A.f32(4, 512); Rcf = [R() for _ in range(4)]
    sq = [A.f32(512) for _ in range(2)]; Rsq = [R(), R()]
    rstd = A.f32(512); Rrstd = R()
    early_free_end = A.off
    EARLY_RES = [Rw_in] + Rdiag + RxT + RhT + Rcf + Rsq + [Rrstd]
    w_out = A.bf16(8, D); Rw_out = R()
    kT = A.bf16(2, 2, NTOK); RkT = R()
    vaug = A.bf16(NBLK, 2, 65); Rvaug = [R() for _ in range(NBLK)]
    EB = A.f32(2, 8, 128); REB = R()
    esink = A.f32(8); Resink = R()
    bv_bc = A.f32(128); Rbv = R()
    again_bc = A.f32(512); Ragain = R()
    bout_bc = A.f32(D); Rbout = R()
    ln1g_bc = A.f32(D); Rln1g = R()
    ln1b_bc = A.f32(D); Rln1b = R()
    qT = A.bf16(4, 512); RqT = R()
    Tb = A.f32(512); RTb = R()
    maskc = Tb[:, 0:256].rearrange("p (a q) -> p a q", a=2); Rmask = RTb
    ab = A.f32(512); Rab = R()
    mean = A.f32(512); Rmean = R()
    tmpf = A.f32(512); Rtmpf = R()
    yT = A.bf16(8, 512); RyT = [R() for _ in range(4)]
    Esb = [Tb, ab]; REsb = [RTb, Rab]
    Pt = [A.bf16(2, 2, 512) for _ in range(2)]
    RPt = [[[R(), R()], [R(), R()]] for _ in range(2)]
    yat = mean; Ryat = Rmean
    junk = tmpf; Rjunk = Rtmpf
    ya = A.bf16(512); Rya = R()
    xt = [A.f32(D) for _ in range(3)]; Rxt = [R(), R(), R()]
    lnst1 = A.f32(3, 16); Rlnst1 = [R(), R(), R()]
    att = A.f32(2, 16); Ratt = [R(), R()]

    wq = "pool"
    for k in range(8):
        P.dma(wq, "ld_x", xT[0][:, k, 0:256], xT_d[k * 128:(k + 1) * 128, 0:256], writes=[RxT[0]], par=True)
    _save = A.off
    A.off = keep_mark
    w_up = A.bf16(NFF, 8, 256); Rw_up = [R() for _ in range(NFF)]
    A.off = _save
    N_EARLY = min(NFF, (early_free_end - keep_mark) // 1024)
    w_up_v = w_up_d.rearrange("(k p) n -> p k n", p=128)

    def load_wup(p_, extra):
        for gi, off in enumerate((0, DFF)):
            c1 = off + p_ * 128
            P.dma(wq, "ld_wu", w_up[:, p_, :, gi * 128:(gi + 1) * 128], w_up_v[:, :, c1:c1 + 128],
                  writes=[Rw_up[p_]] + (extra if gi == 0 else []), par=True)
    w_in_v = w_in_d.rearrange("(k p) n -> p k n", p=128)
    Rw_ag = R()
    segs = [(768, 768, 512, Rw_ag), (1280, 1280, 512, Rw_ag), (0, 0, 512, Rw_in), (512, 512, 64, Rw_in), (576, 512, 64, Rw_in),
            (640, 576, 64, Rw_in), (704, 576, 64, Rw_in), (1792, 640, 128, Rw_in)]
    for si, (dst, src, n, rr) in enumerate(segs):
        if si == 2:
            P.op("pool", lambda e: e.memset(hT[0][:, :, 0:30], 0.0), writes=[RhT[0]])
            P.op("pool", lambda e: e.memset(kT[64:128, :, 0, :], 0.0), writes=[RkT])
            P.op("pool", lambda e: e.memset(kT[0:64, :, 1, :], 0.0), writes=[RkT])
        for k in range(8):
            P.dma(wq, "ld_w", w_in[:, k, dst:dst + n], w_in_v[:, k, src:src + n], writes=[rr], par=True)
    for k in range(8):
        P.dma(wq, "ld_w2", w_out[:, k, :], w_out_d[k * 128:(k + 1) * 128, :], writes=[Rw_out], par=True)

    P.dma("sp", "ld_c", bv_bc, bv_d.partition_broadcast(128), writes=[Rbv])
    P.dma("sp", "ld_c", again_bc, again_d.partition_broadcast(128), writes=[Ragain])
    P.dma("sp", "ld_c", bout_bc, bout_d.partition_broadcast(128), writes=[Rbout])
    P.dma("sp", "ld_c", ln1g_bc, ln1g_d.partition_broadcast(128), writes=[Rln1g])
    P.dma("sp", "ld_c", ln1b_bc, ln1b_d.partition_broadcast(128), writes=[Rln1b])
    P.dma("sp", "ld_c", esink, sinks_d.partition_broadcast(128), writes=[Resink])
    P.dma("sp", "ld_c", EB, biasg_d.rearrange("p (a h q) -> p a h q", a=2, h=8), writes=[REB])
    P.dma("sp", "ld_c", maskc, maskc_d.rearrange("p (a q) -> p a q", a=2), writes=[Rmask])
    P.op("act", lambda e: e.activation(out=esink, in_=esink, func=AF.Exp), reads=[Resink], writes=[Resink])
    P.op("act", lambda e: e.activation(out=EB, in_=EB, func=AF.Exp), reads=[REB], writes=[REB])
    for kb in range(2):
        P.op("dve", lambda e, kb=kb: e.tensor_tensor(out=EB[:, kb], in0=EB[:, kb],
                                                    in1=maskc[:, kb].unsqueeze(1).broadcast_to([128, 8, 128]), op=ALU.mult),
             reads=[REB, Rmask], writes=[REB])
    def build_diag(c):
        for j in range(31):
            i = c * 31 + j
            if j % 2 == 0:
                P.op("act", lambda e, i=i: e.activation(out=diag[:, i, :], in_=ident, func=AF.Identity,
                                                        scale=pcol[:, PC_CW + i:PC_CW + i + 1]),
                     reads=[Rident, Rpcol], writes=[Rdiag[c]])
            else:
                P.op("dve", lambda e, i=i: e.tensor_scalar(out=diag[:, i, :], in0=ident, scalar1=pcol[:, PC_CW + i:PC_CW + i + 1],
                                                           scalar2=None, op0=ALU.mult),
                     reads=[Rident, Rpcol], writes=[Rdiag[c]])
    P.op("dve", lambda e: e.memset(vaug[:, :, :, 64:65], 1.0), writes=Rvaug)

    ps_fm = [0, 1]
    ps_sc = [2, 3]
    ps_pv = [4, 5]
    ps_op = [6, 7]
    fm_ctr = [0]

    def fm_bank():
        b = ps_fm[fm_ctr[0] % 2]
        fm_ctr[0] += 1
        return b

    def rstd_from_var(NM_, bank):
        nbk = NM_ // 128
        t3 = tmpf[:, 0:NM_].rearrange("p (j t) -> p j t", j=nbk)
        P.op("dve", lambda e: e.tensor_tensor(out=t3, in0=t3, in1=identf.unsqueeze(1).broadcast_to([128, nbk, 128]), op=ALU.mult),
             reads=[Rtmpf, Ridentf], writes=[Rtmpf])
        P.op("dve", lambda e: e.tensor_reduce(out=rdt[:, 0:nbk], in_=t3, axis=mybir.AxisListType.X, op=ALU.add),
             reads=[Rtmpf], writes=[Rrdt])
        P.op("pool", lambda e: e.tensor_tensor(out=rdt[:, 4:4 + nbk], in0=rdt[:, 0:nbk], in1=mhalf[:, 0:nbk], op=ALU.pow),
             reads=[Rrdt, Rmhalf], writes=[Rrdt])
        P.op("dve", lambda e: e.tensor_tensor(out=t3, in0=identf.unsqueeze(1).broadcast_to([128, nbk, 128]),
                                              in1=rdt[:, 4:4 + nbk].unsqueeze(2).broadcast_to([128, nbk, 128]), op=ALU.mult),
             reads=[Ridentf, Rrdt], writes=[Rtmpf])
        for j_ in range(nbk):
            P.op("pe", lambda e, j_=j_: e.matmul(psum[:, bank, j_ * 128:(j_ + 1) * 128], lhsT=ones, rhs=tmpf[:, j_ * 128:(j_ + 1) * 128],
                                                 start=True, stop=True),
                 reads=[Rones, Rtmpf], writes=[Rps[bank]], inc=(j_ == nbk - 1))
        P.op("act", lambda e: e.activation(out=rstd[:, 0:NM_], in_=psum[:, bank, 0:NM_], func=AF.Identity),
             reads=[Rps[bank]], writes=[Rrstd])

    tiles = ([(0, 2)] + [(2 + 4 * i, 4) for i in range(4)])[:KA]

    def load_xT(ti_):
        b0_, nb_ = tiles[ti_]
        for k in range(8):
            P.dma(wq, "ld_x", xT[ti_ % 2][:, k, 0:nb_ * 128], xT_d[k * 128:(k + 1) * 128, b0_ * 128:(b0_ + nb_) * 128],
                  writes=[RxT[ti_ % 2]], par=True)

    carry = [None]

    def issue_xt(bi_):
        if 1 <= bi_ < NBLK:
            P.dma("sp", "ld_xt", xt[bi_ % 3], xh_d[bi_ * 128:(bi_ + 1) * 128, :], writes=[Rxt[bi_ % 3]])
    for bi_ in (1, 2, 3):
        issue_xt(bi_)
    for ti, (b0, nb) in enumerate(tiles):
        N = nb * 128
        t0 = b0 * 128
        xb = xT[ti % 2]; Rxb = RxT[ti % 2]
        hb = hT[ti % 2]; Rhb = RhT[ti % 2]
        if ti == 0:
            if len(tiles) > 1:
                load_xT(1)

        def proj_chunk(ch):
            b = fm_bank()
            for k in range(8):
                P.op("pe", lambda e, b=b, k=k, ch=ch: e.matmul(psum[:, b, 0:N], lhsT=w_in[:, k, ch * 128:(ch + 1) * 128],
                                                               rhs=xb[:, k, 0:N], start=(k == 0), stop=(k == 7)),
                     reads=[Rw_ag if ch >= 6 else Rw_in, Rxb], writes=[Rps[b]], inc=(k == 7))
            return b
        def emit_qk():
            for c in range(4):
                b = proj_chunk(c)
                P.op("act", lambda e, b=b, c=c: e.activation(out=qT[:, c, 0:N], in_=psum[:, b, 0:N], func=AF.Identity,
                                                             bias=pcol[:, PC_IN + c:PC_IN + c + 1]),
                     reads=[Rps[b], Rpcol], writes=[RqT])
            for kv in range(2):
                b = proj_chunk(4 + kv)
                for hh in range(2):
                    pr = slice(hh * 64, (hh + 1) * 64)
                    P.op("act", lambda e, b=b, kv=kv, hh=hh, pr=pr: e.activation(
                        out=kT[pr, kv, hh, t0:t0 + N], in_=psum[pr, b, 0:N], func=AF.Identity,
                        bias=pcol[pr, PC_IN + 4 + kv:PC_IN + 5 + kv]),
                        reads=[Rps[b], Rpcol], writes=[RkT])

        if ti > 0:
            pb = hT[(ti - 1) % 2]; Npv = tiles[ti - 1][1] * 128
            P.op("pool", lambda e, pb=pb, Npv=Npv: e.tensor_copy(out=hb[:, :, 0:30], in_=pb[:, :, Npv:Npv + 30]),
                 reads=[RhT[(ti - 1) % 2]], writes=[Rhb])
        for c in range(4):
            b = proj_chunk(6 + c)
            P.op("act", lambda e, b=b, c=c: e.activation(out=ab[:, 0:N], in_=psum[:, b, 0:N], func=AF.Identity,
                                                         scale=0.5, bias=halfb[:, c:c + 1]),
                 reads=[Rps[b], Rhalfb], writes=[Rab])
            b2 = proj_chunk(10 + c)
            P.op("act", lambda e, b2=b2, c=c: e.activation(out=Tb[:, 0:N], in_=psum[:, b2, 0:N], func=AF.Tanh,
                                                           scale=0.5, bias=halfb[:, 4 + c:5 + c]),
                 reads=[Rps[b2], Rhalfb], writes=[RTb])
            P.op("dve", lambda e, c=c: e.scalar_tensor_tensor(out=hb[:, c, 30:30 + N], in0=Tb[:, 0:N], scalar=1.0, in1=ab[:, 0:N],
                                                              op0=ALU.add, op1=ALU.mult),
                 reads=[RTb, Rab], writes=[Rhb])
            if ti == 0:
                build_diag(c)
        if ti == 0:
            P.op("dve", lambda e: e.tensor_scalar(out=hb[:, :, 30:30 + N], in0=hb[:, :, 30:30 + N], scalar1=flag[:, 0:1],
                                                  scalar2=None, op0=ALU.mult),
                 reads=[Rhb, Rflag], writes=[Rhb])
        def emit_v():
            for j in range(nb):
                bi = b0 + j
                b = fm_bank()
                for k in range(8):
                    P.op("pe", lambda e, b=b, k=k, j=j: e.matmul(psum[:, b, 0:128], lhsT=xb[:, k, j * 128:(j + 1) * 128],
                                                                 rhs=w_in[:, k, 1792:1920], start=(k == 0), stop=(k == 7)),
                         reads=[Rw_in, Rxb], writes=[Rps[b]], inc=(k == 7))
                P.op("dve", lambda e, b=b, bi=bi: e.tensor_tensor(out=vaug[:, bi, :, 0:64],
                                                                  in0=psum[:, b, 0:128].rearrange("p (a d) -> p a d", a=2),
                                                                  in1=bv_bc.rearrange("p (a d) -> p a d", a=2), op=ALU.add),
                     reads=[Rps[b], Rbv], writes=[Rvaug[bi]])
                if ti == 0:
                    P.op("dve", lambda e, bi=bi: e.tensor_scalar(out=vaug[:, bi], in0=vaug[:, bi], scalar1=flag[:, 0:1],
                                                                 scalar2=None, op0=ALU.mult),
                         reads=[Rvaug[bi], Rflag], writes=[Rvaug[bi]])

        if carry[0] is not None:
            carry[0]()
            carry[0] = None
        c0 = 128 if ti == 0 else 0
        NM = N - c0
        for c in range(4):
            b = fm_bank()
            for j in range(31):
                P.op("pe", lambda e, b=b, c=c, j=j: e.matmul(psum[:, b, 0:NM], lhsT=diag[:, c * 31 + j, :],
                                                             rhs=hb[:, c, c0 + j:c0 + j + NM], start=(j == 0), stop=(j == 30)),
                     reads=[Rdiag[c], Rhb], writes=[Rps[b]], inc=(j == 30))
            P.op("act", lambda e, b=b, c=c: e.activation(out=cf[:, c, 0:NM], in_=psum[:, b, 0:NM], func=AF.Identity,
                                                         bias=pcol[:, PC_CONV + c:PC_CONV + c + 1]),
                 reads=[Rps[b], Rpcol], writes=[Rcf[c]])
        emit_qk()
        emit_v()
        def stats_gen():
            bS1 = fm_bank()
            for c in range(4):
                P.op("pe", lambda e, c=c: e.matmul(psum[:, bS1, 0:NM], lhsT=ones, rhs=cf[:, c, 0:NM], start=(c == 0), stop=(c == 3)),
                     reads=[Rones, Rcf[c]], writes=[Rps[bS1]], inc=(c == 3))
            bS2 = fm_bank()
            for c in range(4):
                s = sq[c % 2]; Rs = Rsq[c % 2]
                P.op("act", lambda e, c=c, s=s: e.activation(out=s[:, 0:NM], in_=cf[:, c, 0:NM], func=AF.Square),
                     reads=[Rcf[c]], writes=[Rs])
                P.op("pe", lambda e, c=c, s=s: e.matmul(psum[:, bS2, 0:NM], lhsT=ones, rhs=s[:, 0:NM], start=(c == 0), stop=(c == 3)),
                     reads=[Rones, Rs], writes=[Rps[bS2]], inc=True)
            P.op("act", lambda e: e.activation(out=mean[:, 0:NM], in_=psum[:, bS1, 0:NM], func=AF.Identity, scale=1.0 / 512),
                 reads=[Rps[bS1]], writes=[Rmean])
            P.op("dve", lambda e: e.tensor_tensor(out=tmpf[:, 0:NM], in0=mean[:, 0:NM], in1=mean[:, 0:NM], op=ALU.mult),
                 reads=[Rmean], writes=[Rtmpf])
            P.op("dve", lambda e: e.scalar_tensor_tensor(out=tmpf[:, 0:NM], in0=psum[:, bS2, 0:NM], scalar=1.0 / 512, in1=tmpf[:, 0:NM],
                                                         op0=ALU.mult, op1=ALU.subtract),
                 reads=[Rps[bS2], Rtmpf], writes=[Rtmpf])
            P.op("dve", lambda e: e.tensor_scalar(out=tmpf[:, 0:NM], in0=tmpf[:, 0:NM], scalar1=EPS, scalar2=None, op0=ALU.add),
                 reads=[Rtmpf], writes=[Rtmpf])
            yield
            rstd_from_var(NM, bS1)
            bS3 = fm_bank()
            for c in range(4):
                P.op("dve", lambda e, c=c: e.tensor_tensor(out=cf[:, c, 0:NM], in0=cf[:, c, 0:NM], in1=mean[:, 0:NM], op=ALU.subtract),
                     reads=[Rcf[c], Rmean], writes=[Rcf[c]])
                P.op("dve", lambda e, c=c: e.tensor_tensor(out=cf[:, c, 0:NM], in0=cf[:, c, 0:NM], in1=rstd[:, 0:NM], op=ALU.mult),
                     reads=[Rcf[c], Rrstd], writes=[Rcf[c]])
                P.op("act", lambda e, c=c: e.activation(out=cf[:, c, 0:NM], in_=cf[:, c, 0:NM], func=AF.Identity,
                                                        scale=pcol[:, PC_CONV + 4 + c:PC_CONV + 5 + c],
                                                        bias=pcol[:, PC_CONV + 8 + c:PC_CONV + 9 + c]),
                     reads=[Rcf[c], Rpcol], writes=[Rcf[c]])
                P.op("act", lambda e, c=c: e.activation(out=Tb[:, 0:NM], in_=cf[:, c, 0:NM], func=AF.Tanh, scale=0.5),
                     reads=[Rcf[c]], writes=[RTb])
                P.op("dve", lambda e, c=c: e.scalar_tensor_tensor(out=cf[:, c, 0:NM], in0=Tb[:, 0:NM], scalar=1.0, in1=cf[:, c, 0:NM],
                                                                  op0=ALU.add, op1=ALU.mult),
                     reads=[RTb, Rcf[c]], writes=[Rcf[c]])
                s = sq[c % 2]; Rs = Rsq[c % 2]
                P.op("act", lambda e, c=c, s=s: e.activation(out=s[:, 0:NM], in_=cf[:, c, 0:NM], func=AF.Square),
                     reads=[Rcf[c]], writes=[Rs])
                P.op("pe", lambda e, c=c, s=s: e.matmul(psum[:, bS3, 0:NM], lhsT=ones, rhs=s[:, 0:NM], start=(c == 0), stop=(c == 3)),
                     reads=[Rones, Rs], writes=[Rps[bS3]], inc=True)
            yield
            P.op("dve", lambda e: e.tensor_scalar(out=tmpf[:, 0:NM], in0=psum[:, bS3, 0:NM], scalar1=1.0 / 512, scalar2=4 * EPS,
                                                  op0=ALU.mult, op1=ALU.add),
                 reads=[Rps[bS3]], writes=[Rtmpf])
            rstd_from_var(NM, bS2)
            RyT_all = RyT[0:nb]
            for c in range(4):
                P.op("dve", lambda e, c=c: e.scalar_tensor_tensor(out=yT[:, 4 + c, c0:c0 + NM], in0=cf[:, c, 0:NM],
                                                                  scalar=pcol[:, PC_CONV + 12 + c:PC_CONV + 13 + c],
                                                                  in1=rstd[:, 0:NM], op0=ALU.mult, op1=ALU.mult),
                     reads=[Rcf[c], Rpcol, Rrstd], writes=RyT_all)

            yield

        if ti + 2 < len(tiles):
            load_xT(ti + 2)
        pre_list = list(range(N_EARLY)) if (ti == len(tiles) - 1 and KB > 0) else []
        def S0(j):
            bi = b0 + j
            qc0 = j * 128
            Pb = Pt[bi % 2]; RPb = RPt[bi % 2]
            for kvh in range(2):
                for kb in range(2):
                    kbi = bi - 1 + kb
                    sb_ = ps_sc[(kvh * 2 + kb) % 2]
                    for g in range(4):
                        h = kvh * 4 + g
                        half = h % 2
                        P.op("pe", lambda e, sb_=sb_, g=g, kvh=kvh, kbi=kbi, half=half, h=h: e.matmul(
                            psum[:, sb_, g * 128:(g + 1) * 128],
                            lhsT=kT[:, kvh, half, kbi * 128:(kbi + 1) * 128],
                            rhs=qT[:, h // 2, qc0:qc0 + 128], start=True, stop=True),
                            reads=[RkT, RqT], writes=[Rps[sb_]], inc=(g == 3))
                    es = Esb[(kvh * 2 + kb) % 2]; Res_ = REsb[(kvh * 2 + kb) % 2]
                    P.op("act", lambda e, sb_=sb_, es=es: e.activation(out=es, in_=psum[:, sb_, :], func=AF.Exp, scale=0.125),
                         reads=[Rps[sb_]], writes=[Res_])
                    P.op("dve", lambda e, es=es, kvh=kvh, kb=kb: e.tensor_tensor(
                        out=Pb[:, kvh, kb, :].rearrange("p (g q) -> p g q", g=4),
                        in0=es.rearrange("p (g q) -> p g q", g=4),
                        in1=EB[:, kb, kvh * 4:(kvh + 1) * 4, :], op=ALU.mult),
                        reads=[Res_, REB], writes=[RPb[kvh][kb]])

        def S1(j):
            bi = b0 + j
            qc0 = j * 128
            Pb = Pt[bi % 2]; RPb = RPt[bi % 2]
            sm = att[:, bi % 2, :]; Rsm = Ratt[bi % 2]
            for h in range(8):
                kvh, g = h // 4, h % 4
                pb_ = ps_pv[h // 4]
                for kb in range(2):
                    kbi = bi - 1 + kb
                    P.op("pe", lambda e, pb_=pb_, g=g, kvh=kvh, kb=kb, kbi=kbi: e.matmul(
                        psum[:, pb_, g * 65:(g + 1) * 65], lhsT=Pb[:, kvh, kb, g * 128:(g + 1) * 128],
                        rhs=vaug[:, kbi, kvh, :], start=(kb == 0), stop=(kb == 1)),
                        reads=[RPb[kvh][kb], Rvaug[kbi]], writes=[Rps[pb_]], inc=(kb == 1 and g == 3))
            pvv = psum[:, 4:6, 0:260].rearrange("p a (g e) -> p a g e", e=65)
            den = sm[:, 0:8]
            P.op("dve", lambda e: e.tensor_tensor(out=den.rearrange("p (a g) -> p a g", a=2), in0=pvv[:, :, :, 64],
                                                  in1=esink.rearrange("p (a g) -> p a g", a=2), op=ALU.add),
                 reads=[Rps[4], Rps[5], Resink], writes=[Rsm])
            P.op("dve", lambda e: e.reciprocal(out=den, in_=den), reads=[Rsm], writes=[Rsm])
            P.op("dve", lambda e: e.tensor_tensor(out=yat.rearrange("p (a g d) -> p a g d", a=2, g=4), in0=pvv[:, :, :, 0:64],
                                                  in1=den.rearrange("p (a g) -> p a g", a=2).unsqueeze(3).broadcast_to([128, 2, 4, 64]),
                                                  op=ALU.mult),
                 reads=[Rps[4], Rps[5], Rsm], writes=[Ryat])
            ss = sm[:, 8:9]
            P.op("act", lambda e: e.activation(out=junk, in_=yat, func=AF.Square, accum_out=ss),
                 reads=[Ryat], writes=[Rjunk, Rsm])
            P.op("dve", lambda e: e.tensor_scalar(out=sm[:, 9:10], in0=ss, scalar1=1.0 / 512, scalar2=EPS, op0=ALU.mult, op1=ALU.add),
                 reads=[Rsm], writes=[Rsm])
            P.op("pool", lambda e: e.tensor_tensor(out=sm[:, 10:11], in0=sm[:, 9:10], in1=mhalf[:, 0:1], op=ALU.pow),
                 reads=[Rsm, Rmhalf], writes=[Rsm])
            P.op("dve", lambda e: e.scalar_tensor_tensor(out=ya, in0=yat, scalar=sm[:, 10:11], in1=again_bc, op0=ALU.mult, op1=ALU.mult),
                 reads=[Ryat, Rsm, Ragain], writes=[Rya])
            for c in range(4):
                pb_ = ps_pv[c // 2]
                tv = psum[:, pb_, 384:512].bitcast(BF16)
                P.op("pe", lambda e, tv=tv, c=c: e.transpose(tv[:, (c % 2) * 128:(c % 2 + 1) * 128], ya[:, c * 128:(c + 1) * 128], ident),
                     reads=[Rya, Rident], writes=[Rps[pb_]], inc=True)
            for a in range(2):
                tv = psum[:, ps_pv[a], 384:512].bitcast(BF16)
                P.op("act", lambda e, tv=tv, a=a: e.activation(out=yT[:, 2 * a:2 * a + 2, qc0:qc0 + 128],
                                                               in_=tv.rearrange("p (c t) -> p c t", c=2), func=AF.Identity),
                     reads=[Rps[ps_pv[a]]], writes=[RyT[j]])

        def S2a(j):
            bi = b0 + j
            qc0 = j * 128
            xx = xt[bi % 3]; Rxx = Rxt[bi % 3]
            P.op("dve", lambda e: e.scalar_tensor_tensor(out=xx, in0=xx, scalar=ALPHA, in1=bout_bc, op0=ALU.mult, op1=ALU.add),
                 reads=[Rxx, Rbout], writes=[Rxx])
            for hf in range(2):
                ob = ps_op[hf]
                for kc in range(8):
                    P.op("pe", lambda e, ob=ob, kc=kc, hf=hf: e.matmul(psum[:, ob, :], lhsT=yT[:, kc, qc0:qc0 + 128],
                                                                       rhs=w_out[:, kc, hf * 512:(hf + 1) * 512],
                                                                       start=(kc == 0), stop=(kc == 7)),
                         reads=[RyT[j], Rw_out], writes=[Rps[ob]], inc=(kc == 7))
                P.op("dve", lambda e, ob=ob, hf=hf: e.tensor_tensor(out=xx[:, hf * 512:(hf + 1) * 512], in0=xx[:, hf * 512:(hf + 1) * 512],
                                                                    in1=psum[:, ob, :], op=ALU.add),
                     reads=[Rxx, Rps[ob]], writes=[Rxx])

            def tail():
                ln_tail(P, nc, xx, Rxx, lnst1[:, bi % 3, :], Rlnst1[bi % 3], mhalf, Rmhalf, ln1g_bc, Rln1g, ln1b_bc, Rln1b, 0, gb_eng="dve")
                P.dma("pool", "st_x1", x1s_d[(bi - 1) * 128:bi * 128, :], xx, reads=[Rxx])
                issue_xt(bi + 3)
            return tail

        blocks = [j for j in range(nb) if b0 + j != 0]
        pending = None
        sg_ = stats_gen()
        next(sg_)
        S0(blocks[0])
        next(sg_)
        if len(blocks) > 1:
            S0(blocks[1])
        S1(blocks[0])
        for _ in sg_:
            pass
        for idx, j in enumerate(blocks):
            if idx + 2 < len(blocks):
                S0(blocks[idx + 2])
            if idx + 1 < len(blocks):
                S1(blocks[idx + 1])
            tl = S2a(j)
            if pending is not None:
                pending()
            pending = tl
            npre = (len(pre_list) + (len(blocks) - idx) - 1) // (len(blocks) - idx)
            for p_ in pre_list[:npre]:
                load_wup(p_, EARLY_RES + [Rw_ag])
            pre_list = pre_list[npre:]
        if ti == len(tiles) - 1:
            if pending is not None:
                pending()
        else:
            carry[0] = pending

    P.barrier()
    A.off = keep_mark
    A.bf16(NFF, 8, 256)
    w_dn = A.bf16(NFF, D); Rw_dn = R()
    ln2g_bc = A.f32(D); Rln2g = R()
    ln2b_bc = A.f32(D); Rln2b = R()
    x1b = A.bf16(3, D); Rx1b = R()
    x1T = [A.bf16(8, 386) for _ in range(2)]; Rx1T = [R(), R()]
    tg = [A.f32(386) for _ in range(3)]; Rtg = [R(), R(), R()]
    tu = [A.f32(386) for _ in range(3)]; Rtu = [R(), R(), R()]
    sg = [A.f32(384) for _ in range(3)]; Rsg = [R(), R(), R()]
    actb = A.bf16(NFF, 384); Ract = [R() for _ in range(3)]
    xr = [A.f32(D) for _ in range(3)]; Rxr = [R(), R(), R()]
    lnst = A.f32(3, 16); Rlnst = [R(), R(), R()]

    LATE_WUP = list(range(N_EARLY, NFF))
    P.dma("sp", "ld_c2", ln2g_bc, ln2g_d.partition_broadcast(128), writes=[Rln2g])
    P.dma("sp", "ld_c2", ln2b_bc, ln2b_d.partition_broadcast(128), writes=[Rln2b])

    ps_tr = 6
    ps_up = [[0, 1], [2, 3], [4, 5]]
    ps_dn = [5, 6]

    def load_transpose(row0, nblk, dstT, RdstT, col0):
        for j in range(nblk):
            P.dma(wq, "ld_x1b", x1b[:, j, :], x1s_d[row0 + j * 128:row0 + (j + 1) * 128, :], writes=[Rx1b], par=True)
        for j in range(nblk):
            tv = psum[:, ps_tr, :].bitcast(BF16)
            for k in range(8):
                P.op("pe", lambda e, j=j, k=k, tv=tv: e.transpose(tv[:, k * 128:(k + 1) * 128], x1b[:, j, k * 128:(k + 1) * 128], ident),
                     reads=[Rx1b, Rident], writes=[Rps[ps_tr]], inc=(k == 7))
            P.op("act", lambda e, j=j, tv=tv: e.activation(out=dstT[:, :, col0 + j * 128:col0 + (j + 1) * 128],
                                                           in_=tv.rearrange("p (k t) -> p k t", k=8), func=AF.Identity),
                 reads=[Rps[ps_tr]], writes=[RdstT])

    ftiles = [(384 * i, 384) for i in range(5)] + [(1920, 128)]
    ftiles = ftiles[:KB]
    deferred = []
    dn_banks = [6, 7]
    dn_ctr = [0]

    def prep_tile(fi):
        tk0, T = ftiles[fi]
        cur = x1T[fi % 2]; Rcur = Rx1T[fi % 2]
        if fi == 0:
            load_transpose(0, 1, cur, Rcur, 2)
            P.op("dve", lambda e: e.tensor_scalar(out=cur[:, :, 0:2], in0=cur[:, :, 128:130], scalar1=flag[:, 0:1],
                                                  scalar2=None, op0=ALU.mult),
                 reads=[Rcur, Rflag], writes=[Rcur])
        else:
            pT = x1T[(fi - 1) % 2]; Tp = ftiles[fi - 1][1]
            P.op("dve", lambda e: e.tensor_copy(out=cur[:, :, 0:2], in_=pT[:, :, Tp:Tp + 2]),
                 reads=[Rx1T[(fi - 1) % 2]], writes=[Rcur])
        load_transpose(128 + tk0, T // 128, cur, Rcur, 2)

    def issue_xr(fi_, j_):
        if fi_ < len(ftiles) and j_ < ftiles[fi_][1] // 128:
            gb_ = ftiles[fi_][0] // 128 + j_
            P.dma("sp", "ld_xr", xr[j_], x1s_d[128 + gb_ * 128:256 + gb_ * 128, :], writes=[Rxr[j_]])

    if ftiles:
        prep_tile(0)
        for j_ in range(3):
            issue_xr(0, j_)
    for p_ in LATE_WUP:
        load_wup(p_, [])
    for p_ in range(NFF):
        P.dma(wq, "ld_wd", w_dn[:, p_, :], w_down_d[p_ * 128:(p_ + 1) * 128, :], writes=[Rw_dn], par=True)
    for fi, (tk0, T) in enumerate(ftiles):
        nblk = T // 128
        cur = x1T[fi % 2]; Rcur = Rx1T[fi % 2]
        NC = T + 2
        for p_ in range(NFF):
            if p_ == 3 and fi + 1 < len(ftiles):
                prep_tile(fi + 1)
            bg, bu = ps_up[p_ % 3]
            for gi, bb in ((0, bg), (1, bu)):
                for k in range(8):
                    P.op("pe", lambda e, bb=bb, k=k, gi=gi: e.matmul(psum[:, bb, 0:NC], lhsT=w_up[:, p_, k, gi * 128:(gi + 1) * 128],
                                                                     rhs=cur[:, k, 0:NC], start=(k == 0), stop=(k == 7)),
                         reads=[Rw_up[p_], Rcur], writes=[Rps[bb]], inc=(k == 7))
            tgb, Rtgb = tg[p_ % 3], Rtg[p_ % 3]
            tub, Rtub = tu[p_ % 3], Rtu[p_ % 3]
            sgb, Rsgb = sg[p_ % 3], Rsg[p_ % 3]
            chs = ((tgb, Rtgb, bg, p_), (tub, Rtub, bu, NFF + p_))
            for (tb, Rtb, bb, ch) in chs:
                wc = PC_FW + ch * 3
                P.op("act", lambda e, tb=tb, bb=bb, wc=wc, ch=ch: e.activation(
                    out=tb[:, 0:T], in_=psum[:, bb, 2:NC], func=AF.Identity,
                    scale=pcol[:, wc + 2:wc + 3], bias=pcol[:, PC_FB + ch:PC_FB + ch + 1]),
                    reads=[Rps[bb], Rpcol], writes=[Rtb])
            for tap in (1, 0):
                for (tb, Rtb, bb, ch) in chs:
                    wc = PC_FW + ch * 3
                    P.op("dve", lambda e, tb=tb, bb=bb, wc=wc, tap=tap: e.scalar_tensor_tensor(
                        out=tb[:, 0:T], in0=psum[:, bb, tap:tap + T], scalar=pcol[:, wc + tap:wc + tap + 1], in1=tb[:, 0:T],
                        op0=ALU.mult, op1=ALU.add), reads=[Rps[bb], Rpcol, Rtb], writes=[Rtb])
            P.op("act", lambda e: e.activation(out=sgb[:, 0:T], in_=tgb[:, 0:T], func=AF.Silu),
                 reads=[Rtgb], writes=[Rsgb])
            P.op("pool", lambda e: e.tensor_tensor(out=actb[:, p_, 0:T], in0=sgb[:, 0:T], in1=tub[:, 0:T], op=ALU.mult),
                 reads=[Rsgb, Rtub], writes=Ract[0:nblk])
            if p_ in (6, 11, 16) and deferred:
                deferred.pop(0)()
        for j in range(nblk):
            gb = (tk0 // 128) + j
            xx = xr[j]; Rxx = Rxr[j]
            for hf in range(2):
                ob = dn_banks[dn_ctr[0] % 2]
                dn_ctr[0] += 1
                for p_ in range(NFF):
                    P.op("pe", lambda e, ob=ob, p_=p_, hf=hf, j=j: e.matmul(psum[:, ob, :], lhsT=actb[:, p_, j * 128:(j + 1) * 128],
                                                                            rhs=w_dn[:, p_, hf * 512:(hf + 1) * 512],
                                                                            start=(p_ == 0), stop=(p_ == NFF - 1)),
                         reads=[Ract[j], Rw_dn], writes=[Rps[ob]], inc=(p_ == NFF - 1))
                P.op("dve", lambda e, ob=ob, hf=hf: e.scalar_tensor_tensor(out=xx[:, hf * 512:(hf + 1) * 512],
                                                                           in0=xx[:, hf * 512:(hf + 1) * 512], scalar=ALPHA,
                                                                           in1=psum[:, ob, :], op0=ALU.mult, op1=ALU.add),
                     reads=[Rxx, Rps[ob]], writes=[Rxx])

            def tail(xx=xx, Rxx=Rxx, j=j, gb=gb, fi=fi):
                ln_tail(P, nc, xx, Rxx, lnst[:, j, :], Rlnst[j], mhalf, Rmhalf, ln2g_bc, Rln2g, ln2b_bc, Rln2b, 0)
                P.dma("pool", "st_out", out_d[gb * 128:(gb + 1) * 128, :], xx, reads=[Rxx])
                issue_xr(fi + 1, j)
            deferred.append(tail)
    while deferred:
        deferred.pop(0)()

    P.wait_all("sp")
    P.wait_all("pool")
    P.emit()
    return nc


def ln_tail(P, nc, xx, Rxx, sc, Rsc, mhalf, Rmhalf, g_bc, Rg, b_bc, Rb, so, gb_eng="pool"):
    st = sc[:, so:so + 12]
    mv = sc[:, so + 12:so + 14]
    P.op("dve", lambda e: e.bn_stats(out=st[:, 0:6], in_=xx[:, 0:512]), reads=[Rxx], writes=[Rsc])
    P.op("dve", lambda e: e.bn_stats(out=st[:, 6:12], in_=xx[:, 512:1024]), reads=[Rxx], writes=[Rsc])
    P.op("dve", lambda e: e.bn_aggr(out=mv, in_=st), reads=[Rsc], writes=[Rsc])
    P.op("dve", lambda e: e.tensor_scalar(out=sc[:, so + 14:so + 15], in0=mv[:, 1:2], scalar1=EPS, scalar2=None, op0=ALU.add),
         reads=[Rsc], writes=[Rsc])
    P.op("pool", lambda e: e.tensor_tensor(out=sc[:, so + 15:so + 16], in0=sc[:, so + 14:so + 15], in1=mhalf[:, 0:1], op=ALU.pow),
         reads=[Rsc, Rmhalf], writes=[Rsc])
    P.op("dve", lambda e: e.scalar_tensor_tensor(out=sc[:, so + 14:so + 15], in0=mv[:, 0:1], scalar=-1.0, in1=sc[:, so + 15:so + 16],
                                                 op0=ALU.mult, op1=ALU.mult),
         reads=[Rsc], writes=[Rsc])
    P.op("act", lambda e: e.activation(out=xx, in_=xx, func=AF.Identity, scale=sc[:, so + 15:so + 16], bias=sc[:, so + 14:so + 15]),
         reads=[Rxx, Rsc], writes=[Rxx])
    P.op(gb_eng, lambda e: e.tensor_tensor(out=xx, in0=xx, in1=g_bc, op=ALU.mult), reads=[Rxx, Rg], writes=[Rxx])
    P.op(gb_eng, lambda e: e.tensor_tensor(out=xx, in0=xx, in1=b_bc, op=ALU.add), reads=[Rxx, Rb], writes=[Rxx])


def _bucket_idx():
    q = np.arange(128)[:, None]
    kc = np.arange(256)[None, :]
    dist = q + 128 - kc
    n = np.maximum(dist, 0)
    nf = np.maximum(n, 16).astype(np.float32)
    large = 16 + (np.log(nf / np.float32(16)) / np.float32(math.log(128 / 16)) * np.float32(16)).astype(np.int32)
    large = np.minimum(large, 31)
    bucket = np.where(n < 16, n, large)
    ok = (dist >= 0) & (dist < 128)
    return bucket, ok


def _cols(v, nchunk):
    return np.ascontiguousarray(np.asarray(v, np.float32).reshape(nchunk, 128).T)


_NC_CACHE = {}


def kernel(x, w_in, b_in, attn_sinks, rel_bias_table, conv_dw_w, conv_dw_b, conv_ln_g, conv_ln_b,
           attn_out_gain, conv_out_gain, w_out, b_out, ln1_g, ln1_b, w_up, ffn_dw_w, ffn_dw_b, w_down,
           ln2_g, ln2_b):
    f = lambda a: np.asarray(a, dtype=np.float32)
    x = f(x)
    w_in2, b_in1 = f(w_in)[0], f(b_in)[0]
    bq, bk, bvv, ba, bg = b_in1[0:512], b_in1[512:640], b_in1[640:768], b_in1[768:1280], b_in1[1280:1792]
    pcol = np.zeros((128, PC_N), np.float32)
    pcol[:, 0:4] = _cols(bq, 4)
    pcol[:, 4] = np.concatenate([bk[0:64], bk[0:64]])
    pcol[:, 5] = np.concatenate([bk[64:128], bk[64:128]])
    pcol[:, 6:10] = _cols(ba, 4)
    pcol[:, 10:14] = _cols(bg, 4)
    pcol[:, 14:18] = _cols(f(conv_dw_b)[0], 4)
    pcol[:, 18:22] = _cols(f(conv_ln_g)[0], 4)
    pcol[:, 22:26] = _cols(f(conv_ln_b)[0], 4)
    pcol[:, 26:30] = _cols(f(conv_out_gain)[0], 4)
    cw = f(conv_dw_w)[0]
    pcol[:, PC_CW:PC_CW + 124] = cw.reshape(31, 4, 128).transpose(2, 1, 0).reshape(128, 124)
    fw = f(ffn_dw_w)[0]
    pcol[:, PC_FW:PC_FW + 132] = fw.reshape(3, 44, 128).transpose(2, 1, 0).reshape(128, 132)
    pcol[:, PC_FB:PC_FB + 44] = _cols(f(ffn_dw_b)[0], 44)
    bucket, ok = _bucket_idx()
    tab = f(rel_bias_table)
    bias_full = tab[bucket]
    biasg = np.ascontiguousarray(bias_full.reshape(128, 2, 128, 8).transpose(2, 1, 3, 0)).reshape(128, 2 * 8 * 128)
    maskc = np.ascontiguousarray(ok.astype(np.float32).reshape(128, 2, 128).transpose(2, 1, 0)).reshape(128, 256)
    shared = {
        "w_in": w_in2, "w_out": f(w_out)[0], "w_up": f(w_up)[0], "w_down": f(w_down)[0], "pcol": pcol,
        "bv": bvv.reshape(1, 128), "again": f(attn_out_gain), "bout": f(b_out), "ln1g": f(ln1_g), "ln1b": f(ln1_b),
        "ln2g": f(ln2_g), "ln2b": f(ln2_b), "sinks": f(attn_sinks), "biasg": biasg, "maskc": maskc,
    }
    in_maps = []
    for core in range(8):
        b, c = core // 4, core % 4
        s0 = c * OWN
        xh = np.zeros((NTOK, D), np.float32)
        lo = s0 - 256
        if lo >= 0:
            xh[:] = x[b, lo:s0 + OWN]
        else:
            xh[256:] = x[b, 0:OWN]
        m = dict(shared)
        m["xh"] = xh
        m["xT"] = np.ascontiguousarray(xh.T)
        m["flag"] = np.full((128, 1), 1.0 if c > 0 else 0.0, np.float32)
        in_maps.append(m)
    if "nc" not in _NC_CACHE:
        _NC_CACHE["nc"] = build()
    res = run_bass_kernel_spmd(_NC_CACHE["nc"], in_maps, core_ids=list(range(8)))
    out = np.empty((2, 8192, D), np.float32)
    for core in range(8):
        b, c = core // 4, core % 4
        out[b, c * OWN:(c + 1) * OWN] = res.results[core]["out"]
    if DEBUG:
        kernel.dbg = [r for r in res.results]
    return out
```

```python
import math
import numpy as np
import concourse.bass as bass
import concourse.mybir as mybir
from concourse.bass_utils import run_bass_kernel_spmd

F32 = mybir.dt.float32
BF16 = mybir.dt.bfloat16
AF = mybir.ActivationFunctionType
ALU = mybir.AluOpType


class Res:
    __slots__ = ("name", "lw", "rd", "lws")

    def __init__(self, name):
        self.name = name
        self.lw = None
        self.lws = []
        self.rd = []


class Lane:
    def __init__(self, name, sem, unit):
        self.name, self.sem, self.unit, self.count = name, sem, unit, 0


class Eng:
    def __init__(self, name, lane):
        self.name, self.lane, self.ops, self.known = name, lane, [], {}


class _Rec:
    def __getattr__(self, name):
        return lambda *a, **k: (name, a, k)


class Prog:
    def __init__(self, nc):
        self.nc = nc
        self.eng = {}
        for n in ("pe", "act", "dve", "pool", "sp"):
            self.eng[n] = Eng(n, Lane(n, nc.alloc_semaphore("s_" + n), 1))
        self.dma_lanes = {}
        self.lane_rr = {}
        self.nres = 0

    def res(self, name=None):
        self.nres += 1
        return Res(name or f"r{self.nres}")

    LANES = {"ld_c": 4, "ld_w": 8, "ld_w2": 4, "ld_x": 8, "ld_xt": 3, "st_x1": 2, "ld_wu": 12, "ld_wd": 6,
             "ld_x1b": 3, "ld_xr": 3, "st_out": 2, "ld_c2": 2}

    def lane(self, name):
        k = self.LANES.get(name, 2)
        i = self.lane_rr.get(name, 0)
        self.lane_rr[name] = i + 1
        key = f"{name}{i % k}"
        if key not in self.dma_lanes:
            self.dma_lanes[key] = Lane(key, self.nc.alloc_semaphore("d_" + key), 16)
        return self.dma_lanes[key]

    def _deps(self, e, reads, writes, par=False):
        need = {}

        def add(t):
            if t is not None and need.get(t[0], 0) < t[1]:
                need[t[0]] = t[1]
        for r in reads:
            add(r.lw)
            for t in r.lws:
                add(t)
        for w in writes:
            for t in w.rd:
                add(t)
            if par and not w.rd:
                continue
            add(w.lw)
            for t in w.lws:
                add(t)
        waits = []
        for ln, idx in need.items():
            if ln is e.lane and e.name == "pe":
                continue
            if e.known.get(ln, 0) >= idx:
                continue
            e.known[ln] = idx
            waits.append((ln.sem, idx * ln.unit))
        return waits

    def _mark(self, t, reads, writes, par=False):
        for r in reads:
            r.rd.append(t)
        for w in writes:
            if par:
                if w.rd:
                    w.lw, w.lws, w.rd = None, [], []
                w.lws.append(t)
            else:
                w.lw, w.lws, w.rd = t, [], []

    def op(self, en, fn, reads=(), writes=(), inc=True):
        name, a, k = fn(_Rec())
        fn = (lambda eng, name=name, a=a, k=k: getattr(eng, name)(*a, **k))
        e = self.eng[en]
        waits = self._deps(e, reads, writes)
        ln = e.lane
        if inc:
            ln.count += 1
            idx = ln.count
        else:
            idx = ln.count + 1
        e.ops.append((waits, fn, ln if inc else None, 1))
        self._mark((ln, idx), reads, writes)

    def dma(self, qn, lane_name, out, in_, reads=(), writes=(), par=False):
        e = self.eng[qn]
        ln = self.lane(lane_name)
        waits = self._deps(e, reads, writes, par)
        if ln.count and e.known.get(ln, 0) < ln.count:
            e.known[ln] = ln.count
            waits.append((ln.sem, ln.count * ln.unit))
        ln.count += 1
        e.ops.append((waits, (lambda eng, o=out, i=in_: eng.dma_start(out=o, in_=i)), ln, 16))
        self._mark((ln, ln.count), reads, writes, par)

    def barrier(self):
        lanes = [e.lane for e in self.eng.values()] + list(self.dma_lanes.values())
        for e in self.eng.values():
            waits = []
            for ln in lanes:
                if ln is e.lane or ln.count == 0:
                    continue
                if e.known.get(ln, 0) < ln.count:
                    e.known[ln] = ln.count
                    waits.append((ln.sem, ln.count * ln.unit))
            e.ops.append((waits, None, None, 0))

    def wait_all(self, en):
        e = self.eng[en]
        waits = []
        for ln in self.dma_lanes.values():
            if ln.count and e.known.get(ln, 0) < ln.count:
                e.known[ln] = ln.count
                waits.append((ln.sem, ln.count * ln.unit))
        e.ops.append((waits, None, None, 0))

    def emit(self):
        with self.nc.Block() as block:
            def run(e, eng):
                for waits, fn, ln, amt in e.ops:
                    for sem, val in waits:
                        eng.wait_ge(sem, val)
                    if fn is None:
                        continue
                    ins = fn(eng)
                    if ln is not None:
                        ins.then_inc(ln.sem, amt)

            @block.tensor
            def _(eng):
                run(self.eng["pe"], eng)

            @block.scalar
            def _(eng):
                run(self.eng["act"], eng)

            @block.vector
            def _(eng):
                run(self.eng["dve"], eng)

            @block.gpsimd
            def _(eng):
                run(self.eng["pool"], eng)

            @block.sync
            def _(eng):
                run(self.eng["sp"], eng)


class Arena:
    def __init__(self, nc, words):
        self.t = nc.alloc_sbuf_tensor("arena", [128, words], F32)
        self.words = words
        self.off = 0

    def _take(self, n):
        o = self.off
        self.off += n
        assert self.off <= self.words, f"arena overflow {self.off * 4} > {self.words * 4}"
        return o

    @staticmethod
    def _shape(v, dims):
        if len(dims) == 1:
            return v
        names = " ".join(f"d{i}" for i in range(len(dims)))
        kw = {f"d{i}": d for i, d in enumerate(dims[:-1])}
        return v.rearrange(f"p ({names}) -> p {names}", **kw)

    def f32(self, *dims):
        n = int(np.prod(dims))
        o = self._take(n)
        return self._shape(self.t[:, o:o + n], dims)

    def bf16(self, *dims):
        n = int(np.prod(dims))
        n4 = (n + 1) // 2
        o = self._take(n4)
        v = self.t[:, o:o + n4].bitcast(BF16)[:, 0:n]
        return self._shape(v, dims)


D = 1024
NTOK = 2304
NBLK = 18
OWN = 2048
DFF = 2816
NFF = 22
ALPHA = float(2.0 ** 0.25)
EPS = 1e-5
W_IN_COLS = 14 * 128 + 128
PC_IN, PC_CONV, PC_CW, PC_FW, PC_FB = 0, 14, 30, 154, 286
PC_N = 330

DEBUG = False
KA, KB, KSUB = 99, 99, 9

def build():
    nc = bass.Bass("TRN2", target_bir_lowering=False)
    dt = lambda n, s, k="ExternalInput": nc.dram_tensor(n, s, F32, kind=k).ap()
    xh_d = dt("xh", [NTOK, D])
    xT_d = dt("xT", [D, NTOK])
    flag_d = dt("flag", [128, 1])
    w_in_d = dt("w_in", [D, 1792])
    w_out_d = dt("w_out", [D, D])
    w_up_d = dt("w_up", [D, 2 * DFF])
    w_down_d = dt("w_down", [DFF, D])
    pcol_d = dt("pcol", [128, PC_N])
    bv_d = dt("bv", [1, 128])
    again_d = dt("again", [1, 512])
    bout_d = dt("bout", [1, D])
    ln1g_d = dt("ln1g", [1, D])
    ln1b_d = dt("ln1b", [1, D])
    ln2g_d = dt("ln2g", [1, D])
    ln2b_d = dt("ln2b", [1, D])
    sinks_d = dt("sinks", [1, 8])
    biasg_d = dt("biasg", [128, 2 * 8 * 128])
    maskc_d = dt("maskc", [128, 2 * 128])
    out_d = dt("out", [OWN, D], "ExternalOutput")
    x1s_d = dt("x1s", [NTOK - 128, D], "ExternalOutput" if DEBUG else "Internal")

    P = Prog(nc)
    A = Arena(nc, 53000)
    psum = nc.alloc_psum_tensor("psum", [128, 8, 512], F32)
    R = P.res
    Rps = [R(f"ps{i}") for i in range(8)]

    pcol = A.f32(PC_N); Rpcol = R()
    flag = A.f32(1); Rflag = R()
    ident = A.bf16(128); Rident = R()
    ones = A.f32(128); Rones = R()
    mhalf = A.f32(512); Rmhalf = R()
    halfb = A.f32(8); Rhalfb = R()
    small = A.f32(64); Rsmall = R()
    identf = A.f32(128); Ridentf = R()
    rdt = A.f32(8); Rrdt = R()
    keep_mark = A.off

    P.dma("sp", "ld_c", pcol, pcol_d, writes=[Rpcol])
    P.dma("sp", "ld_c", flag, flag_d, writes=[Rflag])
    P.op("pool", lambda e: e.memset(ident, 0.0), writes=[Rident])
    P.op("pool", lambda e: e.affine_select(out=ident, in_=ident, compare_op=ALU.not_equal, fill=1.0,
                                          base=0, pattern=[[-1, 128]], channel_multiplier=1),
         reads=[Rident], writes=[Rident])
    P.op("pool", lambda e: e.memset(ones, 1.0), writes=[Rones])
    P.op("pool", lambda e: e.tensor_copy(out=identf, in_=ident), reads=[Rident], writes=[Ridentf])
    P.op("pool", lambda e: e.memset(mhalf, -0.5), writes=[Rmhalf])
    P.op("dve", lambda e: e.tensor_scalar(out=halfb[:, 0:4], in0=pcol[:, PC_IN + 6:PC_IN + 10], scalar1=0.5, scalar2=None, op0=ALU.mult),
         reads=[Rpcol], writes=[Rhalfb])
    P.op("dve", lambda e: e.tensor_scalar(out=halfb[:, 4:8], in0=pcol[:, PC_IN + 10:PC_IN + 14], scalar1=0.5, scalar2=None, op0=ALU.mult),
         reads=[Rpcol], writes=[Rhalfb])

    w_in = A.bf16(8, W_IN_COLS); Rw_in = R()
    diag = A.bf16(124, 128); Rdiag = [R() for _ in range(4)]
    xT = [A.bf16(8, 512) for _ in range(2)]; RxT = [R(), R()]
    hT = [A.bf16(4, 30 + 512) for _ in range(2)]; RhT = [R(), R()]
    cf = A.f32(4, 512); Rcf = [R() for _ in range(4)]
    sq = [A.f32(512) for _ in range(2)]; Rsq = [R(), R()]
    rstd = A.f32(512); Rrstd = R()
    early_free_end = A.off
    EARLY_RES = [Rw_in] + Rdiag + RxT + RhT + Rcf + Rsq + [Rrstd]
    w_out = A.bf16(8, D); Rw_out = R()
    kT = A.bf16(2, 2, NTOK); RkT = R()
    vaug = A.bf16(NBLK, 2, 65); Rvaug = [R() for _ in range(NBLK)]
    EB = A.f32(2, 8, 128); REB = R()
    esink = A.f32(8); Resink = R()
    bv_bc = A.f32(128); Rbv = R()
    again_bc = A.f32(512); Ragain = R()
    bout_bc = A.f32(D); Rbout = R()
    ln1g_bc = A.f32(D); Rln1g = R()
    ln1b_bc = A.f32(D); Rln1b = R()
    qT = A.bf16(4, 512); RqT = R()
    Tb = A.f32(512); RTb = R()
    maskc = Tb[:, 0:256].rearrange("p (a q) -> p a q", a=2); Rmask = RTb
    ab = A.f32(512); Rab = R()
    mean = A.f32(512); Rmean = R()
    tmpf = A.f32(512); Rtmpf = R()
    yT = A.bf16(8, 512); RyT = [R() for _ in range(4)]
    Esb = [Tb, ab]; REsb = [RTb, Rab]
    Pt = [A.bf16(2, 2, 512) for _ in range(2)]
    RPt = [[[R(), R()], [R(), R()]] for _ in range(2)]
    yat = mean; Ryat = Rmean
    junk = tmpf; Rjunk = Rtmpf
    ya = A.bf16(512); Rya = R()
    xt = [A.f32(D) for _ in range(3)]; Rxt = [R(), R(), R()]
    lnst1 = A.f32(3, 16); Rlnst1 = [R(), R(), R()]
    att = A.f32(2, 16); Ratt = [R(), R()]

    wq = "pool"
    for k in range(8):
        P.dma(wq, "ld_x", xT[0][:, k, 0:256], xT_d[k * 128:(k + 1) * 128, 0:256], writes=[RxT[0]], par=True)
    _save = A.off
    A.off = keep_mark
    w_up = A.bf16(NFF, 8, 256); Rw_up = [R() for _ in range(NFF)]
    A.off = _save
    N_EARLY = min(NFF, (early_free_end - keep_mark) // 1024)
    w_up_v = w_up_d.rearrange("(k p) n -> p k n", p=128)

    def load_wup(p_, extra):
        for gi, off in enumerate((0, DFF)):
            c1 = off + p_ * 128
            P.dma(wq, "ld_wu", w_up[:, p_, :, gi * 128:(gi + 1) * 128], w_up_v[:, :, c1:c1 + 128],
                  writes=[Rw_up[p_]] + (extra if gi == 0 else []), par=True)
    w_in_v = w_in_d.rearrange("(k p) n -> p k n", p=128)
    Rw_ag = R()
    segs = [(768, 768, 512, Rw_ag), (1280, 1280, 512, Rw_ag), (0, 0, 512, Rw_in), (512, 512, 64, Rw_in), (576, 512, 64, Rw_in),
            (640, 576, 64, Rw_in), (704, 576, 64, Rw_in), (1792, 640, 128, Rw_in)]
    for si, (dst, src, n, rr) in enumerate(segs):
        if si == 2:
            P.op("pool", lambda e: e.memset(hT[0][:, :, 0:30], 0.0), writes=[RhT[0]])
            P.op("pool", lambda e: e.memset(kT[64:128, :, 0, :], 0.0), writes=[RkT])
            P.op("pool", lambda e: e.memset(kT[0:64, :, 1, :], 0.0), writes=[RkT])
        for k in range(8):
            P.dma(wq, "ld_w", w_in[:, k, dst:dst + n], w_in_v[:, k, src:src + n], writes=[rr], par=True)
    for k in range(8):
        P.dma(wq, "ld_w2", w_out[:, k, :], w_out_d[k * 128:(k + 1) * 128, :], writes=[Rw_out], par=True)

    P.dma("sp", "ld_c", bv_bc, bv_d.partition_broadcast(128), writes=[Rbv])
    P.dma("sp", "ld_c", again_bc, again_d.partition_broadcast(128), writes=[Ragain])
    P.dma("sp", "ld_c", bout_bc, bout_d.partition_broadcast(128), writes=[Rbout])
    P.dma("sp", "ld_c", ln1g_bc, ln1g_d.partition_broadcast(128), writes=[Rln1g])
    P.dma("sp", "ld_c", ln1b_bc, ln1b_d.partition_broadcast(128), writes=[Rln1b])
    P.dma("sp", "ld_c", esink, sinks_d.partition_broadcast(128), writes=[Resink])
    P.dma("sp", "ld_c", EB, biasg_d.rearrange("p (a h q) -> p a h q", a=2, h=8), writes=[REB])
    P.dma("sp", "ld_c", maskc, maskc_d.rearrange("p (a q) -> p a q", a=2), writes=[Rmask])
    P.op("act", lambda e: e.activation(out=esink, in_=esink, func=AF.Exp), reads=[Resink], writes=[Resink])
    P.op("act", lambda e: e.activation(out=EB, in_=EB, func=AF.Exp), reads=[REB], writes=[REB])
    for kb in range(2):
        P.op("dve", lambda e, kb=kb: e.tensor_tensor(out=EB[:, kb], in0=EB[:, kb],
                                                    in1=maskc[:, kb].unsqueeze(1).broadcast_to([128, 8, 128]), op=ALU.mult),
             reads=[REB, Rmask], writes=[REB])
    def build_diag(c):
        for j in range(31):
            i = c * 31 + j
            if j % 2 == 0:
                P.op("act", lambda e, i=i: e.activation(out=diag[:, i, :], in_=ident, func=AF.Identity,
                                                        scale=pcol[:, PC_CW + i:PC_CW + i + 1]),
                     reads=[Rident, Rpcol], writes=[Rdiag[c]])
            else:
                P.op("dve", lambda e, i=i: e.tensor_scalar(out=diag[:, i, :], in0=ident, scalar1=pcol[:, PC_CW + i:PC_CW + i + 1],
                                                           scalar2=None, op0=ALU.mult),
                     reads=[Rident, Rpcol], writes=[Rdiag[c]])
    P.op("dve", lambda e: e.memset(vaug[:, :, :, 64:65], 1.0), writes=Rvaug)

    ps_fm = [0, 1]
    ps_sc = [2, 3]
    ps_pv = [4, 5]
    ps_op = [6, 7]
    fm_ctr = [0]

    def fm_bank():
        b = ps_fm[fm_ctr[0] % 2]
        fm_ctr[0] += 1
        return b

    def rstd_from_var(NM_, bank):
        nbk = NM_ // 128
        t3 = tmpf[:, 0:NM_].rearrange("p (j t) -> p j t", j=nbk)
        P.op("dve", lambda e: e.tensor_tensor(out=t3, in0=t3, in1=identf.unsqueeze(1).broadcast_to([128, nbk, 128]), op=ALU.mult),
             reads=[Rtmpf, Ridentf], writes=[Rtmpf])
        P.op("dve", lambda e: e.tensor_reduce(out=rdt[:, 0:nbk], in_=t3, axis=mybir.AxisListType.X, op=ALU.add),
             reads=[Rtmpf], writes=[Rrdt])
        P.op("pool", lambda e: e.tensor_tensor(out=rdt[:, 4:4 + nbk], in0=rdt[:, 0:nbk], in1=mhalf[:, 0:nbk], op=ALU.pow),
             reads=[Rrdt, Rmhalf], writes=[Rrdt])
        P.op("dve", lambda e: e.tensor_tensor(out=t3, in0=identf.unsqueeze(1).broadcast_to([128, nbk, 128]),
                                              in1=rdt[:, 4:4 + nbk].unsqueeze(2).broadcast_to([128, nbk, 128]), op=ALU.mult),
             reads=[Ridentf, Rrdt], writes=[Rtmpf])
        for j_ in range(nbk):
            P.op("pe", lambda e, j_=j_: e.matmul(psum[:, bank, j_ * 128:(j_ + 1) * 128], lhsT=ones, rhs=tmpf[:, j_ * 128:(j_ + 1) * 128],
                                                 start=True, stop=True),
                 reads=[Rones, Rtmpf], writes=[Rps[bank]], inc=(j_ == nbk - 1))
        P.op("act", lambda e: e.activation(out=rstd[:, 0:NM_], in_=psum[:, bank, 0:NM_], func=AF.Identity),
             reads=[Rps[bank]], writes=[Rrstd])

    tiles = ([(0, 2)] + [(2 + 4 * i, 4) for i in range(4)])[:KA]

    def load_xT(ti_):
        b0_, nb_ = tiles[ti_]
        for k in range(8):
            P.dma(wq, "ld_x", xT[ti_ % 2][:, k, 0:nb_ * 128], xT_d[k * 128:(k + 1) * 128, b0_ * 128:(b0_ + nb_) * 128],
                  writes=[RxT[ti_ % 2]], par=True)

    carry = [None]

    def issue_xt(bi_):
        if 1 <= bi_ < NBLK:
            P.dma("sp", "ld_xt", xt[bi_ % 3], xh_d[bi_ * 128:(bi_ + 1) * 128, :], writes=[Rxt[bi_ % 3]])
    for bi_ in (1, 2, 3):
        issue_xt(bi_)
    def dense_ag(ti_, chunks):
        N_ = tiles[ti_][1] * 128
        xb_ = xT[ti_ % 2]; Rxb_ = RxT[ti_ % 2]
        hb_ = hT[ti_ % 2]; Rhb_ = RhT[ti_ % 2]

        def pc(ch):
            b = fm_bank()
            for k in range(8):
                P.op("pe", lambda e, b=b, k=k, ch=ch: e.matmul(psum[:, b, 0:N_], lhsT=w_in[:, k, ch * 128:(ch + 1) * 128],
                                                               rhs=xb_[:, k, 0:N_], start=(k == 0), stop=(k == 7)),
                     reads=[Rw_ag, Rxb_], writes=[Rps[b]], inc=(k == 7))
            return b
        if 0 in chunks and ti_ > 0:
            pb = hT[(ti_ - 1) % 2]; Npv = tiles[ti_ - 1][1] * 128
            P.op("pool", lambda e: e.tensor_copy(out=hb_[:, :, 0:30], in_=pb[:, :, Npv:Npv + 30]),
                 reads=[RhT[(ti_ - 1) % 2]], writes=[Rhb_])
        for c in chunks:
            b = pc(6 + c)
            P.op("act", lambda e, b=b, c=c: e.activation(out=ab[:, 0:N_], in_=psum[:, b, 0:N_], func=AF.Identity,
                                                         scale=0.5, bias=halfb[:, c:c + 1]),
                 reads=[Rps[b], Rhalfb], writes=[Rab])
            b2 = pc(10 + c)
            P.op("act", lambda e, b2=b2, c=c: e.activation(out=Tb[:, 0:N_], in_=psum[:, b2, 0:N_], func=AF.Tanh,
                                                           scale=0.5, bias=halfb[:, 4 + c:5 + c]),
                 reads=[Rps[b2], Rhalfb], writes=[RTb])
            P.op("dve", lambda e, c=c: e.scalar_tensor_tensor(out=hb_[:, c, 30:30 + N_], in0=Tb[:, 0:N_], scalar=1.0, in1=ab[:, 0:N_],
                                                              op0=ALU.add, op1=ALU.mult),
                 reads=[RTb, Rab], writes=[Rhb_])
            if ti_ == 0:
                build_diag(c)
        if 3 in chunks and ti_ == 0:
            P.op("dve", lambda e: e.tensor_scalar(out=hb_[:, :, 30:30 + N_], in0=hb_[:, :, 30:30 + N_], scalar1=flag[:, 0:1],
                                                  scalar2=None, op0=ALU.mult),
                 reads=[Rhb_, Rflag], writes=[Rhb_])

    def dense_conv(ti_, chunks):
        N_ = tiles[ti_][1] * 128
        hb_ = hT[ti_ % 2]; Rhb_ = RhT[ti_ % 2]
        c0_ = 128 if ti_ == 0 else 0
        NM_ = N_ - c0_
        for c in chunks:
            b = fm_bank()
            for j in range(31):
                P.op("pe", lambda e, b=b, c=c, j=j: e.matmul(psum[:, b, 0:NM_], lhsT=diag[:, c * 31 + j, :],
                                                             rhs=hb_[:, c, c0_ + j:c0_ + j + NM_], start=(j == 0), stop=(j == 30)),
                     reads=[Rdiag[c], Rhb_], writes=[Rps[b]], inc=(j == 30))
            P.op("act", lambda e, b=b, c=c: e.activation(out=cf[:, c, 0:NM_], in_=psum[:, b, 0:NM_], func=AF.Identity,
                                                         bias=pcol[:, PC_CONV + c:PC_CONV + c + 1]),
                 reads=[Rps[b], Rpcol], writes=[Rcf[c]])

    for ti, (b0, nb) in enumerate(tiles):
        N = nb * 128
        t0 = b0 * 128
        xb = xT[ti % 2]; Rxb = RxT[ti % 2]
        hb = hT[ti % 2]; Rhb = RhT[ti % 2]
        if ti == 0:
            if len(tiles) > 1:
                load_xT(1)

        def proj_chunk(ch):
            b = fm_bank()
            for k in range(8):
                P.op("pe", lambda e, b=b, k=k, ch=ch: e.matmul(psum[:, b, 0:N], lhsT=w_in[:, k, ch * 128:(ch + 1) * 128],
                                                               rhs=xb[:, k, 0:N], start=(k == 0), stop=(k == 7)),
                     reads=[Rw_ag if ch >= 6 else Rw_in, Rxb], writes=[Rps[b]], inc=(k == 7))
            return b
        def emit_qk():
            for c in range(4):
                b = proj_chunk(c)
                P.op("act", lambda e, b=b, c=c: e.activation(out=qT[:, c, 0:N], in_=psum[:, b, 0:N], func=AF.Identity,
                                                             bias=pcol[:, PC_IN + c:PC_IN + c + 1]),
                     reads=[Rps[b], Rpcol], writes=[RqT])
            for kv in range(2):
                b = proj_chunk(4 + kv)
                for hh in range(2):
                    pr = slice(hh * 64, (hh + 1) * 64)
                    P.op("act", lambda e, b=b, kv=kv, hh=hh, pr=pr: e.activation(
                        out=kT[pr, kv, hh, t0:t0 + N], in_=psum[pr, b, 0:N], func=AF.Identity,
                        bias=pcol[pr, PC_IN + 4 + kv:PC_IN + 5 + kv]),
                        reads=[Rps[b], Rpcol], writes=[RkT])

        if ti == 0:
            dense_ag(0, [0, 1, 2, 3])
        def emit_v():
            for j in range(nb):
                bi = b0 + j
                b = fm_bank()
                for k in range(8):
                    P.op("pe", lambda e, b=b, k=k, j=j: e.matmul(psum[:, b, 0:128], lhsT=xb[:, k, j * 128:(j + 1) * 128],
                                                                 rhs=w_in[:, k, 1792:1920], start=(k == 0), stop=(k == 7)),
                         reads=[Rw_in, Rxb], writes=[Rps[b]], inc=(k == 7))
                P.op("dve", lambda e, b=b, bi=bi: e.tensor_tensor(out=vaug[:, bi, :, 0:64],
                                                                  in0=psum[:, b, 0:128].rearrange("p (a d) -> p a d", a=2),
                                                                  in1=bv_bc.rearrange("p (a d) -> p a d", a=2), op=ALU.add),
                     reads=[Rps[b], Rbv], writes=[Rvaug[bi]])
                if ti == 0:
                    P.op("dve", lambda e, bi=bi: e.tensor_scalar(out=vaug[:, bi], in0=vaug[:, bi], scalar1=flag[:, 0:1],
                                                                 scalar2=None, op0=ALU.mult),
                         reads=[Rvaug[bi], Rflag], writes=[Rvaug[bi]])

        if carry[0] is not None:
            carry[0]()
            carry[0] = None
        c0 = 128 if ti == 0 else 0
        NM = N - c0
        if ti == 0:
            dense_conv(0, [0, 1, 2, 3])
        emit_qk()
        emit_v()
        def stats_gen():
            bS1 = fm_bank()
            for c in range(4):
                P.op("pe", lambda e, c=c: e.matmul(psum[:, bS1, 0:NM], lhsT=ones, rhs=cf[:, c, 0:NM], start=(c == 0), stop=(c == 3)),
                     reads=[Rones, Rcf[c]], writes=[Rps[bS1]], inc=(c == 3))
            bS2 = fm_bank()
            for c in range(4):
                s = sq[c % 2]; Rs = Rsq[c % 2]
                P.op("act", lambda e, c=c, s=s: e.activation(out=s[:, 0:NM], in_=cf[:, c, 0:NM], func=AF.Square),
                     reads=[Rcf[c]], writes=[Rs])
                P.op("pe", lambda e, c=c, s=s: e.matmul(psum[:, bS2, 0:NM], lhsT=ones, rhs=s[:, 0:NM], start=(c == 0), stop=(c == 3)),
                     reads=[Rones, Rs], writes=[Rps[bS2]], inc=True)
            P.op("act", lambda e: e.activation(out=mean[:, 0:NM], in_=psum[:, bS1, 0:NM], func=AF.Identity, scale=1.0 / 512),
                 reads=[Rps[bS1]], writes=[Rmean])
            P.op("dve", lambda e: e.tensor_tensor(out=tmpf[:, 0:NM], in0=mean[:, 0:NM], in1=mean[:, 0:NM], op=ALU.mult),
                 reads=[Rmean], writes=[Rtmpf])
            P.op("dve", lambda e: e.scalar_tensor_tensor(out=tmpf[:, 0:NM], in0=psum[:, bS2, 0:NM], scalar=1.0 / 512, in1=tmpf[:, 0:NM],
                                                         op0=ALU.mult, op1=ALU.subtract),
                 reads=[Rps[bS2], Rtmpf], writes=[Rtmpf])
            P.op("dve", lambda e: e.tensor_scalar(out=tmpf[:, 0:NM], in0=tmpf[:, 0:NM], scalar1=EPS, scalar2=None, op0=ALU.add),
                 reads=[Rtmpf], writes=[Rtmpf])
            yield
            rstd_from_var(NM, bS1)
            bS3 = fm_bank()
            for c in range(4):
                P.op("dve", lambda e, c=c: e.tensor_tensor(out=cf[:, c, 0:NM], in0=cf[:, c, 0:NM], in1=mean[:, 0:NM], op=ALU.subtract),
                     reads=[Rcf[c], Rmean], writes=[Rcf[c]])
                P.op("dve", lambda e, c=c: e.tensor_tensor(out=cf[:, c, 0:NM], in0=cf[:, c, 0:NM], in1=rstd[:, 0:NM], op=ALU.mult),
                     reads=[Rcf[c], Rrstd], writes=[Rcf[c]])
                P.op("act", lambda e, c=c: e.activation(out=cf[:, c, 0:NM], in_=cf[:, c, 0:NM], func=AF.Identity,
                                                        scale=pcol[:, PC_CONV + 4 + c:PC_CONV + 5 + c],
                                                        bias=pcol[:, PC_CONV + 8 + c:PC_CONV + 9 + c]),
                     reads=[Rcf[c], Rpcol], writes=[Rcf[c]])
                P.op("act", lambda e, c=c: e.activation(out=Tb[:, 0:NM], in_=cf[:, c, 0:NM], func=AF.Tanh, scale=0.5),
                     reads=[Rcf[c]], writes=[RTb])
                P.op("dve", lambda e, c=c: e.scalar_tensor_tensor(out=cf[:, c, 0:NM], in0=Tb[:, 0:NM], scalar=1.0, in1=cf[:, c, 0:NM],
                                                                  op0=ALU.add, op1=ALU.mult),
                     reads=[RTb, Rcf[c]], writes=[Rcf[c]])
                s = sq[c % 2]; Rs = Rsq[c % 2]
                P.op("act", lambda e, c=c, s=s: e.activation(out=s[:, 0:NM], in_=cf[:, c, 0:NM], func=AF.Square),
                     reads=[Rcf[c]], writes=[Rs])
                P.op("pe", lambda e, c=c, s=s: e.matmul(psum[:, bS3, 0:NM], lhsT=ones, rhs=s[:, 0:NM], start=(c == 0), stop=(c == 3)),
                     reads=[Rones, Rs], writes=[Rps[bS3]], inc=True)
            yield
            P.op("dve", lambda e: e.tensor_scalar(out=tmpf[:, 0:NM], in0=psum[:, bS3, 0:NM], scalar1=1.0 / 512, scalar2=4 * EPS,
                                                  op0=ALU.mult, op1=ALU.add),
                 reads=[Rps[bS3]], writes=[Rtmpf])
            rstd_from_var(NM, bS2)
            RyT_all = RyT[0:nb]
            for c in range(4):
                P.op("dve", lambda e, c=c: e.scalar_tensor_tensor(out=yT[:, 4 + c, c0:c0 + NM], in0=cf[:, c, 0:NM],
                                                                  scalar=pcol[:, PC_CONV + 12 + c:PC_CONV + 13 + c],
                                                                  in1=rstd[:, 0:NM], op0=ALU.mult, op1=ALU.mult),
                     reads=[Rcf[c], Rpcol, Rrstd], writes=RyT_all)

            yield

        if ti + 2 < len(tiles):
            load_xT(ti + 2)
        pre_list = list(range(N_EARLY)) if (ti == len(tiles) - 1 and KB > 0) else []
        def S0(j):
            bi = b0 + j
            qc0 = j * 128
            Pb = Pt[bi % 2]; RPb = RPt[bi % 2]
            for kvh in range(2):
                for kb in range(2):
                    kbi = bi - 1 + kb
                    sb_ = ps_sc[(kvh * 2 + kb) % 2]
                    for g in range(4):
                        h = kvh * 4 + g
                        half = h % 2
                        P.op("pe", lambda e, sb_=sb_, g=g, kvh=kvh, kbi=kbi, half=half, h=h: e.matmul(
                            psum[:, sb_, g * 128:(g + 1) * 128],
                            lhsT=kT[:, kvh, half, kbi * 128:(kbi + 1) * 128],
                            rhs=qT[:, h // 2, qc0:qc0 + 128], start=True, stop=True),
                            reads=[RkT, RqT], writes=[Rps[sb_]], inc=(g == 3))
                    es = Esb[(kvh * 2 + kb) % 2]; Res_ = REsb[(kvh * 2 + kb) % 2]
                    P.op("act", lambda e, sb_=sb_, es=es: e.activation(out=es, in_=psum[:, sb_, :], func=AF.Exp, scale=0.125),
                         reads=[Rps[sb_]], writes=[Res_])
                    P.op("dve", lambda e, es=es, kvh=kvh, kb=kb: e.tensor_tensor(
                        out=Pb[:, kvh, kb, :].rearrange("p (g q) -> p g q", g=4),
                        in0=es.rearrange("p (g q) -> p g q", g=4),
                        in1=EB[:, kb, kvh * 4:(kvh + 1) * 4, :], op=ALU.mult),
                        reads=[Res_, REB], writes=[RPb[kvh][kb]])

        def S1(j):
            bi = b0 + j
            qc0 = j * 128
            Pb = Pt[bi % 2]; RPb = RPt[bi % 2]
            sm = att[:, bi % 2, :]; Rsm = Ratt[bi % 2]
            for h in range(8):
                kvh, g = h // 4, h % 4
                pb_ = ps_pv[h // 4]
                for kb in range(2):
                    kbi = bi - 1 + kb
                    P.op("pe", lambda e, pb_=pb_, g=g, kvh=kvh, kb=kb, kbi=kbi: e.matmul(
                        psum[:, pb_, g * 65:(g + 1) * 65], lhsT=Pb[:, kvh, kb, g * 128:(g + 1) * 128],
                        rhs=vaug[:, kbi, kvh, :], start=(kb == 0), stop=(kb == 1)),
                        reads=[RPb[kvh][kb], Rvaug[kbi]], writes=[Rps[pb_]], inc=(kb == 1 and g == 3))
            pvv = psum[:, 4:6, 0:260].rearrange("p a (g e) -> p a g e", e=65)
            den = sm[:, 0:8]
            P.op("dve", lambda e: e.tensor_tensor(out=den.rearrange("p (a g) -> p a g", a=2), in0=pvv[:, :, :, 64],
                                                  in1=esink.rearrange("p (a g) -> p a g", a=2), op=ALU.add),
                 reads=[Rps[4], Rps[5], Resink], writes=[Rsm])
            P.op("dve", lambda e: e.reciprocal(out=den, in_=den), reads=[Rsm], writes=[Rsm])
            P.op("dve", lambda e: e.tensor_tensor(out=yat.rearrange("p (a g d) -> p a g d", a=2, g=4), in0=pvv[:, :, :, 0:64],
                                                  in1=den.rearrange("p (a g) -> p a g", a=2).unsqueeze(3).broadcast_to([128, 2, 4, 64]),
                                                  op=ALU.mult),
                 reads=[Rps[4], Rps[5], Rsm], writes=[Ryat])
            ss = sm[:, 8:9]
            P.op("act", lambda e: e.activation(out=junk, in_=yat, func=AF.Square, accum_out=ss),
                 reads=[Ryat], writes=[Rjunk, Rsm])
            P.op("dve", lambda e: e.tensor_scalar(out=sm[:, 9:10], in0=ss, scalar1=1.0 / 512, scalar2=EPS, op0=ALU.mult, op1=ALU.add),
                 reads=[Rsm], writes=[Rsm])
            P.op("pool", lambda e: e.tensor_tensor(out=sm[:, 10:11], in0=sm[:, 9:10], in1=mhalf[:, 0:1], op=ALU.pow),
                 reads=[Rsm, Rmhalf], writes=[Rsm])
            P.op("dve", lambda e: e.scalar_tensor_tensor(out=ya, in0=yat, scalar=sm[:, 10:11], in1=again_bc, op0=ALU.mult, op1=ALU.mult),
                 reads=[Ryat, Rsm, Ragain], writes=[Rya])
            for c in range(4):
                pb_ = ps_pv[c // 2]
                tv = psum[:, pb_, 384:512].bitcast(BF16)
                P.op("pe", lambda e, tv=tv, c=c: e.transpose(tv[:, (c % 2) * 128:(c % 2 + 1) * 128], ya[:, c * 128:(c + 1) * 128], ident),
                     reads=[Rya, Rident], writes=[Rps[pb_]], inc=True)
            for a in range(2):
                tv = psum[:, ps_pv[a], 384:512].bitcast(BF16)
                P.op("act", lambda e, tv=tv, a=a: e.activation(out=yT[:, 2 * a:2 * a + 2, qc0:qc0 + 128],
                                                               in_=tv.rearrange("p (c t) -> p c t", c=2), func=AF.Identity),
                     reads=[Rps[ps_pv[a]]], writes=[RyT[j]])

        def S2a(j):
            bi = b0 + j
            qc0 = j * 128
            xx = xt[bi % 3]; Rxx = Rxt[bi % 3]
            P.op("dve", lambda e: e.scalar_tensor_tensor(out=xx, in0=xx, scalar=ALPHA, in1=bout_bc, op0=ALU.mult, op1=ALU.add),
                 reads=[Rxx, Rbout], writes=[Rxx])
            for hf in range(2):
                ob = ps_op[hf]
                for kc in range(8):
                    P.op("pe", lambda e, ob=ob, kc=kc, hf=hf: e.matmul(psum[:, ob, :], lhsT=yT[:, kc, qc0:qc0 + 128],
                                                                       rhs=w_out[:, kc, hf * 512:(hf + 1) * 512],
                                                                       start=(kc == 0), stop=(kc == 7)),
                         reads=[RyT[j], Rw_out], writes=[Rps[ob]], inc=(kc == 7))
                P.op("dve", lambda e, ob=ob, hf=hf: e.tensor_tensor(out=xx[:, hf * 512:(hf + 1) * 512], in0=xx[:, hf * 512:(hf + 1) * 512],
                                                                    in1=psum[:, ob, :], op=ALU.add),
                     reads=[Rxx, Rps[ob]], writes=[Rxx])

            def tail():
                ln_tail(P, nc, xx, Rxx, lnst1[:, bi % 3, :], Rlnst1[bi % 3], mhalf, Rmhalf, ln1g_bc, Rln1g, ln1b_bc, Rln1b, 0, gb_eng="dve")
                P.dma("pool", "st_x1", x1s_d[(bi - 1) * 128:bi * 128, :], xx, reads=[Rxx])
                issue_xt(bi + 3)
            return tail

        blocks = [j for j in range(nb) if b0 + j != 0]
        pending = None
        sg_ = stats_gen()
        next(sg_)
        S0(blocks[0])
        next(sg_)
        if len(blocks) > 1:
            S0(blocks[1])
        S1(blocks[0])
        for _ in sg_:
            pass
        for idx, j in enumerate(blocks):
            if idx + 2 < len(blocks):
                S0(blocks[idx + 2])
            if idx + 1 < len(blocks):
                S1(blocks[idx + 1])
            tl = S2a(j)
            if pending is not None:
                pending()
            pending = tl
            if ti + 1 < len(tiles):
                nxt = [("ag", [0, 1]), ("ag", [2, 3]), ("conv", [0, 1]), ("conv", [2, 3])]
                per = (len(nxt) + len(blocks) - 1) // len(blocks)
                for kind_, chs_ in nxt[idx * per:(idx + 1) * per]:
                    (dense_ag if kind_ == "ag" else dense_conv)(ti + 1, chs_)
            npre = (len(pre_list) + (len(blocks) - idx) - 1) // (len(blocks) - idx)
            for p_ in pre_list[:npre]:
                load_wup(p_, EARLY_RES + [Rw_ag])
            pre_list = pre_list[npre:]
        if ti == len(tiles) - 1:
            if pending is not None:
                pending()
        else:
            carry[0] = pending

    P.barrier()
    A.off = keep_mark
    A.bf16(NFF, 8, 256)
    w_dn = A.bf16(NFF, D); Rw_dn = R()
    ln2g_bc = A.f32(D); Rln2g = R()
    ln2b_bc = A.f32(D); Rln2b = R()
    x1b = A.bf16(3, D); Rx1b = R()
    x1T = [A.bf16(8, 386) for _ in range(2)]; Rx1T = [R(), R()]
    tg = [A.f32(386) for _ in range(3)]; Rtg = [R(), R(), R()]
    tu = [A.f32(386) for _ in range(3)]; Rtu = [R(), R(), R()]
    sg = [A.f32(384) for _ in range(3)]; Rsg = [R(), R(), R()]
    actb = A.bf16(NFF, 384); Ract = [R() for _ in range(3)]
    xr = [A.f32(D) for _ in range(3)]; Rxr = [R(), R(), R()]
    lnst = A.f32(3, 16); Rlnst = [R(), R(), R()]

    LATE_WUP = list(range(N_EARLY, NFF))
    P.dma("sp", "ld_c2", ln2g_bc, ln2g_d.partition_broadcast(128), writes=[Rln2g])
    P.dma("sp", "ld_c2", ln2b_bc, ln2b_d.partition_broadcast(128), writes=[Rln2b])

    ps_tr = 6
    ps_up = [[0, 1], [2, 3], [4, 5]]
    ps_dn = [5, 6]

    def load_transpose(row0, nblk, dstT, RdstT, col0):
        for j in range(nblk):
            P.dma(wq, "ld_x1b", x1b[:, j, :], x1s_d[row0 + j * 128:row0 + (j + 1) * 128, :], writes=[Rx1b], par=True)
        for j in range(nblk):
            tv = psum[:, ps_tr, :].bitcast(BF16)
            for k in range(8):
                P.op("pe", lambda e, j=j, k=k, tv=tv: e.transpose(tv[:, k * 128:(k + 1) * 128], x1b[:, j, k * 128:(k + 1) * 128], ident),
                     reads=[Rx1b, Rident], writes=[Rps[ps_tr]], inc=(k == 7))
            P.op("act", lambda e, j=j, tv=tv: e.activation(out=dstT[:, :, col0 + j * 128:col0 + (j + 1) * 128],
                                                           in_=tv.rearrange("p (k t) -> p k t", k=8), func=AF.Identity),
                 reads=[Rps[ps_tr]], writes=[RdstT])

    ftiles = [(384 * i, 384) for i in range(5)] + [(1920, 128)]
    ftiles = ftiles[:KB]
    deferred = []
    dn_banks = [6, 7]
    dn_ctr = [0]

    def prep_tile(fi):
        tk0, T = ftiles[fi]
        cur = x1T[fi % 2]; Rcur = Rx1T[fi % 2]
        if fi == 0:
            load_transpose(0, 1, cur, Rcur, 2)
            P.op("dve", lambda e: e.tensor_scalar(out=cur[:, :, 0:2], in0=cur[:, :, 128:130], scalar1=flag[:, 0:1],
                                                  scalar2=None, op0=ALU.mult),
                 reads=[Rcur, Rflag], writes=[Rcur])
        else:
            pT = x1T[(fi - 1) % 2]; Tp = ftiles[fi - 1][1]
            P.op("dve", lambda e: e.tensor_copy(out=cur[:, :, 0:2], in_=pT[:, :, Tp:Tp + 2]),
                 reads=[Rx1T[(fi - 1) % 2]], writes=[Rcur])
        load_transpose(128 + tk0, T // 128, cur, Rcur, 2)

    def issue_xr(fi_, j_):
        if fi_ < len(ftiles) and j_ < ftiles[fi_][1] // 128:
            gb_ = ftiles[fi_][0] // 128 + j_
            P.dma("sp", "ld_xr", xr[j_], x1s_d[128 + gb_ * 128:256 + gb_ * 128, :], writes=[Rxr[j_]])

    if ftiles:
        prep_tile(0)
        for j_ in range(3):
            issue_xr(0, j_)
    for p_ in LATE_WUP:
        load_wup(p_, [])
    for p_ in range(NFF):
        P.dma(wq, "ld_wd", w_dn[:, p_, :], w_down_d[p_ * 128:(p_ + 1) * 128, :], writes=[Rw_dn], par=True)
    for fi, (tk0, T) in enumerate(ftiles):
        nblk = T // 128
        cur = x1T[fi % 2]; Rcur = Rx1T[fi % 2]
        NC = T + 2
        for p_ in range(NFF):
            if p_ == 3 and fi + 1 < len(ftiles):
                prep_tile(fi + 1)
            bg, bu = ps_up[p_ % 3]
            for gi, bb in ((0, bg), (1, bu)):
                for k in range(8):
                    P.op("pe", lambda e, bb=bb, k=k, gi=gi: e.matmul(psum[:, bb, 0:NC], lhsT=w_up[:, p_, k, gi * 128:(gi + 1) * 128],
                                                                     rhs=cur[:, k, 0:NC], start=(k == 0), stop=(k == 7)),
                         reads=[Rw_up[p_], Rcur], writes=[Rps[bb]], inc=(k == 7))
            tgb, Rtgb = tg[p_ % 3], Rtg[p_ % 3]
            tub, Rtub = tu[p_ % 3], Rtu[p_ % 3]
            sgb, Rsgb = sg[p_ % 3], Rsg[p_ % 3]
            for (tb, Rtb, bb, ch) in ((tgb, Rtgb, bg, p_), (tub, Rtub, bu, NFF + p_)):
                wc = PC_FW + ch * 3
                P.op("act", lambda e, tb=tb, bb=bb, wc=wc, ch=ch: e.activation(
                    out=tb[:, 0:T], in_=psum[:, bb, 2:NC], func=AF.Identity,
                    scale=pcol[:, wc + 2:wc + 3], bias=pcol[:, PC_FB + ch:PC_FB + ch + 1]),
                    reads=[Rps[bb], Rpcol], writes=[Rtb])
                P.op("dve", lambda e, tb=tb, bb=bb, wc=wc: e.scalar_tensor_tensor(
                    out=tb[:, 0:T], in0=psum[:, bb, 1:NC - 1], scalar=pcol[:, wc + 1:wc + 2], in1=tb[:, 0:T],
                    op0=ALU.mult, op1=ALU.add), reads=[Rps[bb], Rpcol, Rtb], writes=[Rtb])
                P.op("dve", lambda e, tb=tb, bb=bb, wc=wc: e.scalar_tensor_tensor(
                    out=tb[:, 0:T], in0=psum[:, bb, 0:NC - 2], scalar=pcol[:, wc:wc + 1], in1=tb[:, 0:T],
                    op0=ALU.mult, op1=ALU.add), reads=[Rps[bb], Rpcol, Rtb], writes=[Rtb])
            P.op("act", lambda e: e.activation(out=sgb[:, 0:T], in_=tgb[:, 0:T], func=AF.Silu),
                 reads=[Rtgb], writes=[Rsgb])
            P.op("pool", lambda e: e.tensor_tensor(out=actb[:, p_, 0:T], in0=sgb[:, 0:T], in1=tub[:, 0:T], op=ALU.mult),
                 reads=[Rsgb, Rtub], writes=Ract[0:nblk])
            if p_ in (6, 11, 16) and deferred:
                deferred.pop(0)()
        for j in range(nblk):
            gb = (tk0 // 128) + j
            xx = xr[j]; Rxx = Rxr[j]
            for hf in range(2):
                ob = dn_banks[dn_ctr[0] % 2]
                dn_ctr[0] += 1
                for p_ in range(NFF):
                    P.op("pe", lambda e, ob=ob, p_=p_, hf=hf, j=j: e.matmul(psum[:, ob, :], lhsT=actb[:, p_, j * 128:(j + 1) * 128],
                                                                            rhs=w_dn[:, p_, hf * 512:(hf + 1) * 512],
                                                                            start=(p_ == 0), stop=(p_ == NFF - 1)),
                         reads=[Ract[j], Rw_dn], writes=[Rps[ob]], inc=(p_ == NFF - 1))
                P.op("dve", lambda e, ob=ob, hf=hf: e.scalar_tensor_tensor(out=xx[:, hf * 512:(hf + 1) * 512],
                                                                           in0=xx[:, hf * 512:(hf + 1) * 512], scalar=ALPHA,
                                                                           in1=psum[:, ob, :], op0=ALU.mult, op1=ALU.add),
                     reads=[Rxx, Rps[ob]], writes=[Rxx])

            def tail(xx=xx, Rxx=Rxx, j=j, gb=gb, fi=fi):
                ln_tail(P, nc, xx, Rxx, lnst[:, j, :], Rlnst[j], mhalf, Rmhalf, ln2g_bc, Rln2g, ln2b_bc, Rln2b, 0)
                P.dma("pool", "st_out", out_d[gb * 128:(gb + 1) * 128, :], xx, reads=[Rxx])
                issue_xr(fi + 1, j)
            deferred.append(tail)
    while deferred:
        deferred.pop(0)()

    P.wait_all("sp")
    P.wait_all("pool")
    P.emit()
    return nc


def ln_tail(P, nc, xx, Rxx, sc, Rsc, mhalf, Rmhalf, g_bc, Rg, b_bc, Rb, so, gb_eng="pool"):
    st = sc[:, so:so + 12]
    mv = sc[:, so + 12:so + 14]
    P.op("dve", lambda e: e.bn_stats(out=st[:, 0:6], in_=xx[:, 0:512]), reads=[Rxx], writes=[Rsc])
    P.op("dve", lambda e: e.bn_stats(out=st[:, 6:12], in_=xx[:, 512:1024]), reads=[Rxx], writes=[Rsc])
    P.op("dve", lambda e: e.bn_aggr(out=mv, in_=st), reads=[Rsc], writes=[Rsc])
    P.op("dve", lambda e: e.tensor_scalar(out=sc[:, so + 14:so + 15], in0=mv[:, 1:2], scalar1=EPS, scalar2=None, op0=ALU.add),
         reads=[Rsc], writes=[Rsc])
    P.op("pool", lambda e: e.tensor_tensor(out=sc[:, so + 15:so + 16], in0=sc[:, so + 14:so + 15], in1=mhalf[:, 0:1], op=ALU.pow),
         reads=[Rsc, Rmhalf], writes=[Rsc])
    P.op("dve", lambda e: e.scalar_tensor_tensor(out=sc[:, so + 14:so + 15], in0=mv[:, 0:1], scalar=-1.0, in1=sc[:, so + 15:so + 16],
                                                 op0=ALU.mult, op1=ALU.mult),
         reads=[Rsc], writes=[Rsc])
    P.op("act", lambda e: e.activation(out=xx, in_=xx, func=AF.Identity, scale=sc[:, so + 15:so + 16], bias=sc[:, so + 14:so + 15]),
         reads=[Rxx, Rsc], writes=[Rxx])
    P.op(gb_eng, lambda e: e.tensor_tensor(out=xx, in0=xx, in1=g_bc, op=ALU.mult), reads=[Rxx, Rg], writes=[Rxx])
    P.op(gb_eng, lambda e: e.tensor_tensor(out=xx, in0=xx, in1=b_bc, op=ALU.add), reads=[Rxx, Rb], writes=[Rxx])


def _bucket_idx():
    q = np.arange(128)[:, None]
    kc = np.arange(256)[None, :]
    dist = q + 128 - kc
    n = np.maximum(dist, 0)
    nf = np.maximum(n, 16).astype(np.float32)
    large = 16 + (np.log(nf / np.float32(16)) / np.float32(math.log(128 / 16)) * np.float32(16)).astype(np.int32)
    large = np.minimum(large, 31)
    bucket = np.where(n < 16, n, large)
    ok = (dist >= 0) & (dist < 128)
    return bucket, ok


def _cols(v, nchunk):
    return np.ascontiguousarray(np.asarray(v, np.float32).reshape(nchunk, 128).T)


_NC_CACHE = {}


def kernel(x, w_in, b_in, attn_sinks, rel_bias_table, conv_dw_w, conv_dw_b, conv_ln_g, conv_ln_b,
           attn_out_gain, conv_out_gain, w_out, b_out, ln1_g, ln1_b, w_up, ffn_dw_w, ffn_dw_b, w_down,
           ln2_g, ln2_b):
    f = lambda a: np.asarray(a, dtype=np.float32)
    x = f(x)
    w_in2, b_in1 = f(w_in)[0], f(b_in)[0]
    bq, bk, bvv, ba, bg = b_in1[0:512], b_in1[512:640], b_in1[640:768], b_in1[768:1280], b_in1[1280:1792]
    pcol = np.zeros((128, PC_N), np.float32)
    pcol[:, 0:4] = _cols(bq, 4)
    pcol[:, 4] = np.concatenate([bk[0:64], bk[0:64]])
    pcol[:, 5] = np.concatenate([bk[64:128], bk[64:128]])
    pcol[:, 6:10] = _cols(ba, 4)
    pcol[:, 10:14] = _cols(bg, 4)
    pcol[:, 14:18] = _cols(f(conv_dw_b)[0], 4)
    pcol[:, 18:22] = _cols(f(conv_ln_g)[0], 4)
    pcol[:, 22:26] = _cols(f(conv_ln_b)[0], 4)
    pcol[:, 26:30] = _cols(f(conv_out_gain)[0], 4)
    cw = f(conv_dw_w)[0]
    pcol[:, PC_CW:PC_CW + 124] = cw.reshape(31, 4, 128).transpose(2, 1, 0).reshape(128, 124)
    fw = f(ffn_dw_w)[0]
    pcol[:, PC_FW:PC_FW + 132] = fw.reshape(3, 44, 128).transpose(2, 1, 0).reshape(128, 132)
    pcol[:, PC_FB:PC_FB + 44] = _cols(f(ffn_dw_b)[0], 44)
    bucket, ok = _bucket_idx()
    tab = f(rel_bias_table)
    bias_full = tab[bucket]
    biasg = np.ascontiguousarray(bias_full.reshape(128, 2, 128, 8).transpose(2, 1, 3, 0)).reshape(128, 2 * 8 * 128)
    maskc = np.ascontiguousarray(ok.astype(np.float32).reshape(128, 2, 128).transpose(2, 1, 0)).reshape(128, 256)
    shared = {
        "w_in": w_in2, "w_out": f(w_out)[0], "w_up": f(w_up)[0], "w_down": f(w_down)[0], "pcol": pcol,
        "bv": bvv.reshape(1, 128), "again": f(attn_out_gain), "bout": f(b_out), "ln1g": f(ln1_g), "ln1b": f(ln1_b),
        "ln2g": f(ln2_g), "ln2b": f(ln2_b), "sinks": f(attn_sinks), "biasg": biasg, "maskc": maskc,
    }
    in_maps = []
    for core in range(8):
        b, c = core // 4, core % 4
        s0 = c * OWN
        xh = np.zeros((NTOK, D), np.float32)
        lo = s0 - 256
        if lo >= 0:
            xh[:] = x[b, lo:s0 + OWN]
        else:
            xh[256:] = x[b, 0:OWN]
        m = dict(shared)
        m["xh"] = xh
        m["xT"] = np.ascontiguousarray(xh.T)
        m["flag"] = np.full((128, 1), 1.0 if c > 0 else 0.0, np.float32)
        in_maps.append(m)
    if "nc" not in _NC_CACHE:
        _NC_CACHE["nc"] = build()
    res = run_bass_kernel_spmd(_NC_CACHE["nc"], in_maps, core_ids=list(range(8)))
    out = np.empty((2, 8192, D), np.float32)
    for core in range(8):
        b, c = core // 4, core % 4
        out[b, c * OWN:(c + 1) * OWN] = res.results[core]["out"]
    if DEBUG:
        kernel.dbg = [r for r in res.results]
    return out
```

```python
import math
import numpy as np
import concourse.bass as bass
import concourse.mybir as mybir
from concourse.bass_utils import run_bass_kernel_spmd

F32 = mybir.dt.float32
BF16 = mybir.dt.bfloat16
AF = mybir.ActivationFunctionType
ALU = mybir.AluOpType


class Res:
    __slots__ = ("name", "lw", "rd", "lws")

    def __init__(self, name):
        self.name = name
        self.lw = None
        self.lws = []
        self.rd = []


class Lane:
    def __init__(self, name, sem, unit):
        self.name, self.sem, self.unit, self.count = name, sem, unit, 0


class Eng:
    def __init__(self, name, lane):
        self.name, self.lane, self.ops, self.known = name, lane, [], {}


class _Rec:
    def __getattr__(self, name):
        return lambda *a, **k: (name, a, k)


class Prog:
    def __init__(self, nc):
        self.nc = nc
        self.eng = {}
        for n in ("pe", "act", "dve", "pool", "sp"):
            self.eng[n] = Eng(n, Lane(n, nc.alloc_semaphore("s_" + n), 1))
        self.dma_lanes = {}
        self.lane_rr = {}
        self.nres = 0

    def res(self, name=None):
        self.nres += 1
        return Res(name or f"r{self.nres}")

    LANES = {"ld_c": 4, "ld_w": 8, "ld_w2": 4, "ld_x": 8, "ld_xt": 3, "st_x1": 2, "ld_wu": 12, "ld_wd": 6,
             "ld_x1b": 3, "ld_xr": 3, "st_out": 2, "ld_c2": 2}

    def lane(self, name):
        k = self.LANES.get(name, 2)
        i = self.lane_rr.get(name, 0)
        self.lane_rr[name] = i + 1
        key = f"{name}{i % k}"
        if key not in self.dma_lanes:
            self.dma_lanes[key] = Lane(key, self.nc.alloc_semaphore("d_" + key), 16)
        return self.dma_lanes[key]

    def _deps(self, e, reads, writes, par=False):
        need = {}

        def add(t):
            if t is not None and need.get(t[0], 0) < t[1]:
                need[t[0]] = t[1]
        for r in reads:
            add(r.lw)
            for t in r.lws:
                add(t)
        for w in writes:
            for t in w.rd:
                add(t)
            if par and not w.rd:
                continue
            add(w.lw)
            for t in w.lws:
                add(t)
        waits = []
        for ln, idx in need.items():
            if ln is e.lane and e.name == "pe":
                continue
            if e.known.get(ln, 0) >= idx:
                continue
            e.known[ln] = idx
            waits.append((ln.sem, idx * ln.unit))
        return waits

    def _mark(self, t, reads, writes, par=False):
        for r in reads:
            r.rd.append(t)
        for w in writes:
            if par:
                if w.rd:
                    w.lw, w.lws, w.rd = None, [], []
                w.lws.append(t)
            else:
                w.lw, w.lws, w.rd = t, [], []

    def op(self, en, fn, reads=(), writes=(), inc=True):
        name, a, k = fn(_Rec())
        fn = (lambda eng, name=name, a=a, k=k: getattr(eng, name)(*a, **k))
        e = self.eng[en]
        waits = self._deps(e, reads, writes)
        ln = e.lane
        if inc:
            ln.count += 1
            idx = ln.count
        else:
            idx = ln.count + 1
        e.ops.append((waits, fn, ln if inc else None, 1))
        self._mark((ln, idx), reads, writes)

    def dma(self, qn, lane_name, out, in_, reads=(), writes=(), par=False):
        e = self.eng[qn]
        ln = self.lane(lane_name)
        waits = self._deps(e, reads, writes, par)
        if ln.count and e.known.get(ln, 0) < ln.count:
            e.known[ln] = ln.count
            waits.append((ln.sem, ln.count * ln.unit))
        ln.count += 1
        e.ops.append((waits, (lambda eng, o=out, i=in_: eng.dma_start(out=o, in_=i)), ln, 16))
        self._mark((ln, ln.count), reads, writes, par)

    def barrier(self):
        lanes = [e.lane for e in self.eng.values()] + list(self.dma_lanes.values())
        for e in self.eng.values():
            waits = []
            for ln in lanes:
                if ln is e.lane or ln.count == 0:
                    continue
                if e.known.get(ln, 0) < ln.count:
                    e.known[ln] = ln.count
                    waits.append((ln.sem, ln.count * ln.unit))
            e.ops.append((waits, None, None, 0))

    def wait_all(self, en):
        e = self.eng[en]
        waits = []
        for ln in self.dma_lanes.values():
            if ln.count and e.known.get(ln, 0) < ln.count:
                e.known[ln] = ln.count
                waits.append((ln.sem, ln.count * ln.unit))
        e.ops.append((waits, None, None, 0))

    def emit(self):
        with self.nc.Block() as block:
            def run(e, eng):
                for waits, fn, ln, amt in e.ops:
                    for sem, val in waits:
                        eng.wait_ge(sem, val)
                    if fn is None:
                        continue
                    ins = fn(eng)
                    if ln is not None:
                        ins.then_inc(ln.sem, amt)

            @block.tensor
            def _(eng):
                run(self.eng["pe"], eng)

            @block.scalar
            def _(eng):
                run(self.eng["act"], eng)

            @block.vector
            def _(eng):
                run(self.eng["dve"], eng)

            @block.gpsimd
            def _(eng):
                run(self.eng["pool"], eng)

            @block.sync
            def _(eng):
                run(self.eng["sp"], eng)


class Arena:
    def __init__(self, nc, words):
        self.t = nc.alloc_sbuf_tensor("arena", [128, words], F32)
        self.words = words
        self.off = 0

    def _take(self, n):
        o = self.off
        self.off += n
        assert self.off <= self.words, f"arena overflow {self.off * 4} > {self.words * 4}"
        return o

    @staticmethod
    def _shape(v, dims):
        if len(dims) == 1:
            return v
        names = " ".join(f"d{i}" for i in range(len(dims)))
        kw = {f"d{i}": d for i, d in enumerate(dims[:-1])}
        return v.rearrange(f"p ({names}) -> p {names}", **kw)

    def f32(self, *dims):
        n = int(np.prod(dims))
        o = self._take(n)
        return self._shape(self.t[:, o:o + n], dims)

    def bf16(self, *dims):
        n = int(np.prod(dims))
        n4 = (n + 1) // 2
        o = self._take(n4)
        v = self.t[:, o:o + n4].bitcast(BF16)[:, 0:n]
        return self._shape(v, dims)


D = 1024
NTOK = 2304
NBLK = 18
OWN = 2048
DFF = 2816
NFF = 22
ALPHA = float(2.0 ** 0.25)
EPS = 1e-5
W_IN_COLS = 14 * 128 + 128
PC_IN, PC_CONV, PC_CW, PC_FW, PC_FB = 0, 14, 30, 154, 286
PC_N = 330

DEBUG = False
KA, KB, KSUB = 99, 99, 9

def build():
    nc = bass.Bass("TRN2", target_bir_lowering=False)
    dt = lambda n, s, k="ExternalInput": nc.dram_tensor(n, s, F32, kind=k).ap()
    xh_d = dt("xh", [NTOK, D])
    xT_d = dt("xT", [D, NTOK])
    flag_d = dt("flag", [128, 1])
    w_in_d = dt("w_in", [D, 1792])
    w_out_d = dt("w_out", [D, D])
    w_up_d = dt("w_up", [D, 2 * DFF])
    w_down_d = dt("w_down", [DFF, D])
    pcol_d = dt("pcol", [128, PC_N])
    bv_d = dt("bv", [1, 128])
    again_d = dt("again", [1, 512])
    bout_d = dt("bout", [1, D])
    ln1g_d = dt("ln1g", [1, D])
    ln1b_d = dt("ln1b", [1, D])
    ln2g_d = dt("ln2g", [1, D])
    ln2b_d = dt("ln2b", [1, D])
    sinks_d = dt("sinks", [1, 8])
    biasg_d = dt("biasg", [128, 2 * 8 * 128])
    maskc_d = dt("maskc", [128, 2 * 128])
    out_d = dt("out", [OWN, D], "ExternalOutput")
    x1s_d = dt("x1s", [NTOK - 128, D], "ExternalOutput" if DEBUG else "Internal")

    P = Prog(nc)
    A = Arena(nc, 53000)
    psum = nc.alloc_psum_tensor("psum", [128, 8, 512], F32)
    R = P.res
    Rps = [R(f"ps{i}") for i in range(8)]

    pcol = A.f32(PC_N); Rpcol = R()
    flag = A.f32(1); Rflag = R()
    ident = A.bf16(128); Rident = R()
    ones = A.f32(128); Rones = R()
    mhalf = A.f32(512); Rmhalf = R()
    halfb = A.f32(8); Rhalfb = R()
    small = A.f32(64); Rsmall = R()
    identf = A.f32(128); Ridentf = R()
    rdt = A.f32(8); Rrdt = R()
    keep_mark = A.off

    P.dma("sp", "ld_c", pcol, pcol_d, writes=[Rpcol])
    P.dma("sp", "ld_c", flag, flag_d, writes=[Rflag])
    P.op("pool", lambda e: e.memset(ident, 0.0), writes=[Rident])
    P.op("pool", lambda e: e.affine_select(out=ident, in_=ident, compare_op=ALU.not_equal, fill=1.0,
                                          base=0, pattern=[[-1, 128]], channel_multiplier=1),
         reads=[Rident], writes=[Rident])
    P.op("pool", lambda e: e.memset(ones, 1.0), writes=[Rones])
    P.op("pool", lambda e: e.tensor_copy(out=identf, in_=ident), reads=[Rident], writes=[Ridentf])
    P.op("pool", lambda e: e.memset(mhalf, -0.5), writes=[Rmhalf])
    P.op("dve", lambda e: e.tensor_scalar(out=halfb[:, 0:4], in0=pcol[:, PC_IN + 6:PC_IN + 10], scalar1=0.5, scalar2=None, op0=ALU.mult),
         reads=[Rpcol], writes=[Rhalfb])
    P.op("dve", lambda e: e.tensor_scalar(out=halfb[:, 4:8], in0=pcol[:, PC_IN + 10:PC_IN + 14], scalar1=0.5, scalar2=None, op0=ALU.mult),
         reads=[Rpcol], writes=[Rhalfb])

    w_in = A.bf16(8, W_IN_COLS); Rw_in = R()
    diag = A.bf16(124, 128); Rdiag = [R() for _ in range(4)]
    xT = [A.bf16(8, 512) for _ in range(2)]; RxT = [R(), R()]
    hT = [A.bf16(4, 30 + 512) for _ in range(2)]; RhT = [R(), R()]
    cf = A.f32(4, 512); Rcf = [R() for _ in range(4)]
    sq = [A.f32(512) for _ in range(2)]; Rsq = [R(), R()]
    rstd = A.f32(512); Rrstd = R()
    early_free_end = A.off
    EARLY_RES = [Rw_in] + Rdiag + RxT + RhT + Rcf + Rsq + [Rrstd]
    w_out = A.bf16(8, D); Rw_out = R()
    kT = A.bf16(2, 2, NTOK); RkT = R()
    vaug = A.bf16(NBLK, 2, 65); Rvaug = [R() for _ in range(NBLK)]
    EB = A.f32(2, 8, 128); REB = R()
    esink = A.f32(8); Resink = R()
    bv_bc = A.f32(128); Rbv = R()
    again_bc = A.f32(512); Ragain = R()
    bout_bc = A.f32(D); Rbout = R()
    ln1g_bc = A.f32(D); Rln1g = R()
    ln1b_bc = A.f32(D); Rln1b = R()
    qT = A.bf16(4, 512); RqT = R()
    Tb = A.f32(512); RTb = R()
    maskc = Tb[:, 0:256].rearrange("p (a q) -> p a q", a=2); Rmask = RTb
    ab = A.f32(512); Rab = R()
    mean = A.f32(512); Rmean = R()
    tmpf = A.f32(512); Rtmpf = R()
    yT = A.bf16(8, 512); RyT = [R() for _ in range(4)]
    Esb = [Tb, ab]; REsb = [RTb, Rab]
    Pt = [A.bf16(2, 2, 512) for _ in range(2)]
    RPt = [[[R(), R()], [R(), R()]] for _ in range(2)]
    yat = mean; Ryat = Rmean
    junk = tmpf; Rjunk = Rtmpf
    ya = A.bf16(512); Rya = R()
    xt = [A.f32(D) for _ in range(3)]; Rxt = [R(), R(), R()]
    lnst1 = A.f32(3, 16); Rlnst1 = [R(), R(), R()]
    att = A.f32(2, 16); Ratt = [R(), R()]

    wq = "pool"
    for k in range(8):
        P.dma(wq, "ld_x", xT[0][:, k, 0:256], xT_d[k * 128:(k + 1) * 128, 0:256], writes=[RxT[0]], par=True)
    _save = A.off
    A.off = keep_mark
    w_up = A.bf16(NFF, 8, 256); Rw_up = [R() for _ in range(NFF)]
    A.off = _save
    N_EARLY = min(NFF, (early_free_end - keep_mark) // 1024)
    w_up_v = w_up_d.rearrange("(k p) n -> p k n", p=128)

    def load_wup(p_, extra):
        for gi, off in enumerate((0, DFF)):
            c1 = off + p_ * 128
            P.dma(wq, "ld_wu", w_up[:, p_, :, gi * 128:(gi + 1) * 128], w_up_v[:, :, c1:c1 + 128],
                  writes=[Rw_up[p_]] + (extra if gi == 0 else []), par=True)
    w_in_v = w_in_d.rearrange("(k p) n -> p k n", p=128)
    Rw_ag = R()
    segs = [(768, 768, 512, Rw_ag), (1280, 1280, 512, Rw_ag), (0, 0, 512, Rw_in), (512, 512, 64, Rw_in), (576, 512, 64, Rw_in),
            (640, 576, 64, Rw_in), (704, 576, 64, Rw_in), (1792, 640, 128, Rw_in)]
    for si, (dst, src, n, rr) in enumerate(segs):
        if si == 2:
            P.op("pool", lambda e: e.memset(hT[0][:, :, 0:30], 0.0), writes=[RhT[0]])
            P.op("pool", lambda e: e.memset(kT[64:128, :, 0, :], 0.0), writes=[RkT])
            P.op("pool", lambda e: e.memset(kT[0:64, :, 1, :], 0.0), writes=[RkT])
        for k in range(8):
            P.dma(wq, "ld_w", w_in[:, k, dst:dst + n], w_in_v[:, k, src:src + n], writes=[rr], par=True)
    for k in range(8):
        P.dma(wq, "ld_w2", w_out[:, k, :], w_out_d[k * 128:(k + 1) * 128, :], writes=[Rw_out], par=True)

    P.dma("sp", "ld_c", bv_bc, bv_d.partition_broadcast(128), writes=[Rbv])
    P.dma("sp", "ld_c", again_bc, again_d.partition_broadcast(128), writes=[Ragain])
    P.dma("sp", "ld_c", bout_bc, bout_d.partition_broadcast(128), writes=[Rbout])
    P.dma("sp", "ld_c", ln1g_bc, ln1g_d.partition_broadcast(128), writes=[Rln1g])
    P.dma("sp", "ld_c", ln1b_bc, ln1b_d.partition_broadcast(128), writes=[Rln1b])
    P.dma("sp", "ld_c", esink, sinks_d.partition_broadcast(128), writes=[Resink])
    P.dma("sp", "ld_c", EB, biasg_d.rearrange("p (a h q) -> p a h q", a=2, h=8), writes=[REB])
    P.dma("sp", "ld_c", maskc, maskc_d.rearrange("p (a q) -> p a q", a=2), writes=[Rmask])
    P.op("act", lambda e: e.activation(out=esink, in_=esink, func=AF.Exp), reads=[Resink], writes=[Resink])
    P.op("act", lambda e: e.activation(out=EB, in_=EB, func=AF.Exp), reads=[REB], writes=[REB])
    for kb in range(2):
        P.op("dve", lambda e, kb=kb: e.tensor_tensor(out=EB[:, kb], in0=EB[:, kb],
                                                    in1=maskc[:, kb].unsqueeze(1).broadcast_to([128, 8, 128]), op=ALU.mult),
             reads=[REB, Rmask], writes=[REB])
    def build_diag(c):
        for j in range(31):
            i = c * 31 + j
            if j % 2 == 0:
                P.op("act", lambda e, i=i: e.activation(out=diag[:, i, :], in_=ident, func=AF.Identity,
                                                        scale=pcol[:, PC_CW + i:PC_CW + i + 1]),
                     reads=[Rident, Rpcol], writes=[Rdiag[c]])
            else:
                P.op("dve", lambda e, i=i: e.tensor_scalar(out=diag[:, i, :], in0=ident, scalar1=pcol[:, PC_CW + i:PC_CW + i + 1],
                                                           scalar2=None, op0=ALU.mult),
                     reads=[Rident, Rpcol], writes=[Rdiag[c]])
    P.op("dve", lambda e: e.memset(vaug[:, :, :, 64:65], 1.0), writes=Rvaug)

    ps_fm = [0, 1]
    ps_sc = [2, 3]
    ps_pv = [4, 5]
    ps_op = [6, 7]
    fm_ctr = [0]

    def fm_bank():
        b = ps_fm[fm_ctr[0] % 2]
        fm_ctr[0] += 1
        return b

    def rstd_from_var(NM_, bank):
        nbk = NM_ // 128
        t3 = tmpf[:, 0:NM_].rearrange("p (j t) -> p j t", j=nbk)
        P.op("dve", lambda e: e.tensor_tensor(out=t3, in0=t3, in1=identf.unsqueeze(1).broadcast_to([128, nbk, 128]), op=ALU.mult),
             reads=[Rtmpf, Ridentf], writes=[Rtmpf])
        P.op("dve", lambda e: e.tensor_reduce(out=rdt[:, 0:nbk], in_=t3, axis=mybir.AxisListType.X, op=ALU.add),
             reads=[Rtmpf], writes=[Rrdt])
        P.op("pool", lambda e: e.tensor_tensor(out=rdt[:, 4:4 + nbk], in0=rdt[:, 0:nbk], in1=mhalf[:, 0:nbk], op=ALU.pow),
             reads=[Rrdt, Rmhalf], writes=[Rrdt])
        P.op("dve", lambda e: e.tensor_tensor(out=t3, in0=identf.unsqueeze(1).broadcast_to([128, nbk, 128]),
                                              in1=rdt[:, 4:4 + nbk].unsqueeze(2).broadcast_to([128, nbk, 128]), op=ALU.mult),
             reads=[Ridentf, Rrdt], writes=[Rtmpf])
        for j_ in range(nbk):
            P.op("pe", lambda e, j_=j_: e.matmul(psum[:, bank, j_ * 128:(j_ + 1) * 128], lhsT=ones, rhs=tmpf[:, j_ * 128:(j_ + 1) * 128],
                                                 start=True, stop=True),
                 reads=[Rones, Rtmpf], writes=[Rps[bank]], inc=(j_ == nbk - 1))
        P.op("act", lambda e: e.activation(out=rstd[:, 0:NM_], in_=psum[:, bank, 0:NM_], func=AF.Identity),
             reads=[Rps[bank]], writes=[Rrstd])

    tiles = ([(0, 2)] + [(2 + 4 * i, 4) for i in range(4)])[:KA]

    def load_xT(ti_):
        b0_, nb_ = tiles[ti_]
        for k in range(8):
            P.dma(wq, "ld_x", xT[ti_ % 2][:, k, 0:nb_ * 128], xT_d[k * 128:(k + 1) * 128, b0_ * 128:(b0_ + nb_) * 128],
                  writes=[RxT[ti_ % 2]], par=True)

    carry = [None]

    def issue_xt(bi_):
        if 1 <= bi_ < NBLK:
            P.dma("sp", "ld_xt", xt[bi_ % 3], xh_d[bi_ * 128:(bi_ + 1) * 128, :], writes=[Rxt[bi_ % 3]])
    for bi_ in (1, 2, 3):
        issue_xt(bi_)
    def dense_ag(ti_, chunks):
        N_ = tiles[ti_][1] * 128
        xb_ = xT[ti_ % 2]; Rxb_ = RxT[ti_ % 2]
        hb_ = hT[ti_ % 2]; Rhb_ = RhT[ti_ % 2]

        def pc(ch):
            b = fm_bank()
            for k in range(8):
                P.op("pe", lambda e, b=b, k=k, ch=ch: e.matmul(psum[:, b, 0:N_], lhsT=w_in[:, k, ch * 128:(ch + 1) * 128],
                                                               rhs=xb_[:, k, 0:N_], start=(k == 0), stop=(k == 7)),
                     reads=[Rw_ag, Rxb_], writes=[Rps[b]], inc=(k == 7))
            return b
        if 0 in chunks and ti_ > 0:
            pb = hT[(ti_ - 1) % 2]; Npv = tiles[ti_ - 1][1] * 128
            P.op("pool", lambda e: e.tensor_copy(out=hb_[:, :, 0:30], in_=pb[:, :, Npv:Npv + 30]),
                 reads=[RhT[(ti_ - 1) % 2]], writes=[Rhb_])
        for c in chunks:
            b = pc(6 + c)
            P.op("act", lambda e, b=b, c=c: e.activation(out=ab[:, 0:N_], in_=psum[:, b, 0:N_], func=AF.Identity,
                                                         scale=0.5, bias=halfb[:, c:c + 1]),
                 reads=[Rps[b], Rhalfb], writes=[Rab])
            b2 = pc(10 + c)
            P.op("act", lambda e, b2=b2, c=c: e.activation(out=Tb[:, 0:N_], in_=psum[:, b2, 0:N_], func=AF.Tanh,
                                                           scale=0.5, bias=halfb[:, 4 + c:5 + c]),
                 reads=[Rps[b2], Rhalfb], writes=[RTb])
            P.op("dve", lambda e, c=c: e.scalar_tensor_tensor(out=hb_[:, c, 30:30 + N_], in0=Tb[:, 0:N_], scalar=1.0, in1=ab[:, 0:N_],
                                                              op0=ALU.add, op1=ALU.mult),
                 reads=[RTb, Rab], writes=[Rhb_])
            if ti_ == 0:
                build_diag(c)
        if 3 in chunks and ti_ == 0:
            P.op("dve", lambda e: e.tensor_scalar(out=hb_[:, :, 30:30 + N_], in0=hb_[:, :, 30:30 + N_], scalar1=flag[:, 0:1],
                                                  scalar2=None, op0=ALU.mult),
                 reads=[Rhb_, Rflag], writes=[Rhb_])

    def dense_conv(ti_, chunks):
        N_ = tiles[ti_][1] * 128
        hb_ = hT[ti_ % 2]; Rhb_ = RhT[ti_ % 2]
        c0_ = 128 if ti_ == 0 else 0
        NM_ = N_ - c0_
        for c in chunks:
            b = fm_bank()
            for j in range(31):
                P.op("pe", lambda e, b=b, c=c, j=j: e.matmul(psum[:, b, 0:NM_], lhsT=diag[:, c * 31 + j, :],
                                                             rhs=hb_[:, c, c0_ + j:c0_ + j + NM_], start=(j == 0), stop=(j == 30)),
                     reads=[Rdiag[c], Rhb_], writes=[Rps[b]], inc=(j == 30))
            P.op("act", lambda e, b=b, c=c: e.activation(out=cf[:, c, 0:NM_], in_=psum[:, b, 0:NM_], func=AF.Identity,
                                                         bias=pcol[:, PC_CONV + c:PC_CONV + c + 1]),
                 reads=[Rps[b], Rpcol], writes=[Rcf[c]])

    def dense_qkv(ti_):
        b0_, nb_ = tiles[ti_]
        N_ = nb_ * 128
        t0_ = b0_ * 128
        xb_ = xT[ti_ % 2]; Rxb_ = RxT[ti_ % 2]

        def pc(ch):
            b = fm_bank()
            for k in range(8):
                P.op("pe", lambda e, b=b, k=k, ch=ch: e.matmul(psum[:, b, 0:N_], lhsT=w_in[:, k, ch * 128:(ch + 1) * 128],
                                                               rhs=xb_[:, k, 0:N_], start=(k == 0), stop=(k == 7)),
                     reads=[Rw_in, Rxb_], writes=[Rps[b]], inc=(k == 7))
            return b
        for c in range(4):
            b = pc(c)
            P.op("act", lambda e, b=b, c=c: e.activation(out=qT[:, c, 0:N_], in_=psum[:, b, 0:N_], func=AF.Identity,
                                                         bias=pcol[:, PC_IN + c:PC_IN + c + 1]),
                 reads=[Rps[b], Rpcol], writes=[RqT])
        for kv in range(2):
            b = pc(4 + kv)
            for hh in range(2):
                pr = slice(hh * 64, (hh + 1) * 64)
                P.op("act", lambda e, b=b, kv=kv, hh=hh, pr=pr: e.activation(
                    out=kT[pr, kv, hh, t0_:t0_ + N_], in_=psum[pr, b, 0:N_], func=AF.Identity,
                    bias=pcol[pr, PC_IN + 4 + kv:PC_IN + 5 + kv]),
                    reads=[Rps[b], Rpcol], writes=[RkT])
        for j in range(nb_):
            bi = b0_ + j
            b = fm_bank()
            for k in range(8):
                P.op("pe", lambda e, b=b, k=k, j=j: e.matmul(psum[:, b, 0:128], lhsT=xb_[:, k, j * 128:(j + 1) * 128],
                                                             rhs=w_in[:, k, 1792:1920], start=(k == 0), stop=(k == 7)),
                     reads=[Rw_in, Rxb_], writes=[Rps[b]], inc=(k == 7))
            P.op("dve", lambda e, b=b, bi=bi: e.tensor_tensor(out=vaug[:, bi, :, 0:64],
                                                              in0=psum[:, b, 0:128].rearrange("p (a d) -> p a d", a=2),
                                                              in1=bv_bc.rearrange("p (a d) -> p a d", a=2), op=ALU.add),
                 reads=[Rps[b], Rbv], writes=[Rvaug[bi]])
            if ti_ == 0:
                P.op("dve", lambda e, bi=bi: e.tensor_scalar(out=vaug[:, bi], in0=vaug[:, bi], scalar1=flag[:, 0:1],
                                                             scalar2=None, op0=ALU.mult),
                     reads=[Rvaug[bi], Rflag], writes=[Rvaug[bi]])

    for ti, (b0, nb) in enumerate(tiles):
        N = nb * 128
        t0 = b0 * 128
        xb = xT[ti % 2]; Rxb = RxT[ti % 2]
        hb = hT[ti % 2]; Rhb = RhT[ti % 2]
        if ti == 0:
            if len(tiles) > 1:
                load_xT(1)

        def proj_chunk(ch):
            b = fm_bank()
            for k in range(8):
                P.op("pe", lambda e, b=b, k=k, ch=ch: e.matmul(psum[:, b, 0:N], lhsT=w_in[:, k, ch * 128:(ch + 1) * 128],
                                                               rhs=xb[:, k, 0:N], start=(k == 0), stop=(k == 7)),
                     reads=[Rw_ag if ch >= 6 else Rw_in, Rxb], writes=[Rps[b]], inc=(k == 7))
            return b
        def emit_qk():
            for c in range(4):
                b = proj_chunk(c)
                P.op("act", lambda e, b=b, c=c: e.activation(out=qT[:, c, 0:N], in_=psum[:, b, 0:N], func=AF.Identity,
                                                             bias=pcol[:, PC_IN + c:PC_IN + c + 1]),
                     reads=[Rps[b], Rpcol], writes=[RqT])
            for kv in range(2):
                b = proj_chunk(4 + kv)
                for hh in range(2):
                    pr = slice(hh * 64, (hh + 1) * 64)
                    P.op("act", lambda e, b=b, kv=kv, hh=hh, pr=pr: e.activation(
                        out=kT[pr, kv, hh, t0:t0 + N], in_=psum[pr, b, 0:N], func=AF.Identity,
                        bias=pcol[pr, PC_IN + 4 + kv:PC_IN + 5 + kv]),
                        reads=[Rps[b], Rpcol], writes=[RkT])

        if ti == 0:
            dense_ag(0, [0, 1, 2, 3])
        def emit_v():
            for j in range(nb):
                bi = b0 + j
                b = fm_bank()
                for k in range(8):
                    P.op("pe", lambda e, b=b, k=k, j=j: e.matmul(psum[:, b, 0:128], lhsT=xb[:, k, j * 128:(j + 1) * 128],
                                                                 rhs=w_in[:, k, 1792:1920], start=(k == 0), stop=(k == 7)),
                         reads=[Rw_in, Rxb], writes=[Rps[b]], inc=(k == 7))
                P.op("dve", lambda e, b=b, bi=bi: e.tensor_tensor(out=vaug[:, bi, :, 0:64],
                                                                  in0=psum[:, b, 0:128].rearrange("p (a d) -> p a d", a=2),
                                                                  in1=bv_bc.rearrange("p (a d) -> p a d", a=2), op=ALU.add),
                     reads=[Rps[b], Rbv], writes=[Rvaug[bi]])
                if ti == 0:
                    P.op("dve", lambda e, bi=bi: e.tensor_scalar(out=vaug[:, bi], in0=vaug[:, bi], scalar1=flag[:, 0:1],
                                                                 scalar2=None, op0=ALU.mult),
                         reads=[Rvaug[bi], Rflag], writes=[Rvaug[bi]])

        if carry[0] is not None:
            carry[0]()
            carry[0] = None
        c0 = 128 if ti == 0 else 0
        NM = N - c0
        if ti == 0:
            dense_conv(0, [0, 1, 2, 3])
        if ti == 0:
            dense_qkv(0)
        def stats_gen():
            bS1 = fm_bank()
            for c in range(4):
                P.op("pe", lambda e, c=c: e.matmul(psum[:, bS1, 0:NM], lhsT=ones, rhs=cf[:, c, 0:NM], start=(c == 0), stop=(c == 3)),
                     reads=[Rones, Rcf[c]], writes=[Rps[bS1]], inc=(c == 3))
            bS2 = fm_bank()
            for c in range(4):
                s = sq[c % 2]; Rs = Rsq[c % 2]
                P.op("act", lambda e, c=c, s=s: e.activation(out=s[:, 0:NM], in_=cf[:, c, 0:NM], func=AF.Square),
                     reads=[Rcf[c]], writes=[Rs])
                P.op("pe", lambda e, c=c, s=s: e.matmul(psum[:, bS2, 0:NM], lhsT=ones, rhs=s[:, 0:NM], start=(c == 0), stop=(c == 3)),
                     reads=[Rones, Rs], writes=[Rps[bS2]], inc=True)
            P.op("act", lambda e: e.activation(out=mean[:, 0:NM], in_=psum[:, bS1, 0:NM], func=AF.Identity, scale=1.0 / 512),
                 reads=[Rps[bS1]], writes=[Rmean])
            P.op("dve", lambda e: e.tensor_tensor(out=tmpf[:, 0:NM], in0=mean[:, 0:NM], in1=mean[:, 0:NM], op=ALU.mult),
                 reads=[Rmean], writes=[Rtmpf])
            P.op("dve", lambda e: e.scalar_tensor_tensor(out=tmpf[:, 0:NM], in0=psum[:, bS2, 0:NM], scalar=1.0 / 512, in1=tmpf[:, 0:NM],
                                                         op0=ALU.mult, op1=ALU.subtract),
                 reads=[Rps[bS2], Rtmpf], writes=[Rtmpf])
            P.op("dve", lambda e: e.tensor_scalar(out=tmpf[:, 0:NM], in0=tmpf[:, 0:NM], scalar1=EPS, scalar2=None, op0=ALU.add),
                 reads=[Rtmpf], writes=[Rtmpf])
            yield
            rstd_from_var(NM, bS1)
            bS3 = fm_bank()
            for c in range(4):
                P.op("dve", lambda e, c=c: e.tensor_tensor(out=cf[:, c, 0:NM], in0=cf[:, c, 0:NM], in1=mean[:, 0:NM], op=ALU.subtract),
                     reads=[Rcf[c], Rmean], writes=[Rcf[c]])
                P.op("dve", lambda e, c=c: e.tensor_tensor(out=cf[:, c, 0:NM], in0=cf[:, c, 0:NM], in1=rstd[:, 0:NM], op=ALU.mult),
                     reads=[Rcf[c], Rrstd], writes=[Rcf[c]])
                P.op("act", lambda e, c=c: e.activation(out=cf[:, c, 0:NM], in_=cf[:, c, 0:NM], func=AF.Identity,
                                                        scale=pcol[:, PC_CONV + 4 + c:PC_CONV + 5 + c],
                                                        bias=pcol[:, PC_CONV + 8 + c:PC_CONV + 9 + c]),
                     reads=[Rcf[c], Rpcol], writes=[Rcf[c]])
                P.op("act", lambda e, c=c: e.activation(out=Tb[:, 0:NM], in_=cf[:, c, 0:NM], func=AF.Tanh, scale=0.5),
                     reads=[Rcf[c]], writes=[RTb])
                P.op("dve", lambda e, c=c: e.scalar_tensor_tensor(out=cf[:, c, 0:NM], in0=Tb[:, 0:NM], scalar=1.0, in1=cf[:, c, 0:NM],
                                                                  op0=ALU.add, op1=ALU.mult),
                     reads=[RTb, Rcf[c]], writes=[Rcf[c]])
                s = sq[c % 2]; Rs = Rsq[c % 2]
                P.op("act", lambda e, c=c, s=s: e.activation(out=s[:, 0:NM], in_=cf[:, c, 0:NM], func=AF.Square),
                     reads=[Rcf[c]], writes=[Rs])
                P.op("pe", lambda e, c=c, s=s: e.matmul(psum[:, bS3, 0:NM], lhsT=ones, rhs=s[:, 0:NM], start=(c == 0), stop=(c == 3)),
                     reads=[Rones, Rs], writes=[Rps[bS3]], inc=True)
            yield
            P.op("dve", lambda e: e.tensor_scalar(out=tmpf[:, 0:NM], in0=psum[:, bS3, 0:NM], scalar1=1.0 / 512, scalar2=4 * EPS,
                                                  op0=ALU.mult, op1=ALU.add),
                 reads=[Rps[bS3]], writes=[Rtmpf])
            rstd_from_var(NM, bS2)
            RyT_all = RyT[0:nb]
            for c in range(4):
                P.op("dve", lambda e, c=c: e.scalar_tensor_tensor(out=yT[:, 4 + c, c0:c0 + NM], in0=cf[:, c, 0:NM],
                                                                  scalar=pcol[:, PC_CONV + 12 + c:PC_CONV + 13 + c],
                                                                  in1=rstd[:, 0:NM], op0=ALU.mult, op1=ALU.mult),
                     reads=[Rcf[c], Rpcol, Rrstd], writes=RyT_all)

            yield

        if ti + 2 < len(tiles):
            load_xT(ti + 2)
        pre_list = list(range(N_EARLY)) if (ti == len(tiles) - 1 and KB > 0) else []
        def S0(j):
            bi = b0 + j
            qc0 = j * 128
            Pb = Pt[bi % 2]; RPb = RPt[bi % 2]
            for kvh in range(2):
                for kb in range(2):
                    kbi = bi - 1 + kb
                    sb_ = ps_sc[(kvh * 2 + kb) % 2]
                    for g in range(4):
                        h = kvh * 4 + g
                        half = h % 2
                        P.op("pe", lambda e, sb_=sb_, g=g, kvh=kvh, kbi=kbi, half=half, h=h: e.matmul(
                            psum[:, sb_, g * 128:(g + 1) * 128],
                            lhsT=kT[:, kvh, half, kbi * 128:(kbi + 1) * 128],
                            rhs=qT[:, h // 2, qc0:qc0 + 128], start=True, stop=True),
                            reads=[RkT, RqT], writes=[Rps[sb_]], inc=(g == 3))
                    es = Esb[(kvh * 2 + kb) % 2]; Res_ = REsb[(kvh * 2 + kb) % 2]
                    P.op("act", lambda e, sb_=sb_, es=es: e.activation(out=es, in_=psum[:, sb_, :], func=AF.Exp, scale=0.125),
                         reads=[Rps[sb_]], writes=[Res_])
                    P.op("dve", lambda e, es=es, kvh=kvh, kb=kb: e.tensor_tensor(
                        out=Pb[:, kvh, kb, :].rearrange("p (g q) -> p g q", g=4),
                        in0=es.rearrange("p (g q) -> p g q", g=4),
                        in1=EB[:, kb, kvh * 4:(kvh + 1) * 4, :], op=ALU.mult),
                        reads=[Res_, REB], writes=[RPb[kvh][kb]])

        def S1(j):
            bi = b0 + j
            qc0 = j * 128
            Pb = Pt[bi % 2]; RPb = RPt[bi % 2]
            sm = att[:, bi % 2, :]; Rsm = Ratt[bi % 2]
            for h in range(8):
                kvh, g = h // 4, h % 4
                pb_ = ps_pv[h // 4]
                for kb in range(2):
                    kbi = bi - 1 + kb
                    P.op("pe", lambda e, pb_=pb_, g=g, kvh=kvh, kb=kb, kbi=kbi: e.matmul(
                        psum[:, pb_, g * 65:(g + 1) * 65], lhsT=Pb[:, kvh, kb, g * 128:(g + 1) * 128],
                        rhs=vaug[:, kbi, kvh, :], start=(kb == 0), stop=(kb == 1)),
                        reads=[RPb[kvh][kb], Rvaug[kbi]], writes=[Rps[pb_]], inc=(kb == 1 and g == 3))
            pvv = psum[:, 4:6, 0:260].rearrange("p a (g e) -> p a g e", e=65)
            den = sm[:, 0:8]
            P.op("dve", lambda e: e.tensor_tensor(out=den.rearrange("p (a g) -> p a g", a=2), in0=pvv[:, :, :, 64],
                                                  in1=esink.rearrange("p (a g) -> p a g", a=2), op=ALU.add),
                 reads=[Rps[4], Rps[5], Resink], writes=[Rsm])
            P.op("dve", lambda e: e.reciprocal(out=den, in_=den), reads=[Rsm], writes=[Rsm])
            P.op("dve", lambda e: e.tensor_tensor(out=yat.rearrange("p (a g d) -> p a g d", a=2, g=4), in0=pvv[:, :, :, 0:64],
                                                  in1=den.rearrange("p (a g) -> p a g", a=2).unsqueeze(3).broadcast_to([128, 2, 4, 64]),
                                                  op=ALU.mult),
                 reads=[Rps[4], Rps[5], Rsm], writes=[Ryat])
            ss = sm[:, 8:9]
            P.op("act", lambda e: e.activation(out=junk, in_=yat, func=AF.Square, accum_out=ss),
                 reads=[Ryat], writes=[Rjunk, Rsm])
            P.op("dve", lambda e: e.tensor_scalar(out=sm[:, 9:10], in0=ss, scalar1=1.0 / 512, scalar2=EPS, op0=ALU.mult, op1=ALU.add),
                 reads=[Rsm], writes=[Rsm])
            P.op("pool", lambda e: e.tensor_tensor(out=sm[:, 10:11], in0=sm[:, 9:10], in1=mhalf[:, 0:1], op=ALU.pow),
                 reads=[Rsm, Rmhalf], writes=[Rsm])
            P.op("dve", lambda e: e.scalar_tensor_tensor(out=ya, in0=yat, scalar=sm[:, 10:11], in1=again_bc, op0=ALU.mult, op1=ALU.mult),
                 reads=[Ryat, Rsm, Ragain], writes=[Rya])
            for c in range(4):
                pb_ = ps_pv[c // 2]
                tv = psum[:, pb_, 384:512].bitcast(BF16)
                P.op("pe", lambda e, tv=tv, c=c: e.transpose(tv[:, (c % 2) * 128:(c % 2 + 1) * 128], ya[:, c * 128:(c + 1) * 128], ident),
                     reads=[Rya, Rident], writes=[Rps[pb_]], inc=True)
            for a in range(2):
                tv = psum[:, ps_pv[a], 384:512].bitcast(BF16)
                P.op("act", lambda e, tv=tv, a=a: e.activation(out=yT[:, 2 * a:2 * a + 2, qc0:qc0 + 128],
                                                               in_=tv.rearrange("p (c t) -> p c t", c=2), func=AF.Identity),
                     reads=[Rps[ps_pv[a]]], writes=[RyT[j]])

        def S2a(j):
            bi = b0 + j
            qc0 = j * 128
            xx = xt[bi % 3]; Rxx = Rxt[bi % 3]
            P.op("dve", lambda e: e.scalar_tensor_tensor(out=xx, in0=xx, scalar=ALPHA, in1=bout_bc, op0=ALU.mult, op1=ALU.add),
                 reads=[Rxx, Rbout], writes=[Rxx])
            for hf in range(2):
                ob = ps_op[hf]
                for kc in range(8):
                    P.op("pe", lambda e, ob=ob, kc=kc, hf=hf: e.matmul(psum[:, ob, :], lhsT=yT[:, kc, qc0:qc0 + 128],
                                                                       rhs=w_out[:, kc, hf * 512:(hf + 1) * 512],
                                                                       start=(kc == 0), stop=(kc == 7)),
                         reads=[RyT[j], Rw_out], writes=[Rps[ob]], inc=(kc == 7))
                P.op("dve", lambda e, ob=ob, hf=hf: e.tensor_tensor(out=xx[:, hf * 512:(hf + 1) * 512], in0=xx[:, hf * 512:(hf + 1) * 512],
                                                                    in1=psum[:, ob, :], op=ALU.add),
                     reads=[Rxx, Rps[ob]], writes=[Rxx])

            def tail():
                ln_tail(P, nc, xx, Rxx, lnst1[:, bi % 3, :], Rlnst1[bi % 3], mhalf, Rmhalf, ln1g_bc, Rln1g, ln1b_bc, Rln1b, 0, gb_eng="dve")
                P.dma("pool", "st_x1", x1s_d[(bi - 1) * 128:bi * 128, :], xx, reads=[Rxx])
                issue_xt(bi + 3)
            return tail

        blocks = [j for j in range(nb) if b0 + j != 0]
        pending = None
        sg_ = stats_gen()
        next(sg_)
        S0(blocks[0])
        if ti + 1 < len(tiles):
            dense_ag(ti + 1, [0, 1])
        next(sg_)
        if len(blocks) > 1:
            S0(blocks[1])
        S1(blocks[0])
        for _ in sg_:
            pass
        if ti + 1 < len(tiles):
            dense_ag(ti + 1, [2, 3])
        for idx, j in enumerate(blocks):
            if idx + 2 < len(blocks):
                S0(blocks[idx + 2])
            if idx + 1 < len(blocks):
                S1(blocks[idx + 1])
            tl = S2a(j)
            if pending is not None:
                pending()
            pending = tl
            if ti + 1 < len(tiles):
                nxt = [[0], [1], [2], [3]]
                per = (len(nxt) + len(blocks) - 1) // len(blocks)
                for chs_ in nxt[idx * per:(idx + 1) * per]:
                    dense_conv(ti + 1, chs_)
                if idx == len(blocks) - 1:
                    dense_qkv(ti + 1)
            npre = (len(pre_list) + (len(blocks) - idx) - 1) // (len(blocks) - idx)
            for p_ in pre_list[:npre]:
                load_wup(p_, EARLY_RES + [Rw_ag])
            pre_list = pre_list[npre:]
        if ti == len(tiles) - 1:
            if pending is not None:
                pending()
        else:
            carry[0] = pending

    P.barrier()
    A.off = keep_mark
    A.bf16(NFF, 8, 256)
    w_dn = A.bf16(NFF, D); Rw_dn = R()
    ln2g_bc = A.f32(D); Rln2g = R()
    ln2b_bc = A.f32(D); Rln2b = R()
    x1b = A.bf16(3, D); Rx1b = R()
    x1T = [A.bf16(8, 386) for _ in range(2)]; Rx1T = [R(), R()]
    tg = [A.f32(386) for _ in range(3)]; Rtg = [R(), R(), R()]
    tu = [A.f32(386) for _ in range(3)]; Rtu = [R(), R(), R()]
    sg = [A.f32(384) for _ in range(3)]; Rsg = [R(), R(), R()]
    actb = A.bf16(NFF, 384); Ract = [R() for _ in range(3)]
    xr = [A.f32(D) for _ in range(3)]; Rxr = [R(), R(), R()]
    lnst = A.f32(3, 16); Rlnst = [R(), R(), R()]

    LATE_WUP = list(range(N_EARLY, NFF))
    P.dma("sp", "ld_c2", ln2g_bc, ln2g_d.partition_broadcast(128), writes=[Rln2g])
    P.dma("sp", "ld_c2", ln2b_bc, ln2b_d.partition_broadcast(128), writes=[Rln2b])

    ps_tr = 6
    ps_up = [[0, 1], [2, 3], [4, 5]]
    ps_dn = [5, 6]

    def load_transpose(row0, nblk, dstT, RdstT, col0):
        for j in range(nblk):
            P.dma(wq, "ld_x1b", x1b[:, j, :], x1s_d[row0 + j * 128:row0 + (j + 1) * 128, :], writes=[Rx1b], par=True)
        for j in range(nblk):
            tv = psum[:, ps_tr, :].bitcast(BF16)
            for k in range(8):
                P.op("pe", lambda e, j=j, k=k, tv=tv: e.transpose(tv[:, k * 128:(k + 1) * 128], x1b[:, j, k * 128:(k + 1) * 128], ident),
                     reads=[Rx1b, Rident], writes=[Rps[ps_tr]], inc=(k == 7))
            P.op("act", lambda e, j=j, tv=tv: e.activation(out=dstT[:, :, col0 + j * 128:col0 + (j + 1) * 128],
                                                           in_=tv.rearrange("p (k t) -> p k t", k=8), func=AF.Identity),
                 reads=[Rps[ps_tr]], writes=[RdstT])

    ftiles = [(384 * i, 384) for i in range(5)] + [(1920, 128)]
    ftiles = ftiles[:KB]
    deferred = []
    dn_banks = [6, 7]
    dn_ctr = [0]

    def prep_tile(fi):
        tk0, T = ftiles[fi]
        cur = x1T[fi % 2]; Rcur = Rx1T[fi % 2]
        if fi == 0:
            load_transpose(0, 1, cur, Rcur, 2)
            P.op("dve", lambda e: e.tensor_scalar(out=cur[:, :, 0:2], in0=cur[:, :, 128:130], scalar1=flag[:, 0:1],
                                                  scalar2=None, op0=ALU.mult),
                 reads=[Rcur, Rflag], writes=[Rcur])
        else:
            pT = x1T[(fi - 1) % 2]; Tp = ftiles[fi - 1][1]
            P.op("dve", lambda e: e.tensor_copy(out=cur[:, :, 0:2], in_=pT[:, :, Tp:Tp + 2]),
                 reads=[Rx1T[(fi - 1) % 2]], writes=[Rcur])
        load_transpose(128 + tk0, T // 128, cur, Rcur, 2)

    def issue_xr(fi_, j_):
        if fi_ < len(ftiles) and j_ < ftiles[fi_][1] // 128:
            gb_ = ftiles[fi_][0] // 128 + j_
            P.dma("sp", "ld_xr", xr[j_], x1s_d[128 + gb_ * 128:256 + gb_ * 128, :], writes=[Rxr[j_]])

    if ftiles:
        prep_tile(0)
        for j_ in range(3):
            issue_xr(0, j_)
    for p_ in LATE_WUP:
        load_wup(p_, [])
    for p_ in range(NFF):
        P.dma(wq, "ld_wd", w_dn[:, p_, :], w_down_d[p_ * 128:(p_ + 1) * 128, :], writes=[Rw_dn], par=True)
    for fi, (tk0, T) in enumerate(ftiles):
        nblk = T // 128
        cur = x1T[fi % 2]; Rcur = Rx1T[fi % 2]
        NC = T + 2
        for p_ in range(NFF):
            if p_ == 3 and fi + 1 < len(ftiles):
                prep_tile(fi + 1)
            bg, bu = ps_up[p_ % 3]
            for gi, bb in ((0, bg), (1, bu)):
                for k in range(8):
                    P.op("pe", lambda e, bb=bb, k=k, gi=gi: e.matmul(psum[:, bb, 0:NC], lhsT=w_up[:, p_, k, gi * 128:(gi + 1) * 128],
                                                                     rhs=cur[:, k, 0:NC], start=(k == 0), stop=(k == 7)),
                         reads=[Rw_up[p_], Rcur], writes=[Rps[bb]], inc=(k == 7))
            tgb, Rtgb = tg[p_ % 3], Rtg[p_ % 3]
            tub, Rtub = tu[p_ % 3], Rtu[p_ % 3]
            sgb, Rsgb = sg[p_ % 3], Rsg[p_ % 3]
            for (tb, Rtb, bb, ch) in ((tgb, Rtgb, bg, p_), (tub, Rtub, bu, NFF + p_)):
                wc = PC_FW + ch * 3
                P.op("act", lambda e, tb=tb, bb=bb, wc=wc, ch=ch: e.activation(
                    out=tb[:, 0:T], in_=psum[:, bb, 2:NC], func=AF.Identity,
                    scale=pcol[:, wc + 2:wc + 3], bias=pcol[:, PC_FB + ch:PC_FB + ch + 1]),
                    reads=[Rps[bb], Rpcol], writes=[Rtb])
                P.op("dve", lambda e, tb=tb, bb=bb, wc=wc: e.scalar_tensor_tensor(
                    out=tb[:, 0:T], in0=psum[:, bb, 1:NC - 1], scalar=pcol[:, wc + 1:wc + 2], in1=tb[:, 0:T],
                    op0=ALU.mult, op1=ALU.add), reads=[Rps[bb], Rpcol, Rtb], writes=[Rtb])
                P.op("dve", lambda e, tb=tb, bb=bb, wc=wc: e.scalar_tensor_tensor(
                    out=tb[:, 0:T], in0=psum[:, bb, 0:NC - 2], scalar=pcol[:, wc:wc + 1], in1=tb[:, 0:T],
                    op0=ALU.mult, op1=ALU.add), reads=[Rps[bb], Rpcol, Rtb], writes=[Rtb])
            P.op("act", lambda e: e.activation(out=sgb[:, 0:T], in_=tgb[:, 0:T], func=AF.Silu),
                 reads=[Rtgb], writes=[Rsgb])
            P.op("pool", lambda e: e.tensor_tensor(out=actb[:, p_, 0:T], in0=sgb[:, 0:T], in1=tub[:, 0:T], op=ALU.mult),
                 reads=[Rsgb, Rtub], writes=Ract[0:nblk])
            if p_ in (6, 11, 16) and deferred:
                deferred.pop(0)()
        for j in range(nblk):
            gb = (tk0 // 128) + j
            xx = xr[j]; Rxx = Rxr[j]
            for hf in range(2):
                ob = dn_banks[dn_ctr[0] % 2]
                dn_ctr[0] += 1
                for p_ in range(NFF):
                    P.op("pe", lambda e, ob=ob, p_=p_, hf=hf, j=j: e.matmul(psum[:, ob, :], lhsT=actb[:, p_, j * 128:(j + 1) * 128],
                                                                            rhs=w_dn[:, p_, hf * 512:(hf + 1) * 512],
                                                                            start=(p_ == 0), stop=(p_ == NFF - 1)),
                         reads=[Ract[j], Rw_dn], writes=[Rps[ob]], inc=(p_ == NFF - 1))
                P.op("dve", lambda e, ob=ob, hf=hf: e.scalar_tensor_tensor(out=xx[:, hf * 512:(hf + 1) * 512],
                                                                           in0=xx[:, hf * 512:(hf + 1) * 512], scalar=ALPHA,
                                                                           in1=psum[:, ob, :], op0=ALU.mult, op1=ALU.add),
                     reads=[Rxx, Rps[ob]], writes=[Rxx])

            def tail(xx=xx, Rxx=Rxx, j=j, gb=gb, fi=fi):
                ln_tail(P, nc, xx, Rxx, lnst[:, j, :], Rlnst[j], mhalf, Rmhalf, ln2g_bc, Rln2g, ln2b_bc, Rln2b, 0)
                P.dma("pool", "st_out", out_d[gb * 128:(gb + 1) * 128, :], xx, reads=[Rxx])
                issue_xr(fi + 1, j)
            deferred.append(tail)
    while deferred:
        deferred.pop(0)()

    P.wait_all("sp")
    P.wait_all("pool")
    P.emit()
    return nc


def ln_tail(P, nc, xx, Rxx, sc, Rsc, mhalf, Rmhalf, g_bc, Rg, b_bc, Rb, so, gb_eng="pool"):
    st = sc[:, so:so + 12]
    mv = sc[:, so + 12:so + 14]
    P.op("dve", lambda e: e.bn_stats(out=st[:, 0:6], in_=xx[:, 0:512]), reads=[Rxx], writes=[Rsc])
    P.op("dve", lambda e: e.bn_stats(out=st[:, 6:12], in_=xx[:, 512:1024]), reads=[Rxx], writes=[Rsc])
    P.op("dve", lambda e: e.bn_aggr(out=mv, in_=st), reads=[Rsc], writes=[Rsc])
    P.op("dve", lambda e: e.tensor_scalar(out=sc[:, so + 14:so + 15], in0=mv[:, 1:2], scalar1=EPS, scalar2=None, op0=ALU.add),
         reads=[Rsc], writes=[Rsc])
    P.op("pool", lambda e: e.tensor_tensor(out=sc[:, so + 15:so + 16], in0=sc[:, so + 14:so + 15], in1=mhalf[:, 0:1], op=ALU.pow),
         reads=[Rsc, Rmhalf], writes=[Rsc])
    P.op("dve", lambda e: e.scalar_tensor_tensor(out=sc[:, so + 14:so + 15], in0=mv[:, 0:1], scalar=-1.0, in1=sc[:, so + 15:so + 16],
                                                 op0=ALU.mult, op1=ALU.mult),
         reads=[Rsc], writes=[Rsc])
    P.op("act", lambda e: e.activation(out=xx, in_=xx, func=AF.Identity, scale=sc[:, so + 15:so + 16], bias=sc[:, so + 14:so + 15]),
         reads=[Rxx, Rsc], writes=[Rxx])
    P.op(gb_eng, lambda e: e.tensor_tensor(out=xx, in0=xx, in1=g_bc, op=ALU.mult), reads=[Rxx, Rg], writes=[Rxx])
    P.op(gb_eng, lambda e: e.tensor_tensor(out=xx, in0=xx, in1=b_bc, op=ALU.add), reads=[Rxx, Rb], writes=[Rxx])


def _bucket_idx():
    q = np.arange(128)[:, None]
    kc = np.arange(256)[None, :]
    dist = q + 128 - kc
    n = np.maximum(dist, 0)
    nf = np.maximum(n, 16).astype(np.float32)
    large = 16 + (np.log(nf / np.float32(16)) / np.float32(math.log(128 / 16)) * np.float32(16)).astype(np.int32)
    large = np.minimum(large, 31)
    bucket = np.where(n < 16, n, large)
    ok = (dist >= 0) & (dist < 128)
    return bucket, ok


def _cols(v, nchunk):
    return np.ascontiguousarray(np.asarray(v, np.float32).reshape(nchunk, 128).T)


_NC_CACHE = {}


def kernel(x, w_in, b_in, attn_sinks, rel_bias_table, conv_dw_w, conv_dw_b, conv_ln_g, conv_ln_b,
           attn_out_gain, conv_out_gain, w_out, b_out, ln1_g, ln1_b, w_up, ffn_dw_w, ffn_dw_b, w_down,
           ln2_g, ln2_b):
    f = lambda a: np.asarray(a, dtype=np.float32)
    x = f(x)
    w_in2, b_in1 = f(w_in)[0], f(b_in)[0]
    bq, bk, bvv, ba, bg = b_in1[0:512], b_in1[512:640], b_in1[640:768], b_in1[768:1280], b_in1[1280:1792]
    pcol = np.zeros((128, PC_N), np.float32)
    pcol[:, 0:4] = _cols(bq, 4)
    pcol[:, 4] = np.concatenate([bk[0:64], bk[0:64]])
    pcol[:, 5] = np.concatenate([bk[64:128], bk[64:128]])
    pcol[:, 6:10] = _cols(ba, 4)
    pcol[:, 10:14] = _cols(bg, 4)
    pcol[:, 14:18] = _cols(f(conv_dw_b)[0], 4)
    pcol[:, 18:22] = _cols(f(conv_ln_g)[0], 4)
    pcol[:, 22:26] = _cols(f(conv_ln_b)[0], 4)
    pcol[:, 26:30] = _cols(f(conv_out_gain)[0], 4)
    cw = f(conv_dw_w)[0]
    pcol[:, PC_CW:PC_CW + 124] = cw.reshape(31, 4, 128).transpose(2, 1, 0).reshape(128, 124)
    fw = f(ffn_dw_w)[0]
    pcol[:, PC_FW:PC_FW + 132] = fw.reshape(3, 44, 128).transpose(2, 1, 0).reshape(128, 132)
    pcol[:, PC_FB:PC_FB + 44] = _cols(f(ffn_dw_b)[0], 44)
    bucket, ok = _bucket_idx()
    tab = f(rel_bias_table)
    bias_full = tab[bucket]
    biasg = np.ascontiguousarray(bias_full.reshape(128, 2, 128, 8).transpose(2, 1, 3, 0)).reshape(128, 2 * 8 * 128)
    maskc = np.ascontiguousarray(ok.astype(np.float32).reshape(128, 2, 128).transpose(2, 1, 0)).reshape(128, 256)
    shared = {
        "w_in": w_in2, "w_out": f(w_out)[0], "w_up": f(w_up)[0], "w_down": f(w_down)[0], "pcol": pcol,
        "bv": bvv.reshape(1, 128), "again": f(attn_out_gain), "bout": f(b_out), "ln1g": f(ln1_g), "ln1b": f(ln1_b),
        "ln2g": f(ln2_g), "ln2b": f(ln2_b), "sinks": f(attn_sinks), "biasg": biasg, "maskc": maskc,
    }
    in_maps = []
    for core in range(8):
        b, c = core // 4, core % 4
        s0 = c * OWN
        xh = np.zeros((NTOK, D), np.float32)
        lo = s0 - 256
        if lo >= 0:
            xh[:] = x[b, lo:s0 + OWN]
        else:
            xh[256:] = x[b, 0:OWN]
        m = dict(shared)
        m["xh"] = xh
        m["xT"] = np.ascontiguousarray(xh.T)
        m["flag"] = np.full((128, 1), 1.0 if c > 0 else 0.0, np.float32)
        in_maps.append(m)
    if "nc" not in _NC_CACHE:
        _NC_CACHE["nc"] = build()
    res = run_bass_kernel_spmd(_NC_CACHE["nc"], in_maps, core_ids=list(range(8)))
    out = np.empty((2, 8192, D), np.float32)
    for core in range(8):
        b, c = core // 4, core % 4
        out[b, c * OWN:(c + 1) * OWN] = res.results[core]["out"]
    if DEBUG:
        kernel.dbg = [r for r in res.results]
    return out
```

```python
import math
import numpy as np
import concourse.bass as bass
import concourse.mybir as mybir
from concourse.bass_utils import run_bass_kernel_spmd

F32 = mybir.dt.float32
BF16 = mybir.dt.bfloat16
AF = mybir.ActivationFunctionType
ALU = mybir.AluOpType


class Res:
    __slots__ = ("name", "lw", "rd", "lws")

    def __init__(self, name):
        self.name = name
        self.lw = None
        self.lws = []
        self.rd = []


class Lane:
    def __init__(self, name, sem, unit):
        self.name, self.sem, self.unit, self.count = name, sem, unit, 0


class Eng:
    def __init__(self, name, lane):
        self.name, self.lane, self.ops, self.known = name, lane, [], {}


class _Rec:
    def __getattr__(self, name):
        return lambda *a, **k: (name, a, k)


class Prog:
    def __init__(self, nc):
        self.nc = nc
        self.eng = {}
        for n in ("pe", "act", "dve", "pool", "sp"):
            self.eng[n] = Eng(n, Lane(n, nc.alloc_semaphore("s_" + n), 1))
        self.dma_lanes = {}
        self.lane_rr = {}
        self.nres = 0

    def res(self, name=None):
        self.nres += 1
        return Res(name or f"r{self.nres}")

    LANES = {"ld_c": 4, "ld_w": 8, "ld_w2": 4, "ld_x": 8, "ld_xt": 3, "st_x1": 2, "ld_wu": 12, "ld_wd": 6,
             "ld_x1b": 3, "ld_xr": 3, "st_out": 2, "ld_c2": 2}

    def lane(self, name):
        k = self.LANES.get(name, 2)
        i = self.lane_rr.get(name, 0)
        self.lane_rr[name] = i + 1
        key = f"{name}{i % k}"
        if key not in self.dma_lanes:
            self.dma_lanes[key] = Lane(key, self.nc.alloc_semaphore("d_" + key), 16)
        return self.dma_lanes[key]

    def _deps(self, e, reads, writes, par=False):
        need = {}

        def add(t):
            if t is not None and need.get(t[0], 0) < t[1]:
                need[t[0]] = t[1]
        for r in reads:
            add(r.lw)
            for t in r.lws:
                add(t)
        for w in writes:
            for t in w.rd:
                add(t)
            if par and not w.rd:
                continue
            add(w.lw)
            for t in w.lws:
                add(t)
        waits = []
        for ln, idx in need.items():
            if ln is e.lane and e.name == "pe":
                continue
            if e.known.get(ln, 0) >= idx:
                continue
            e.known[ln] = idx
            waits.append((ln.sem, idx * ln.unit))
        return waits

    def _mark(self, t, reads, writes, par=False):
        for r in reads:
            r.rd.append(t)
        for w in writes:
            if par:
                if w.rd:
                    w.lw, w.lws, w.rd = None, [], []
                w.lws.append(t)
            else:
                w.lw, w.lws, w.rd = t, [], []

    def op(self, en, fn, reads=(), writes=(), inc=True):
        name, a, k = fn(_Rec())
        fn = (lambda eng, name=name, a=a, k=k: getattr(eng, name)(*a, **k))
        e = self.eng[en]
        waits = self._deps(e, reads, writes)
        ln = e.lane
        if inc:
            ln.count += 1
            idx = ln.count
        else:
            idx = ln.count + 1
        e.ops.append((waits, fn, ln if inc else None, 1))
        self._mark((ln, idx), reads, writes)

    def dma(self, qn, lane_name, out, in_, reads=(), writes=(), par=False):
        e = self.eng[qn]
        ln = self.lane(lane_name)
        waits = self._deps(e, reads, writes, par)
        if ln.count and e.known.get(ln, 0) < ln.count:
            e.known[ln] = ln.count
            waits.append((ln.sem, ln.count * ln.unit))
        ln.count += 1
        e.ops.append((waits, (lambda eng, o=out, i=in_: eng.dma_start(out=o, in_=i)), ln, 16))
        self._mark((ln, ln.count), reads, writes, par)

    def barrier(self):
        lanes = [e.lane for e in self.eng.values()] + list(self.dma_lanes.values())
        for e in self.eng.values():
            waits = []
            for ln in lanes:
                if ln is e.lane or ln.count == 0:
                    continue
                if e.known.get(ln, 0) < ln.count:
                    e.known[ln] = ln.count
                    waits.append((ln.sem, ln.count * ln.unit))
            e.ops.append((waits, None, None, 0))

    def wait_all(self, en):
        e = self.eng[en]
        waits = []
        for ln in self.dma_lanes.values():
            if ln.count and e.known.get(ln, 0) < ln.count:
                e.known[ln] = ln.count
                waits.append((ln.sem, ln.count * ln.unit))
        e.ops.append((waits, None, None, 0))

    def emit(self):
        with self.nc.Block() as block:
            def run(e, eng):
                for waits, fn, ln, amt in e.ops:
                    for sem, val in waits:
                        eng.wait_ge(sem, val)
                    if fn is None:
                        continue
                    ins = fn(eng)
                    if ln is not None:
                        ins.then_inc(ln.sem, amt)

            @block.tensor
            def _(eng):
                run(self.eng["pe"], eng)

            @block.scalar
            def _(eng):
                run(self.eng["act"], eng)

            @block.vector
            def _(eng):
                run(self.eng["dve"], eng)

            @block.gpsimd
            def _(eng):
                run(self.eng["pool"], eng)

            @block.sync
            def _(eng):
                run(self.eng["sp"], eng)


class Arena:
    def __init__(self, nc, words):
        self.t = nc.alloc_sbuf_tensor("arena", [128, words], F32)
        self.words = words
        self.off = 0

    def _take(self, n):
        o = self.off
        self.off += n
        assert self.off <= self.words, f"arena overflow {self.off * 4} > {self.words * 4}"
        return o

    @staticmethod
    def _shape(v, dims):
        if len(dims) == 1:
            return v
        names = " ".join(f"d{i}" for i in range(len(dims)))
        kw = {f"d{i}": d for i, d in enumerate(dims[:-1])}
        return v.rearrange(f"p ({names}) -> p {names}", **kw)

    def f32(self, *dims):
        n = int(np.prod(dims))
        o = self._take(n)
        return self._shape(self.t[:, o:o + n], dims)

    def bf16(self, *dims):
        n = int(np.prod(dims))
        n4 = (n + 1) // 2
        o = self._take(n4)
        v = self.t[:, o:o + n4].bitcast(BF16)[:, 0:n]
        return self._shape(v, dims)


D = 1024
NTOK = 2304
NBLK = 18
OWN = 2048
DFF = 2816
NFF = 22
ALPHA = float(2.0 ** 0.25)
EPS = 1e-5
W_IN_COLS = 14 * 128 + 128
PC_IN, PC_CONV, PC_CW, PC_FW, PC_FB = 0, 14, 30, 154, 286
PC_N = 330

DEBUG = False
KA, KB, KSUB = 99, 99, 9

def build():
    nc = bass.Bass("TRN2", target_bir_lowering=False)
    dt = lambda n, s, k="ExternalInput": nc.dram_tensor(n, s, F32, kind=k).ap()
    xh_d = dt("xh", [NTOK, D])
    xT_d = dt("xT", [D, NTOK])
    flag_d = dt("flag", [128, 1])
    w_in_d = dt("w_in", [D, 1792])
    w_out_d = dt("w_out", [D, D])
    w_up_d = dt("w_up", [D, 2 * DFF])
    w_down_d = dt("w_down", [DFF, D])
    pcol_d = dt("pcol", [128, PC_N])
    bv_d = dt("bv", [1, 128])
    again_d = dt("again", [1, 512])
    bout_d = dt("bout", [1, D])
    ln1g_d = dt("ln1g", [1, D])
    ln1b_d = dt("ln1b", [1, D])
    ln2g_d = dt("ln2g", [1, D])
    ln2b_d = dt("ln2b", [1, D])
    sinks_d = dt("sinks", [1, 8])
    biasg_d = dt("biasg", [128, 2 * 8 * 128])
    maskc_d = dt("maskc", [128, 2 * 128])
    out_d = dt("out", [OWN, D], "ExternalOutput")
    x1s_d = dt("x1s", [NTOK - 128, D], "ExternalOutput" if DEBUG else "Internal")

    P = Prog(nc)
    A = Arena(nc, 53000)
    psum = nc.alloc_psum_tensor("psum", [128, 8, 512], F32)
    R = P.res
    Rps = [R(f"ps{i}") for i in range(8)]

    pcol = A.f32(PC_N); Rpcol = R()
    flag = A.f32(1); Rflag = R()
    ident = A.bf16(128); Rident = R()
    ones = A.f32(128); Rones = R()
    mhalf = A.f32(512); Rmhalf = R()
    halfb = A.f32(8); Rhalfb = R()
    small = A.f32(64); Rsmall = R()
    identf = A.f32(128); Ridentf = R()
    rdt = A.f32(8); Rrdt = R()
    keep_mark = A.off

    P.dma("sp", "ld_c", pcol, pcol_d, writes=[Rpcol])
    P.dma("sp", "ld_c", flag, flag_d, writes=[Rflag])
    P.op("pool", lambda e: e.memset(ident, 0.0), writes=[Rident])
    P.op("pool", lambda e: e.affine_select(out=ident, in_=ident, compare_op=ALU.not_equal, fill=1.0,
                                          base=0, pattern=[[-1, 128]], channel_multiplier=1),
         reads=[Rident], writes=[Rident])
    P.op("pool", lambda e: e.memset(ones, 1.0), writes=[Rones])
    P.op("pool", lambda e: e.tensor_copy(out=identf, in_=ident), reads=[Rident], writes=[Ridentf])
    P.op("pool", lambda e: e.memset(mhalf, -0.5), writes=[Rmhalf])
    P.op("dve", lambda e: e.tensor_scalar(out=halfb[:, 0:4], in0=pcol[:, PC_IN + 6:PC_IN + 10], scalar1=0.5, scalar2=None, op0=ALU.mult),
         reads=[Rpcol], writes=[Rhalfb])
    P.op("dve", lambda e: e.tensor_scalar(out=halfb[:, 4:8], in0=pcol[:, PC_IN + 10:PC_IN + 14], scalar1=0.5, scalar2=None, op0=ALU.mult),
         reads=[Rpcol], writes=[Rhalfb])

    w_in = A.bf16(8, W_IN_COLS); Rw_in = R()
    diag = A.bf16(124, 128); Rdiag = [R() for _ in range(4)]
    xT = [A.bf16(8, 512) for _ in range(2)]; RxT = [R(), R()]
    hT = [A.bf16(4, 30 + 512) for _ in range(2)]; RhT = [R(), R()]
    cf = A.f32(4, 512); Rcf = [R() for _ in range(4)]
    sq = [A.f32(512) for _ in range(2)]; Rsq = [R(), R()]
    rstd = A.f32(512); Rrstd = R()
    early_free_end = A.off
    EARLY_RES = [Rw_in] + Rdiag + RxT + RhT + Rcf + Rsq + [Rrstd]
    w_out = A.bf16(8, D); Rw_out = R()
    kT = A.bf16(2, 2, NTOK); RkT = R()
    vaug = A.bf16(NBLK, 2, 65); Rvaug = [R() for _ in range(NBLK)]
    EB = A.f32(2, 8, 128); REB = R()
    esink = A.f32(8); Resink = R()
    bv_bc = A.f32(128); Rbv = R()
    again_bc = A.f32(512); Ragain = R()
    bout_bc = A.f32(D); Rbout = R()
    ln1g_bc = A.f32(D); Rln1g = R()
    ln1b_bc = A.f32(D); Rln1b = R()
    qT = A.bf16(4, 512); RqT = R()
    Tb = A.f32(512); RTb = R()
    maskc = Tb[:, 0:256].rearrange("p (a q) -> p a q", a=2); Rmask = RTb
    ab = A.f32(512); Rab = R()
    mean = A.f32(512); Rmean = R()
    tmpf = A.f32(512); Rtmpf = R()
    yT = A.bf16(8, 512); RyT = [R() for _ in range(4)]
    Esb = [Tb, ab]; REsb = [RTb, Rab]
    Pt = [A.bf16(2, 2, 512) for _ in range(2)]
    RPt = [[[R(), R()], [R(), R()]] for _ in range(2)]
    yat = mean; Ryat = Rmean
    junk = tmpf; Rjunk = Rtmpf
    ya = A.bf16(512); Rya = R()
    xt = [A.f32(D) for _ in range(3)]; Rxt = [R(), R(), R()]
    lnst1 = A.f32(3, 16); Rlnst1 = [R(), R(), R()]
    att = A.f32(2, 16); Ratt = [R(), R()]

    wq = "pool"
    for k in range(8):
        P.dma(wq, "ld_x", xT[0][:, k, 0:256], xT_d[k * 128:(k + 1) * 128, 0:256], writes=[RxT[0]], par=True)
    _save = A.off
    A.off = keep_mark
    w_up = A.bf16(NFF, 8, 256); Rw_up = [R() for _ in range(NFF)]
    A.off = _save
    N_EARLY = min(NFF, (early_free_end - keep_mark) // 1024)
    w_up_v = w_up_d.rearrange("(k p) n -> p k n", p=128)

    def load_wup(p_, extra):
        for gi, off in enumerate((0, DFF)):
            c1 = off + p_ * 128
            P.dma(wq, "ld_wu", w_up[:, p_, :, gi * 128:(gi + 1) * 128], w_up_v[:, :, c1:c1 + 128],
                  writes=[Rw_up[p_]] + (extra if gi == 0 else []), par=True)
    w_in_v = w_in_d.rearrange("(k p) n -> p k n", p=128)
    Rw_ag = R()
    segs = [(768, 768, 512, Rw_ag), (1280, 1280, 512, Rw_ag), (0, 0, 512, Rw_in), (512, 512, 64, Rw_in), (576, 512, 64, Rw_in),
            (640, 576, 64, Rw_in), (704, 576, 64, Rw_in), (1792, 640, 128, Rw_in)]
    for si, (dst, src, n, rr) in enumerate(segs):
        if si == 2:
            P.op("pool", lambda e: e.memset(hT[0][:, :, 0:30], 0.0), writes=[RhT[0]])
            P.op("pool", lambda e: e.memset(kT[64:128, :, 0, :], 0.0), writes=[RkT])
            P.op("pool", lambda e: e.memset(kT[0:64, :, 1, :], 0.0), writes=[RkT])
        for k in range(8):
            P.dma(wq, "ld_w", w_in[:, k, dst:dst + n], w_in_v[:, k, src:src + n], writes=[rr], par=True)
    for k in range(8):
        P.dma(wq, "ld_w2", w_out[:, k, :], w_out_d[k * 128:(k + 1) * 128, :], writes=[Rw_out], par=True)

    P.dma("sp", "ld_c", bv_bc, bv_d.partition_broadcast(128), writes=[Rbv])
    P.dma("sp", "ld_c", again_bc, again_d.partition_broadcast(128), writes=[Ragain])
    P.dma("sp", "ld_c", bout_bc, bout_d.partition_broadcast(128), writes=[Rbout])
    P.dma("sp", "ld_c", ln1g_bc, ln1g_d.partition_broadcast(128), writes=[Rln1g])
    P.dma("sp", "ld_c", ln1b_bc, ln1b_d.partition_broadcast(128), writes=[Rln1b])
    P.dma("sp", "ld_c", esink, sinks_d.partition_broadcast(128), writes=[Resink])
    P.dma("sp", "ld_c", EB, biasg_d.rearrange("p (a h q) -> p a h q", a=2, h=8), writes=[REB])
    P.dma("sp", "ld_c", maskc, maskc_d.rearrange("p (a q) -> p a q", a=2), writes=[Rmask])
    P.op("act", lambda e: e.activation(out=esink, in_=esink, func=AF.Exp), reads=[Resink], writes=[Resink])
    P.op("act", lambda e: e.activation(out=EB, in_=EB, func=AF.Exp), reads=[REB], writes=[REB])
    for kb in range(2):
        P.op("dve", lambda e, kb=kb: e.tensor_tensor(out=EB[:, kb], in0=EB[:, kb],
                                                    in1=maskc[:, kb].unsqueeze(1).broadcast_to([128, 8, 128]), op=ALU.mult),
             reads=[REB, Rmask], writes=[REB])
    def build_diag(c):
        for j in range(31):
            i = c * 31 + j
            if j % 2 == 0:
                P.op("act", lambda e, i=i: e.activation(out=diag[:, i, :], in_=ident, func=AF.Identity,
                                                        scale=pcol[:, PC_CW + i:PC_CW + i + 1]),
                     reads=[Rident, Rpcol], writes=[Rdiag[c]])
            else:
                P.op("dve", lambda e, i=i: e.tensor_scalar(out=diag[:, i, :], in0=ident, scalar1=pcol[:, PC_CW + i:PC_CW + i + 1],
                                                           scalar2=None, op0=ALU.mult),
                     reads=[Rident, Rpcol], writes=[Rdiag[c]])
    P.op("dve", lambda e: e.memset(vaug[:, :, :, 64:65], 1.0), writes=Rvaug)

    ps_fm = [0, 1]
    ps_sc = [2, 3]
    ps_pv = [4, 5]
    ps_op = [6, 7]
    fm_ctr = [0]

    def fm_bank():
        b = ps_fm[fm_ctr[0] % 2]
        fm_ctr[0] += 1
        return b

    def rstd_from_var(NM_, bank):
        nbk = NM_ // 128
        t3 = tmpf[:, 0:NM_].rearrange("p (j t) -> p j t", j=nbk)
        P.op("dve", lambda e: e.tensor_tensor(out=t3, in0=t3, in1=identf.unsqueeze(1).broadcast_to([128, nbk, 128]), op=ALU.mult),
             reads=[Rtmpf, Ridentf], writes=[Rtmpf])
        P.op("dve", lambda e: e.tensor_reduce(out=rdt[:, 0:nbk], in_=t3, axis=mybir.AxisListType.X, op=ALU.add),
             reads=[Rtmpf], writes=[Rrdt])
        P.op("pool", lambda e: e.tensor_tensor(out=rdt[:, 4:4 + nbk], in0=rdt[:, 0:nbk], in1=mhalf[:, 0:nbk], op=ALU.pow),
             reads=[Rrdt, Rmhalf], writes=[Rrdt])
        P.op("dve", lambda e: e.tensor_tensor(out=t3, in0=identf.unsqueeze(1).broadcast_to([128, nbk, 128]),
                                              in1=rdt[:, 4:4 + nbk].unsqueeze(2).broadcast_to([128, nbk, 128]), op=ALU.mult),
             reads=[Ridentf, Rrdt], writes=[Rtmpf])
        for j_ in range(nbk):
            P.op("pe", lambda e, j_=j_: e.matmul(psum[:, bank, j_ * 128:(j_ + 1) * 128], lhsT=ones, rhs=tmpf[:, j_ * 128:(j_ + 1) * 128],
                                                 start=True, stop=True),
                 reads=[Rones, Rtmpf], writes=[Rps[bank]], inc=(j_ == nbk - 1))
        P.op("act", lambda e: e.activation(out=rstd[:, 0:NM_], in_=psum[:, bank, 0:NM_], func=AF.Identity),
             reads=[Rps[bank]], writes=[Rrstd])

    tiles = ([(0, 2)] + [(2 + 4 * i, 4) for i in range(4)])[:KA]

    def load_xT(ti_):
        b0_, nb_ = tiles[ti_]
        for k in range(8):
            P.dma(wq, "ld_x", xT[ti_ % 2][:, k, 0:nb_ * 128], xT_d[k * 128:(k + 1) * 128, b0_ * 128:(b0_ + nb_) * 128],
                  writes=[RxT[ti_ % 2]], par=True)

    carry = [None]

    def issue_xt(bi_):
        if 1 <= bi_ < NBLK:
            P.dma("sp", "ld_xt", xt[bi_ % 3], xh_d[bi_ * 128:(bi_ + 1) * 128, :], writes=[Rxt[bi_ % 3]])
    for bi_ in (1, 2, 3):
        issue_xt(bi_)
    def dense_ag(ti_, chunks):
        N_ = tiles[ti_][1] * 128
        xb_ = xT[ti_ % 2]; Rxb_ = RxT[ti_ % 2]
        hb_ = hT[ti_ % 2]; Rhb_ = RhT[ti_ % 2]

        def pc(ch):
            b = fm_bank()
            for k in range(8):
                P.op("pe", lambda e, b=b, k=k, ch=ch: e.matmul(psum[:, b, 0:N_], lhsT=w_in[:, k, ch * 128:(ch + 1) * 128],
                                                               rhs=xb_[:, k, 0:N_], start=(k == 0), stop=(k == 7)),
                     reads=[Rw_ag, Rxb_], writes=[Rps[b]], inc=(k == 7))
            return b
        if 0 in chunks and ti_ > 0:
            pb = hT[(ti_ - 1) % 2]; Npv = tiles[ti_ - 1][1] * 128
            P.op("pool", lambda e: e.tensor_copy(out=hb_[:, :, 0:30], in_=pb[:, :, Npv:Npv + 30]),
                 reads=[RhT[(ti_ - 1) % 2]], writes=[Rhb_])
        for c in chunks:
            b = pc(6 + c)
            P.op("act", lambda e, b=b, c=c: e.activation(out=ab[:, 0:N_], in_=psum[:, b, 0:N_], func=AF.Identity,
                                                         scale=0.5, bias=halfb[:, c:c + 1]),
                 reads=[Rps[b], Rhalfb], writes=[Rab])
            b2 = pc(10 + c)
            P.op("act", lambda e, b2=b2, c=c: e.activation(out=Tb[:, 0:N_], in_=psum[:, b2, 0:N_], func=AF.Tanh,
                                                           scale=0.5, bias=halfb[:, 4 + c:5 + c]),
                 reads=[Rps[b2], Rhalfb], writes=[RTb])
            P.op("dve", lambda e, c=c: e.scalar_tensor_tensor(out=hb_[:, c, 30:30 + N_], in0=Tb[:, 0:N_], scalar=1.0, in1=ab[:, 0:N_],
                                                              op0=ALU.add, op1=ALU.mult),
                 reads=[RTb, Rab], writes=[Rhb_])
            if ti_ == 0:
                build_diag(c)
        if 3 in chunks and ti_ == 0:
            P.op("dve", lambda e: e.tensor_scalar(out=hb_[:, :, 30:30 + N_], in0=hb_[:, :, 30:30 + N_], scalar1=flag[:, 0:1],
                                                  scalar2=None, op0=ALU.mult),
                 reads=[Rhb_, Rflag], writes=[Rhb_])

    def dense_conv(ti_, chunks):
        N_ = tiles[ti_][1] * 128
        hb_ = hT[ti_ % 2]; Rhb_ = RhT[ti_ % 2]
        c0_ = 128 if ti_ == 0 else 0
        NM_ = N_ - c0_
        for c in chunks:
            b = fm_bank()
            for j in range(31):
                P.op("pe", lambda e, b=b, c=c, j=j: e.matmul(psum[:, b, 0:NM_], lhsT=diag[:, c * 31 + j, :],
                                                             rhs=hb_[:, c, c0_ + j:c0_ + j + NM_], start=(j == 0), stop=(j == 30)),
                     reads=[Rdiag[c], Rhb_], writes=[Rps[b]], inc=(j == 30))
            P.op("act", lambda e, b=b, c=c: e.activation(out=cf[:, c, 0:NM_], in_=psum[:, b, 0:NM_], func=AF.Identity,
                                                         bias=pcol[:, PC_CONV + c:PC_CONV + c + 1]),
                 reads=[Rps[b], Rpcol], writes=[Rcf[c]])

    for ti, (b0, nb) in enumerate(tiles):
        N = nb * 128
        t0 = b0 * 128
        xb = xT[ti % 2]; Rxb = RxT[ti % 2]
        hb = hT[ti % 2]; Rhb = RhT[ti % 2]
        if ti == 0:
            if len(tiles) > 1:
                load_xT(1)

        def proj_chunk(ch):
            b = fm_bank()
            for k in range(8):
                P.op("pe", lambda e, b=b, k=k, ch=ch: e.matmul(psum[:, b, 0:N], lhsT=w_in[:, k, ch * 128:(ch + 1) * 128],
                                                               rhs=xb[:, k, 0:N], start=(k == 0), stop=(k == 7)),
                     reads=[Rw_ag if ch >= 6 else Rw_in, Rxb], writes=[Rps[b]], inc=(k == 7))
            return b
        def emit_qk():
            for c in range(4):
                b = proj_chunk(c)
                P.op("act", lambda e, b=b, c=c: e.activation(out=qT[:, c, 0:N], in_=psum[:, b, 0:N], func=AF.Identity,
                                                             bias=pcol[:, PC_IN + c:PC_IN + c + 1]),
                     reads=[Rps[b], Rpcol], writes=[RqT])
            for kv in range(2):
                b = proj_chunk(4 + kv)
                for hh in range(2):
                    pr = slice(hh * 64, (hh + 1) * 64)
                    P.op("act", lambda e, b=b, kv=kv, hh=hh, pr=pr: e.activation(
                        out=kT[pr, kv, hh, t0:t0 + N], in_=psum[pr, b, 0:N], func=AF.Identity,
                        bias=pcol[pr, PC_IN + 4 + kv:PC_IN + 5 + kv]),
                        reads=[Rps[b], Rpcol], writes=[RkT])

        if ti == 0:
            dense_ag(0, [0, 1, 2, 3])
        def emit_v():
            for j in range(nb):
                bi = b0 + j
                b = fm_bank()
                for k in range(8):
                    P.op("pe", lambda e, b=b, k=k, j=j: e.matmul(psum[:, b, 0:128], lhsT=xb[:, k, j * 128:(j + 1) * 128],
                                                                 rhs=w_in[:, k, 1792:1920], start=(k == 0), stop=(k == 7)),
                         reads=[Rw_in, Rxb], writes=[Rps[b]], inc=(k == 7))
                P.op("dve", lambda e, b=b, bi=bi: e.tensor_tensor(out=vaug[:, bi, :, 0:64],
                                                                  in0=psum[:, b, 0:128].rearrange("p (a d) -> p a d", a=2),
                                                                  in1=bv_bc.rearrange("p (a d) -> p a d", a=2), op=ALU.add),
                     reads=[Rps[b], Rbv], writes=[Rvaug[bi]])
                if ti == 0:
                    P.op("dve", lambda e, bi=bi: e.tensor_scalar(out=vaug[:, bi], in0=vaug[:, bi], scalar1=flag[:, 0:1],
                                                                 scalar2=None, op0=ALU.mult),
                         reads=[Rvaug[bi], Rflag], writes=[Rvaug[bi]])

        if carry[0] is not None:
            carry[0]()
            carry[0] = None
        c0 = 128 if ti == 0 else 0
        NM = N - c0
        if ti == 0:
            dense_conv(0, [0, 1, 2, 3])
        emit_qk()
        emit_v()
        def stats_gen():
            bS1 = fm_bank()
            for c in range(4):
                P.op("pe", lambda e, c=c: e.matmul(psum[:, bS1, 0:NM], lhsT=ones, rhs=cf[:, c, 0:NM], start=(c == 0), stop=(c == 3)),
                     reads=[Rones, Rcf[c]], writes=[Rps[bS1]], inc=(c == 3))
            bS2 = fm_bank()
            for c in range(4):
                s = sq[c % 2]; Rs = Rsq[c % 2]
                P.op("act", lambda e, c=c, s=s: e.activation(out=s[:, 0:NM], in_=cf[:, c, 0:NM], func=AF.Square),
                     reads=[Rcf[c]], writes=[Rs])
                P.op("pe", lambda e, c=c, s=s: e.matmul(psum[:, bS2, 0:NM], lhsT=ones, rhs=s[:, 0:NM], start=(c == 0), stop=(c == 3)),
                     reads=[Rones, Rs], writes=[Rps[bS2]], inc=True)
            P.op("act", lambda e: e.activation(out=mean[:, 0:NM], in_=psum[:, bS1, 0:NM], func=AF.Identity, scale=1.0 / 512),
                 reads=[Rps[bS1]], writes=[Rmean])
            P.op("dve", lambda e: e.tensor_tensor(out=tmpf[:, 0:NM], in0=mean[:, 0:NM], in1=mean[:, 0:NM], op=ALU.mult),
                 reads=[Rmean], writes=[Rtmpf])
            P.op("dve", lambda e: e.scalar_tensor_tensor(out=tmpf[:, 0:NM], in0=psum[:, bS2, 0:NM], scalar=1.0 / 512, in1=tmpf[:, 0:NM],
                                                         op0=ALU.mult, op1=ALU.subtract),
                 reads=[Rps[bS2], Rtmpf], writes=[Rtmpf])
            P.op("dve", lambda e: e.tensor_scalar(out=tmpf[:, 0:NM], in0=tmpf[:, 0:NM], scalar1=EPS, scalar2=None, op0=ALU.add),
                 reads=[Rtmpf], writes=[Rtmpf])
            yield
            rstd_from_var(NM, bS1)
            bS3 = fm_bank()
            for c in range(4):
                P.op("dve", lambda e, c=c: e.tensor_tensor(out=cf[:, c, 0:NM], in0=cf[:, c, 0:NM], in1=mean[:, 0:NM], op=ALU.subtract),
                     reads=[Rcf[c], Rmean], writes=[Rcf[c]])
                P.op("dve", lambda e, c=c: e.tensor_tensor(out=cf[:, c, 0:NM], in0=cf[:, c, 0:NM], in1=rstd[:, 0:NM], op=ALU.mult),
                     reads=[Rcf[c], Rrstd], writes=[Rcf[c]])
                P.op("act", lambda e, c=c: e.activation(out=cf[:, c, 0:NM], in_=cf[:, c, 0:NM], func=AF.Identity,
                                                        scale=pcol[:, PC_CONV + 4 + c:PC_CONV + 5 + c],
                                                        bias=pcol[:, PC_CONV + 8 + c:PC_CONV + 9 + c]),
                     reads=[Rcf[c], Rpcol], writes=[Rcf[c]])
                P.op("act", lambda e, c=c: e.activation(out=Tb[:, 0:NM], in_=cf[:, c, 0:NM], func=AF.Tanh, scale=0.5),
                     reads=[Rcf[c]], writes=[RTb])
                P.op("dve", lambda e, c=c: e.scalar_tensor_tensor(out=cf[:, c, 0:NM], in0=Tb[:, 0:NM], scalar=1.0, in1=cf[:, c, 0:NM],
                                                                  op0=ALU.add, op1=ALU.mult),
                     reads=[RTb, Rcf[c]], writes=[Rcf[c]])
                s = sq[c % 2]; Rs = Rsq[c % 2]
                P.op("act", lambda e, c=c, s=s: e.activation(out=s[:, 0:NM], in_=cf[:, c, 0:NM], func=AF.Square),
                     reads=[Rcf[c]], writes=[Rs])
                P.op("pe", lambda e, c=c, s=s: e.matmul(psum[:, bS3, 0:NM], lhsT=ones, rhs=s[:, 0:NM], start=(c == 0), stop=(c == 3)),
                     reads=[Rones, Rs], writes=[Rps[bS3]], inc=True)
            yield
            P.op("dve", lambda e: e.tensor_scalar(out=tmpf[:, 0:NM], in0=psum[:, bS3, 0:NM], scalar1=1.0 / 512, scalar2=4 * EPS,
                                                  op0=ALU.mult, op1=ALU.add),
                 reads=[Rps[bS3]], writes=[Rtmpf])
            rstd_from_var(NM, bS2)
            RyT_all = RyT[0:nb]
            for c in range(4):
                P.op("dve", lambda e, c=c: e.scalar_tensor_tensor(out=yT[:, 4 + c, c0:c0 + NM], in0=cf[:, c, 0:NM],
                                                                  scalar=pcol[:, PC_CONV + 12 + c:PC_CONV + 13 + c],
                                                                  in1=rstd[:, 0:NM], op0=ALU.mult, op1=ALU.mult),
                     reads=[Rcf[c], Rpcol, Rrstd], writes=RyT_all)

            yield

        if ti + 2 < len(tiles):
            load_xT(ti + 2)
        pre_list = list(range(N_EARLY)) if (ti == len(tiles) - 1 and KB > 0) else []
        def S0(j):
            bi = b0 + j
            qc0 = j * 128
            Pb = Pt[bi % 2]; RPb = RPt[bi % 2]
            for kvh in range(2):
                for kb in range(2):
                    kbi = bi - 1 + kb
                    sb_ = ps_sc[(kvh * 2 + kb) % 2]
                    for g in range(4):
                        h = kvh * 4 + g
                        half = h % 2
                        P.op("pe", lambda e, sb_=sb_, g=g, kvh=kvh, kbi=kbi, half=half, h=h: e.matmul(
                            psum[:, sb_, g * 128:(g + 1) * 128],
                            lhsT=kT[:, kvh, half, kbi * 128:(kbi + 1) * 128],
                            rhs=qT[:, h // 2, qc0:qc0 + 128], start=True, stop=True),
                            reads=[RkT, RqT], writes=[Rps[sb_]], inc=(g == 3))
                    es = Esb[(kvh * 2 + kb) % 2]; Res_ = REsb[(kvh * 2 + kb) % 2]
                    P.op("act", lambda e, sb_=sb_, es=es: e.activation(out=es, in_=psum[:, sb_, :], func=AF.Exp, scale=0.125),
                         reads=[Rps[sb_]], writes=[Res_])
                    P.op("dve", lambda e, es=es, kvh=kvh, kb=kb: e.tensor_tensor(
                        out=Pb[:, kvh, kb, :].rearrange("p (g q) -> p g q", g=4),
                        in0=es.rearrange("p (g q) -> p g q", g=4),
                        in1=EB[:, kb, kvh * 4:(kvh + 1) * 4, :], op=ALU.mult),
                        reads=[Res_, REB], writes=[RPb[kvh][kb]])

        def S1(j, part=None):
            bi = b0 + j
            qc0 = j * 128
            if part in (None, 0):
                S1a(j)
            if part in (None, 1):
                S1b(j)

        def S1a(j):
            bi = b0 + j
            qc0 = j * 128
            Pb = Pt[bi % 2]; RPb = RPt[bi % 2]
            sm = att[:, bi % 2, :]; Rsm = Ratt[bi % 2]
            for h in range(8):
                kvh, g = h // 4, h % 4
                pb_ = ps_pv[h // 4]
                for kb in range(2):
                    kbi = bi - 1 + kb
                    P.op("pe", lambda e, pb_=pb_, g=g, kvh=kvh, kb=kb, kbi=kbi: e.matmul(
                        psum[:, pb_, g * 65:(g + 1) * 65], lhsT=Pb[:, kvh, kb, g * 128:(g + 1) * 128],
                        rhs=vaug[:, kbi, kvh, :], start=(kb == 0), stop=(kb == 1)),
                        reads=[RPb[kvh][kb], Rvaug[kbi]], writes=[Rps[pb_]], inc=(kb == 1 and g == 3))
            pvv = psum[:, 4:6, 0:260].rearrange("p a (g e) -> p a g e", e=65)
            den = sm[:, 0:8]
            P.op("dve", lambda e: e.tensor_tensor(out=den.rearrange("p (a g) -> p a g", a=2), in0=pvv[:, :, :, 64],
                                                  in1=esink.rearrange("p (a g) -> p a g", a=2), op=ALU.add),
                 reads=[Rps[4], Rps[5], Resink], writes=[Rsm])
            P.op("dve", lambda e: e.reciprocal(out=den, in_=den), reads=[Rsm], writes=[Rsm])
            P.op("dve", lambda e: e.tensor_tensor(out=yat.rearrange("p (a g d) -> p a g d", a=2, g=4), in0=pvv[:, :, :, 0:64],
                                                  in1=den.rearrange("p (a g) -> p a g", a=2).unsqueeze(3).broadcast_to([128, 2, 4, 64]),
                                                  op=ALU.mult),
                 reads=[Rps[4], Rps[5], Rsm], writes=[Ryat])
            ss = sm[:, 8:9]
            P.op("act", lambda e: e.activation(out=junk, in_=yat, func=AF.Square, accum_out=ss),
                 reads=[Ryat], writes=[Rjunk, Rsm])
            P.op("dve", lambda e: e.tensor_scalar(out=sm[:, 9:10], in0=ss, scalar1=1.0 / 512, scalar2=EPS, op0=ALU.mult, op1=ALU.add),
                 reads=[Rsm], writes=[Rsm])
            P.op("pool", lambda e: e.tensor_tensor(out=sm[:, 10:11], in0=sm[:, 9:10], in1=mhalf[:, 0:1], op=ALU.pow),
                 reads=[Rsm, Rmhalf], writes=[Rsm])
            P.op("dve", lambda e: e.scalar_tensor_tensor(out=ya, in0=yat, scalar=sm[:, 10:11], in1=again_bc, op0=ALU.mult, op1=ALU.mult),
                 reads=[Ryat, Rsm, Ragain], writes=[Rya])

        def S1b(j):
            bi = b0 + j
            qc0 = j * 128
            for c in range(4):
                pb_ = ps_pv[c // 2]
                tv = psum[:, pb_, 384:512].bitcast(BF16)
                P.op("pe", lambda e, tv=tv, c=c: e.transpose(tv[:, (c % 2) * 128:(c % 2 + 1) * 128], ya[:, c * 128:(c + 1) * 128], ident),
                     reads=[Rya, Rident], writes=[Rps[pb_]], inc=True)
            for a in range(2):
                tv = psum[:, ps_pv[a], 384:512].bitcast(BF16)
                P.op("act", lambda e, tv=tv, a=a: e.activation(out=yT[:, 2 * a:2 * a + 2, qc0:qc0 + 128],
                                                               in_=tv.rearrange("p (c t) -> p c t", c=2), func=AF.Identity),
                     reads=[Rps[ps_pv[a]]], writes=[RyT[j]])

        def S2a(j):
            bi = b0 + j
            qc0 = j * 128
            xx = xt[bi % 3]; Rxx = Rxt[bi % 3]
            P.op("dve", lambda e: e.scalar_tensor_tensor(out=xx, in0=xx, scalar=ALPHA, in1=bout_bc, op0=ALU.mult, op1=ALU.add),
                 reads=[Rxx, Rbout], writes=[Rxx])
            for hf in range(2):
                ob = ps_op[hf]
                for kc in range(8):
                    P.op("pe", lambda e, ob=ob, kc=kc, hf=hf: e.matmul(psum[:, ob, :], lhsT=yT[:, kc, qc0:qc0 + 128],
                                                                       rhs=w_out[:, kc, hf * 512:(hf + 1) * 512],
                                                                       start=(kc == 0), stop=(kc == 7)),
                         reads=[RyT[j], Rw_out], writes=[Rps[ob]], inc=(kc == 7))
                P.op("dve", lambda e, ob=ob, hf=hf: e.tensor_tensor(out=xx[:, hf * 512:(hf + 1) * 512], in0=xx[:, hf * 512:(hf + 1) * 512],
                                                                    in1=psum[:, ob, :], op=ALU.add),
                     reads=[Rxx, Rps[ob]], writes=[Rxx])

            def tail():
                ln_tail(P, nc, xx, Rxx, lnst1[:, bi % 3, :], Rlnst1[bi % 3], mhalf, Rmhalf, ln1g_bc, Rln1g, ln1b_bc, Rln1b, 0, gb_eng="dve")
                P.dma("pool", "st_x1", x1s_d[(bi - 1) * 128:bi * 128, :], xx, reads=[Rxx])
                issue_xt(bi + 3)
            return tail

        blocks = [j for j in range(nb) if b0 + j != 0]
        pending = None
        sg_ = stats_gen()
        next(sg_)
        S0(blocks[0])
        next(sg_)
        if len(blocks) > 1:
            S0(blocks[1])
        S1(blocks[0])
        for _ in sg_:
            pass
        for idx, j in enumerate(blocks):
            if idx + 2 < len(blocks):
                S0(blocks[idx + 2])
            if idx + 1 < len(blocks):
                S1(blocks[idx + 1], 0)
            tl = S2a(j)
            if pending is not None:
                pending()
            pending = tl
            if ti + 1 < len(tiles):
                nxt = [("ag", [0, 1]), ("ag", [2, 3]), ("conv", [0, 1]), ("conv", [2, 3])]
                per = (len(nxt) + len(blocks) - 1) // len(blocks)
                for kind_, chs_ in nxt[idx * per:(idx + 1) * per]:
                    (dense_ag if kind_ == "ag" else dense_conv)(ti + 1, chs_)
            if idx + 1 < len(blocks):
                S1(blocks[idx + 1], 1)
            npre = (len(pre_list) + (len(blocks) - idx) - 1) // (len(blocks) - idx)
            for p_ in pre_list[:npre]:
                load_wup(p_, EARLY_RES + [Rw_ag])
            pre_list = pre_list[npre:]
        if ti == len(tiles) - 1:
            if pending is not None:
                pending()
        else:
            carry[0] = pending

    P.barrier()
    A.off = keep_mark
    A.bf16(NFF, 8, 256)
    w_dn = A.bf16(NFF, D); Rw_dn = R()
    ln2g_bc = A.f32(D); Rln2g = R()
    ln2b_bc = A.f32(D); Rln2b = R()
    x1b = A.bf16(3, D); Rx1b = R()
    x1T = [A.bf16(8, 386) for _ in range(2)]; Rx1T = [R(), R()]
    tg = [A.f32(386) for _ in range(3)]; Rtg = [R(), R(), R()]
    tu = [A.f32(386) for _ in range(3)]; Rtu = [R(), R(), R()]
    sg = [A.f32(384) for _ in range(3)]; Rsg = [R(), R(), R()]
    actb = A.bf16(NFF, 384); Ract = [R() for _ in range(3)]
    xr = [A.f32(D) for _ in range(3)]; Rxr = [R(), R(), R()]
    lnst = A.f32(3, 16); Rlnst = [R(), R(), R()]

    LATE_WUP = list(range(N_EARLY, NFF))
    P.dma("sp", "ld_c2", ln2g_bc, ln2g_d.partition_broadcast(128), writes=[Rln2g])
    P.dma("sp", "ld_c2", ln2b_bc, ln2b_d.partition_broadcast(128), writes=[Rln2b])

    ps_tr = 6
    ps_up = [[0, 1], [2, 3], [4, 5]]
    ps_dn = [5, 6]

    def load_transpose(row0, nblk, dstT, RdstT, col0):
        for j in range(nblk):
            P.dma(wq, "ld_x1b", x1b[:, j, :], x1s_d[row0 + j * 128:row0 + (j + 1) * 128, :], writes=[Rx1b], par=True)
        for j in range(nblk):
            tv = psum[:, ps_tr, :].bitcast(BF16)
            for k in range(8):
                P.op("pe", lambda e, j=j, k=k, tv=tv: e.transpose(tv[:, k * 128:(k + 1) * 128], x1b[:, j, k * 128:(k + 1) * 128], ident),
                     reads=[Rx1b, Rident], writes=[Rps[ps_tr]], inc=(k == 7))
            P.op("act", lambda e, j=j, tv=tv: e.activation(out=dstT[:, :, col0 + j * 128:col0 + (j + 1) * 128],
                                                           in_=tv.rearrange("p (k t) -> p k t", k=8), func=AF.Identity),
                 reads=[Rps[ps_tr]], writes=[RdstT])

    ftiles = [(384 * i, 384) for i in range(5)] + [(1920, 128)]
    ftiles = ftiles[:KB]
    deferred = []
    dn_banks = [6, 7]
    dn_ctr = [0]

    def prep_tile(fi):
        tk0, T = ftiles[fi]
        cur = x1T[fi % 2]; Rcur = Rx1T[fi % 2]
        if fi == 0:
            load_transpose(0, 1, cur, Rcur, 2)
            P.op("dve", lambda e: e.tensor_scalar(out=cur[:, :, 0:2], in0=cur[:, :, 128:130], scalar1=flag[:, 0:1],
                                                  scalar2=None, op0=ALU.mult),
                 reads=[Rcur, Rflag], writes=[Rcur])
        else:
            pT = x1T[(fi - 1) % 2]; Tp = ftiles[fi - 1][1]
            P.op("dve", lambda e: e.tensor_copy(out=cur[:, :, 0:2], in_=pT[:, :, Tp:Tp + 2]),
                 reads=[Rx1T[(fi - 1) % 2]], writes=[Rcur])
        load_transpose(128 + tk0, T // 128, cur, Rcur, 2)

    def issue_xr(fi_, j_):
        if fi_ < len(ftiles) and j_ < ftiles[fi_][1] // 128:
            gb_ = ftiles[fi_][0] // 128 + j_
            P.dma("sp", "ld_xr", xr[j_], x1s_d[128 + gb_ * 128:256 + gb_ * 128, :], writes=[Rxr[j_]])

    if ftiles:
        prep_tile(0)
        for j_ in range(3):
            issue_xr(0, j_)
    for p_ in LATE_WUP:
        load_wup(p_, [])
    for p_ in range(NFF):
        P.dma(wq, "ld_wd", w_dn[:, p_, :], w_down_d[p_ * 128:(p_ + 1) * 128, :], writes=[Rw_dn], par=True)
    for fi, (tk0, T) in enumerate(ftiles):
        nblk = T // 128
        cur = x1T[fi % 2]; Rcur = Rx1T[fi % 2]
        NC = T + 2
        for p_ in range(NFF):
            if p_ == 3 and fi + 1 < len(ftiles):
                prep_tile(fi + 1)
            bg, bu = ps_up[p_ % 3]
            for gi, bb in ((0, bg), (1, bu)):
                for k in range(8):
                    P.op("pe", lambda e, bb=bb, k=k, gi=gi: e.matmul(psum[:, bb, 0:NC], lhsT=w_up[:, p_, k, gi * 128:(gi + 1) * 128],
                                                                     rhs=cur[:, k, 0:NC], start=(k == 0), stop=(k == 7)),
                         reads=[Rw_up[p_], Rcur], writes=[Rps[bb]], inc=(k == 7))
            tgb, Rtgb = tg[p_ % 3], Rtg[p_ % 3]
            tub, Rtub = tu[p_ % 3], Rtu[p_ % 3]
            sgb, Rsgb = sg[p_ % 3], Rsg[p_ % 3]
            for (tb, Rtb, bb, ch) in ((tgb, Rtgb, bg, p_), (tub, Rtub, bu, NFF + p_)):
                wc = PC_FW + ch * 3
                P.op("act", lambda e, tb=tb, bb=bb, wc=wc, ch=ch: e.activation(
                    out=tb[:, 0:T], in_=psum[:, bb, 2:NC], func=AF.Identity,
                    scale=pcol[:, wc + 2:wc + 3], bias=pcol[:, PC_FB + ch:PC_FB + ch + 1]),
                    reads=[Rps[bb], Rpcol], writes=[Rtb])
                P.op("dve", lambda e, tb=tb, bb=bb, wc=wc: e.scalar_tensor_tensor(
                    out=tb[:, 0:T], in0=psum[:, bb, 1:NC - 1], scalar=pcol[:, wc + 1:wc + 2], in1=tb[:, 0:T],
                    op0=ALU.mult, op1=ALU.add), reads=[Rps[bb], Rpcol, Rtb], writes=[Rtb])
                P.op("dve", lambda e, tb=tb, bb=bb, wc=wc: e.scalar_tensor_tensor(
                    out=tb[:, 0:T], in0=psum[:, bb, 0:NC - 2], scalar=pcol[:, wc:wc + 1], in1=tb[:, 0:T],
                    op0=ALU.mult, op1=ALU.add), reads=[Rps[bb], Rpcol, Rtb], writes=[Rtb])
            P.op("act", lambda e: e.activation(out=sgb[:, 0:T], in_=tgb[:, 0:T], func=AF.Silu),
                 reads=[Rtgb], writes=[Rsgb])
            P.op("pool", lambda e: e.tensor_tensor(out=actb[:, p_, 0:T], in0=sgb[:, 0:T], in1=tub[:, 0:T], op=ALU.mult),
                 reads=[Rsgb, Rtub], writes=Ract[0:nblk])
            if p_ in (6, 11, 16) and deferred:
                deferred.pop(0)()
        for j in range(nblk):
            gb = (tk0 // 128) + j
            xx = xr[j]; Rxx = Rxr[j]
            for hf in range(2):
                ob = dn_banks[dn_ctr[0] % 2]
                dn_ctr[0] += 1
                for p_ in range(NFF):
                    P.op("pe", lambda e, ob=ob, p_=p_, hf=hf, j=j: e.matmul(psum[:, ob, :], lhsT=actb[:, p_, j * 128:(j + 1) * 128],
                                                                            rhs=w_dn[:, p_, hf * 512:(hf + 1) * 512],
                                                                            start=(p_ == 0), stop=(p_ == NFF - 1)),
                         reads=[Ract[j], Rw_dn], writes=[Rps[ob]], inc=(p_ == NFF - 1))
                P.op("dve", lambda e, ob=ob, hf=hf: e.scalar_tensor_tensor(out=xx[:, hf * 512:(hf + 1) * 512],
                                                                           in0=xx[:, hf * 512:(hf + 1) * 512], scalar=ALPHA,
                                                                           in1=psum[:, ob, :], op0=ALU.mult, op1=ALU.add),
                     reads=[Rxx, Rps[ob]], writes=[Rxx])

            def tail(xx=xx, Rxx=Rxx, j=j, gb=gb, fi=fi):
                ln_tail(P, nc, xx, Rxx, lnst[:, j, :], Rlnst[j], mhalf, Rmhalf, ln2g_bc, Rln2g, ln2b_bc, Rln2b, 0)
                P.dma("pool", "st_out", out_d[gb * 128:(gb + 1) * 128, :], xx, reads=[Rxx])
                issue_xr(fi + 1, j)
            deferred.append(tail)
    while deferred:
        deferred.pop(0)()

    P.wait_all("sp")
    P.wait_all("pool")
    P.emit()
    return nc


def ln_tail(P, nc, xx, Rxx, sc, Rsc, mhalf, Rmhalf, g_bc, Rg, b_bc, Rb, so, gb_eng="pool"):
    st = sc[:, so:so + 12]
    mv = sc[:, so + 12:so + 14]
    P.op("dve", lambda e: e.bn_stats(out=st[:, 0:6], in_=xx[:, 0:512]), reads=[Rxx], writes=[Rsc])
    P.op("dve", lambda e: e.bn_stats(out=st[:, 6:12], in_=xx[:, 512:1024]), reads=[Rxx], writes=[Rsc])
    P.op("dve", lambda e: e.bn_aggr(out=mv, in_=st), reads=[Rsc], writes=[Rsc])
    P.op("dve", lambda e: e.tensor_scalar(out=sc[:, so + 14:so + 15], in0=mv[:, 1:2], scalar1=EPS, scalar2=None, op0=ALU.add),
         reads=[Rsc], writes=[Rsc])
    P.op("pool", lambda e: e.tensor_tensor(out=sc[:, so + 15:so + 16], in0=sc[:, so + 14:so + 15], in1=mhalf[:, 0:1], op=ALU.pow),
         reads=[Rsc, Rmhalf], writes=[Rsc])
    P.op("dve", lambda e: e.scalar_tensor_tensor(out=sc[:, so + 14:so + 15], in0=mv[:, 0:1], scalar=-1.0, in1=sc[:, so + 15:so + 16],
                                                 op0=ALU.mult, op1=ALU.mult),
         reads=[Rsc], writes=[Rsc])
    P.op("act", lambda e: e.activation(out=xx, in_=xx, func=AF.Identity, scale=sc[:, so + 15:so + 16], bias=sc[:, so + 14:so + 15]),
         reads=[Rxx, Rsc], writes=[Rxx])
    P.op(gb_eng, lambda e: e.tensor_tensor(out=xx, in0=xx, in1=g_bc, op=ALU.mult), reads=[Rxx, Rg], writes=[Rxx])
    P.op(gb_eng, lambda e: e.tensor_tensor(out=xx, in0=xx, in1=b_bc, op=ALU.add), reads=[Rxx, Rb], writes=[Rxx])


def _bucket_idx():
    q = np.arange(128)[:, None]
    kc = np.arange(256)[None, :]
    dist = q + 128 - kc
    n = np.maximum(dist, 0)
    nf = np.maximum(n, 16).astype(np.float32)
    large = 16 + (np.log(nf / np.float32(16)) / np.float32(math.log(128 / 16)) * np.float32(16)).astype(np.int32)
    large = np.minimum(large, 31)
    bucket = np.where(n < 16, n, large)
    ok = (dist >= 0) & (dist < 128)
    return bucket, ok


def _cols(v, nchunk):
    return np.ascontiguousarray(np.asarray(v, np.float32).reshape(nchunk, 128).T)


_NC_CACHE = {}


def kernel(x, w_in, b_in, attn_sinks, rel_bias_table, conv_dw_w, conv_dw_b, conv_ln_g, conv_ln_b,
           attn_out_gain, conv_out_gain, w_out, b_out, ln1_g, ln1_b, w_up, ffn_dw_w, ffn_dw_b, w_down,
           ln2_g, ln2_b):
    f = lambda a: np.asarray(a, dtype=np.float32)
    x = f(x)
    w_in2, b_in1 = f(w_in)[0], f(b_in)[0]
    bq, bk, bvv, ba, bg = b_in1[0:512], b_in1[512:640], b_in1[640:768], b_in1[768:1280], b_in1[1280:1792]
    pcol = np.zeros((128, PC_N), np.float32)
    pcol[:, 0:4] = _cols(bq, 4)
    pcol[:, 4] = np.concatenate([bk[0:64], bk[0:64]])
    pcol[:, 5] = np.concatenate([bk[64:128], bk[64:128]])
    pcol[:, 6:10] = _cols(ba, 4)
    pcol[:, 10:14] = _cols(bg, 4)
    pcol[:, 14:18] = _cols(f(conv_dw_b)[0], 4)
    pcol[:, 18:22] = _cols(f(conv_ln_g)[0], 4)
    pcol[:, 22:26] = _cols(f(conv_ln_b)[0], 4)
    pcol[:, 26:30] = _cols(f(conv_out_gain)[0], 4)
    cw = f(conv_dw_w)[0]
    pcol[:, PC_CW:PC_CW + 124] = cw.reshape(31, 4, 128).transpose(2, 1, 0).reshape(128, 124)
    fw = f(ffn_dw_w)[0]
    pcol[:, PC_FW:PC_FW + 132] = fw.reshape(3, 44, 128).transpose(2, 1, 0).reshape(128, 132)
    pcol[:, PC_FB:PC_FB + 44] = _cols(f(ffn_dw_b)[0], 44)
    bucket, ok = _bucket_idx()
    tab = f(rel_bias_table)
    bias_full = tab[bucket]
    biasg = np.ascontiguousarray(bias_full.reshape(128, 2, 128, 8).transpose(2, 1, 3, 0)).reshape(128, 2 * 8 * 128)
    maskc = np.ascontiguousarray(ok.astype(np.float32).reshape(128, 2, 128).transpose(2, 1, 0)).reshape(128, 256)
    shared = {
        "w_in": w_in2, "w_out": f(w_out)[0], "w_up": f(w_up)[0], "w_down": f(w_down)[0], "pcol": pcol,
        "bv": bvv.reshape(1, 128), "again": f(attn_out_gain), "bout": f(b_out), "ln1g": f(ln1_g), "ln1b": f(ln1_b),
        "ln2g": f(ln2_g), "ln2b": f(ln2_b), "sinks": f(attn_sinks), "biasg": biasg, "maskc": maskc,
    }
    in_maps = []
    for core in range(8):
        b, c = core // 4, core % 4
        s0 = c * OWN
        xh = np.zeros((NTOK, D), np.float32)
        lo = s0 - 256
        if lo >= 0:
            xh[:] = x[b, lo:s0 + OWN]
        else:
            xh[256:] = x[b, 0:OWN]
        m = dict(shared)
        m["xh"] = xh
        m["xT"] = np.ascontiguousarray(xh.T)
        m["flag"] = np.full((128, 1), 1.0 if c > 0 else 0.0, np.float32)
        in_maps.append(m)
    if "nc" not in _NC_CACHE:
        _NC_CACHE["nc"] = build()
    res = run_bass_kernel_spmd(_NC_CACHE["nc"], in_maps, core_ids=list(range(8)))
    out = np.empty((2, 8192, D), np.float32)
    for core in range(8):
        b, c = core // 4, core % 4
        out[b, c * OWN:(c + 1) * OWN] = res.results[core]["out"]
    if DEBUG:
        kernel.dbg = [r for r in res.results]
    return out
```

```python
import math
import numpy as np
import concourse.bass as bass
import concourse.mybir as mybir
from concourse.bass_utils import run_bass_kernel_spmd

F32 = mybir.dt.float32
BF16 = mybir.dt.bfloat16
AF = mybir.ActivationFunctionType
ALU = mybir.AluOpType


class Res:
    __slots__ = ("name", "lw", "rd", "lws")

    def __init__(self, name):
        self.name = name
        self.lw = None
        self.lws = []
        self.rd = []


class Lane:
    def __init__(self, name, sem, unit):
        self.name, self.sem, self.unit, self.count = name, sem, unit, 0


class Eng:
    def __init__(self, name, lane):
        self.name, self.lane, self.ops, self.known = name, lane, [], {}


class _Rec:
    def __getattr__(self, name):
        return lambda *a, **k: (name, a, k)


class Prog:
    def __init__(self, nc):
        self.nc = nc
        self.eng = {}
        for n in ("pe", "act", "dve", "pool", "sp"):
            self.eng[n] = Eng(n, Lane(n, nc.alloc_semaphore("s_" + n), 1))
        self.dma_lanes = {}
        self.lane_rr = {}
        self.nres = 0

    def res(self, name=None):
        self.nres += 1
        return Res(name or f"r{self.nres}")

    LANES = {"ld_c": 4, "ld_w": 8, "ld_w2": 4, "ld_x": 8, "ld_xt": 3, "st_x1": 2, "ld_wu": 12, "ld_wd": 6,
             "ld_x1b": 3, "ld_xr": 3, "st_out": 2, "ld_c2": 2}

    def lane(self, name):
        k = self.LANES.get(name, 2)
        i = self.lane_rr.get(name, 0)
        self.lane_rr[name] = i + 1
        key = f"{name}{i % k}"
        if key not in self.dma_lanes:
            self.dma_lanes[key] = Lane(key, self.nc.alloc_semaphore("d_" + key), 16)
        return self.dma_lanes[key]

    def _deps(self, e, reads, writes, par=False):
        need = {}

        def add(t):
            if t is not None and need.get(t[0], 0) < t[1]:
                need[t[0]] = t[1]
        for r in reads:
            add(r.lw)
            for t in r.lws:
                add(t)
        for w in writes:
            for t in w.rd:
                add(t)
            if par and not w.rd:
                continue
            add(w.lw)
            for t in w.lws:
                add(t)
        waits = []
        for ln, idx in need.items():
            if ln is e.lane and e.name == "pe":
                continue
            if e.known.get(ln, 0) >= idx:
                continue
            e.known[ln] = idx
            waits.append((ln.sem, idx * ln.unit))
        return waits

    def _mark(self, t, reads, writes, par=False):
        for r in reads:
            r.rd.append(t)
        for w in writes:
            if par:
                if w.rd:
                    w.lw, w.lws, w.rd = None, [], []
                w.lws.append(t)
            else:
                w.lw, w.lws, w.rd = t, [], []

    def op(self, en, fn, reads=(), writes=(), inc=True):
        name, a, k = fn(_Rec())
        fn = (lambda eng, name=name, a=a, k=k: getattr(eng, name)(*a, **k))
        e = self.eng[en]
        waits = self._deps(e, reads, writes)
        ln = e.lane
        if inc:
            ln.count += 1
            idx = ln.count
        else:
            idx = ln.count + 1
        e.ops.append((waits, fn, ln if inc else None, 1))
        self._mark((ln, idx), reads, writes)

    def dma(self, qn, lane_name, out, in_, reads=(), writes=(), par=False):
        e = self.eng[qn]
        ln = self.lane(lane_name)
        waits = self._deps(e, reads, writes, par)
        if ln.count and e.known.get(ln, 0) < ln.count:
            e.known[ln] = ln.count
            waits.append((ln.sem, ln.count * ln.unit))
        ln.count += 1
        e.ops.append((waits, (lambda eng, o=out, i=in_: eng.dma_start(out=o, in_=i)), ln, 16))
        self._mark((ln, ln.count), reads, writes, par)

    def barrier(self):
        lanes = [e.lane for e in self.eng.values()] + list(self.dma_lanes.values())
        for e in self.eng.values():
            waits = []
            for ln in lanes:
                if ln is e.lane or ln.count == 0:
                    continue
                if e.known.get(ln, 0) < ln.count:
                    e.known[ln] = ln.count
                    waits.append((ln.sem, ln.count * ln.unit))
            e.ops.append((waits, None, None, 0))

    def wait_all(self, en):
        e = self.eng[en]
        waits = []
        for ln in self.dma_lanes.values():
            if ln.count and e.known.get(ln, 0) < ln.count:
                e.known[ln] = ln.count
                waits.append((ln.sem, ln.count * ln.unit))
        e.ops.append((waits, None, None, 0))

    def emit(self):
        with self.nc.Block() as block:
            def run(e, eng):
                for waits, fn, ln, amt in e.ops:
                    for sem, val in waits:
                        eng.wait_ge(sem, val)
                    if fn is None:
                        continue
                    ins = fn(eng)
                    if ln is not None:
                        ins.then_inc(ln.sem, amt)

            @block.tensor
            def _(eng):
                run(self.eng["pe"], eng)

            @block.scalar
            def _(eng):
                run(self.eng["act"], eng)

            @block.vector
            def _(eng):
                run(self.eng["dve"], eng)

            @block.gpsimd
            def _(eng):
                run(self.eng["pool"], eng)

            @block.sync
            def _(eng):
                run(self.eng["sp"], eng)


class Arena:
    def __init__(self, nc, words):
        self.t = nc.alloc_sbuf_tensor("arena", [128, words], F32)
        self.words = words
        self.off = 0

    def _take(self, n):
        o = self.off
        self.off += n
        assert self.off <= self.words, f"arena overflow {self.off * 4} > {self.words * 4}"
        return o

    @staticmethod
    def _shape(v, dims):
        if len(dims) == 1:
            return v
        names = " ".join(f"d{i}" for i in range(len(dims)))
        kw = {f"d{i}": d for i, d in enumerate(dims[:-1])}
        return v.rearrange(f"p ({names}) -> p {names}", **kw)

    def f32(self, *dims):
        n = int(np.prod(dims))
        o = self._take(n)
        return self._shape(self.t[:, o:o + n], dims)

    def bf16(self, *dims):
        n = int(np.prod(dims))
        n4 = (n + 1) // 2
        o = self._take(n4)
        v = self.t[:, o:o + n4].bitcast(BF16)[:, 0:n]
        return self._shape(v, dims)


D = 1024
NTOK = 2304
NBLK = 18
OWN = 2048
DFF = 2816
NFF = 22
ALPHA = float(2.0 ** 0.25)
EPS = 1e-5
W_IN_COLS = 14 * 128 + 128
PC_IN, PC_CONV, PC_CW, PC_FW, PC_FB = 0, 14, 30, 154, 286
PC_N = 330

DEBUG = False
KA, KB, KSUB = 99, 99, 9

def build():
    nc = bass.Bass("TRN2", target_bir_lowering=False)
    dt = lambda n, s, k="ExternalInput": nc.dram_tensor(n, s, F32, kind=k).ap()
    xh_d = dt("xh", [NTOK, D])
    xT_d = dt("xT", [D, NTOK])
    flag_d = dt("flag", [128, 1])
    w_in_d = dt("w_in", [D, 1792])
    w_out_d = dt("w_out", [D, D])
    w_up_d = dt("w_up", [D, 2 * DFF])
    w_down_d = dt("w_down", [DFF, D])
    pcol_d = dt("pcol", [128, PC_N])
    bv_d = dt("bv", [1, 128])
    again_d = dt("again", [1, 512])
    bout_d = dt("bout", [1, D])
    ln1g_d = dt("ln1g", [1, D])
    ln1b_d = dt("ln1b", [1, D])
    ln2g_d = dt("ln2g", [1, D])
    ln2b_d = dt("ln2b", [1, D])
    sinks_d = dt("sinks", [1, 8])
    biasg_d = dt("biasg", [128, 2 * 8 * 128])
    maskc_d = dt("maskc", [128, 2 * 128])
    out_d = dt("out", [OWN, D], "ExternalOutput")
    x1s_d = dt("x1s", [NTOK - 128, D], "ExternalOutput" if DEBUG else "Internal")

    P = Prog(nc)
    A = Arena(nc, 53000)
    psum = nc.alloc_psum_tensor("psum", [128, 8, 512], F32)
    R = P.res
    Rps = [R(f"ps{i}") for i in range(8)]

    pcol = A.f32(PC_N); Rpcol = R()
    flag = A.f32(1); Rflag = R()
    ident = A.bf16(128); Rident = R()
    ones = A.f32(128); Rones = R()
    mhalf = A.f32(512); Rmhalf = R()
    halfb = A.f32(8); Rhalfb = R()
    small = A.f32(64); Rsmall = R()
    identf = A.f32(128); Ridentf = R()
    rdt = A.f32(8); Rrdt = R()
    keep_mark = A.off

    P.dma("sp", "ld_c", pcol, pcol_d, writes=[Rpcol])
    P.dma("sp", "ld_c", flag, flag_d, writes=[Rflag])
    P.op("pool", lambda e: e.memset(ident, 0.0), writes=[Rident])
    P.op("pool", lambda e: e.affine_select(out=ident, in_=ident, compare_op=ALU.not_equal, fill=1.0,
                                          base=0, pattern=[[-1, 128]], channel_multiplier=1),
         reads=[Rident], writes=[Rident])
    P.op("pool", lambda e: e.memset(ones, 1.0), writes=[Rones])
    P.op("pool", lambda e: e.tensor_copy(out=identf, in_=ident), reads=[Rident], writes=[Ridentf])
    P.op("pool", lambda e: e.memset(mhalf, -0.5), writes=[Rmhalf])
    P.op("dve", lambda e: e.tensor_scalar(out=halfb[:, 0:4], in0=pcol[:, PC_IN + 6:PC_IN + 10], scalar1=0.5, scalar2=None, op0=ALU.mult),
         reads=[Rpcol], writes=[Rhalfb])
    P.op("dve", lambda e: e.tensor_scalar(out=halfb[:, 4:8], in0=pcol[:, PC_IN + 10:PC_IN + 14], scalar1=0.5, scalar2=None, op0=ALU.mult),
         reads=[Rpcol], writes=[Rhalfb])

    w_in = A.bf16(8, W_IN_COLS); Rw_in = R()
    diag = A.bf16(124, 128); Rdiag = [R() for _ in range(4)]
    xT = [A.bf16(8, 512) for _ in range(2)]; RxT = [R(), R()]
    hT = [A.bf16(4, 30 + 512) for _ in range(2)]; RhT = [R(), R()]
    cf = A.f32(4, 512); Rcf = [R() for _ in range(4)]
    sq = [A.f32(512) for _ in range(2)]; Rsq = [R(), R()]
    rstd = A.f32(512); Rrstd = R()
    early_free_end = A.off
    EARLY_RES = [Rw_in] + Rdiag + RxT + RhT + Rcf + Rsq + [Rrstd]
    w_out = A.bf16(8, D); Rw_out = R()
    kT = A.bf16(2, 2, NTOK); RkT = R()
    vaug = A.bf16(NBLK, 2, 65); Rvaug = [R() for _ in range(NBLK)]
    EB = A.f32(2, 8, 128); REB = R()
    esink = A.f32(8); Resink = R()
    bv_bc = A.f32(128); Rbv = R()
    again_bc = A.f32(512); Ragain = R()
    bout_bc = A.f32(D); Rbout = R()
    ln1g_bc = A.f32(D); Rln1g = R()
    ln1b_bc = A.f32(D); Rln1b = R()
    qT = A.bf16(4, 512); RqT = R()
    Tb = A.f32(512); RTb = R()
    maskc = Tb[:, 0:256].rearrange("p (a q) -> p a q", a=2); Rmask = RTb
    ab = A.f32(512); Rab = R()
    mean = A.f32(512); Rmean = R()
    tmpf = A.f32(512); Rtmpf = R()
    yT = A.bf16(8, 512); RyT = [R() for _ in range(4)]
    Esb = [Tb, ab]; REsb = [RTb, Rab]
    Pt = [A.bf16(2, 2, 512) for _ in range(2)]
    RPt = [[[R(), R()], [R(), R()]] for _ in range(2)]
    yat = mean; Ryat = Rmean
    junk = tmpf; Rjunk = Rtmpf
    ya = A.bf16(512); Rya = R()
    xt = [A.f32(D) for _ in range(3)]; Rxt = [R(), R(), R()]
    lnst1 = A.f32(3, 16); Rlnst1 = [R(), R(), R()]
    att = A.f32(2, 16); Ratt = [R(), R()]

    wq = "pool"
    for k in range(8):
        P.dma(wq, "ld_x", xT[0][:, k, 0:256], xT_d[k * 128:(k + 1) * 128, 0:256], writes=[RxT[0]], par=True)
    _save = A.off
    A.off = keep_mark
    w_up = A.bf16(NFF, 8, 256); Rw_up = [R() for _ in range(NFF)]
    A.off = _save
    N_EARLY = min(NFF, (early_free_end - keep_mark) // 1024)
    w_up_v = w_up_d.rearrange("(k p) n -> p k n", p=128)

    def load_wup(p_, extra):
        for gi, off in enumerate((0, DFF)):
            c1 = off + p_ * 128
            P.dma(wq, "ld_wu", w_up[:, p_, :, gi * 128:(gi + 1) * 128], w_up_v[:, :, c1:c1 + 128],
                  writes=[Rw_up[p_]] + (extra if gi == 0 else []), par=True)
    w_in_v = w_in_d.rearrange("(k p) n -> p k n", p=128)
    Rw_ag = R()
    segs = [(768, 768, 512, Rw_ag), (1280, 1280, 512, Rw_ag), (0, 0, 512, Rw_in), (512, 512, 64, Rw_in), (576, 512, 64, Rw_in),
            (640, 576, 64, Rw_in), (704, 576, 64, Rw_in), (1792, 640, 128, Rw_in)]
    for si, (dst, src, n, rr) in enumerate(segs):
        if si == 2:
            P.op("pool", lambda e: e.memset(hT[0][:, :, 0:30], 0.0), writes=[RhT[0]])
            P.op("pool", lambda e: e.memset(kT[64:128, :, 0, :], 0.0), writes=[RkT])
            P.op("pool", lambda e: e.memset(kT[0:64, :, 1, :], 0.0), writes=[RkT])
        for k in range(8):
            P.dma(wq, "ld_w", w_in[:, k, dst:dst + n], w_in_v[:, k, src:src + n], writes=[rr], par=True)
    for k in range(8):
        P.dma(wq, "ld_w2", w_out[:, k, :], w_out_d[k * 128:(k + 1) * 128, :], writes=[Rw_out], par=True)

    P.dma("sp", "ld_c", bv_bc, bv_d.partition_broadcast(128), writes=[Rbv])
    P.dma("sp", "ld_c", again_bc, again_d.partition_broadcast(128), writes=[Ragain])
    P.dma("sp", "ld_c", bout_bc, bout_d.partition_broadcast(128), writes=[Rbout])
    P.dma("sp", "ld_c", ln1g_bc, ln1g_d.partition_broadcast(128), writes=[Rln1g])
    P.dma("sp", "ld_c", ln1b_bc, ln1b_d.partition_broadcast(128), writes=[Rln1b])
    P.dma("sp", "ld_c", esink, sinks_d.partition_broadcast(128), writes=[Resink])
    P.dma("sp", "ld_c", EB, biasg_d.rearrange("p (a h q) -> p a h q", a=2, h=8), writes=[REB])
    P.dma("sp", "ld_c", maskc, maskc_d.rearrange("p (a q) -> p a q", a=2), writes=[Rmask])
    P.op("act", lambda e: e.activation(out=esink, in_=esink, func=AF.Exp), reads=[Resink], writes=[Resink])
    P.op("act", lambda e: e.activation(out=EB, in_=EB, func=AF.Exp), reads=[REB], writes=[REB])
    for kb in range(2):
        P.op("dve", lambda e, kb=kb: e.tensor_tensor(out=EB[:, kb], in0=EB[:, kb],
                                                    in1=maskc[:, kb].unsqueeze(1).broadcast_to([128, 8, 128]), op=ALU.mult),
             reads=[REB, Rmask], writes=[REB])
    def build_diag(c):
        for j in range(31):
            i = c * 31 + j
            if j % 2 == 0:
                P.op("act", lambda e, i=i: e.activation(out=diag[:, i, :], in_=ident, func=AF.Identity,
                                                        scale=pcol[:, PC_CW + i:PC_CW + i + 1]),
                     reads=[Rident, Rpcol], writes=[Rdiag[c]])
            else:
                P.op("dve", lambda e, i=i: e.tensor_scalar(out=diag[:, i, :], in0=ident, scalar1=pcol[:, PC_CW + i:PC_CW + i + 1],
                                                           scalar2=None, op0=ALU.mult),
                     reads=[Rident, Rpcol], writes=[Rdiag[c]])
    P.op("dve", lambda e: e.memset(vaug[:, :, :, 64:65], 1.0), writes=Rvaug)

    ps_fm = [0, 1]
    ps_sc = [2, 3]
    ps_pv = [4, 5]
    ps_op = [6, 7]
    fm_ctr = [0]

    def fm_bank():
        b = ps_fm[fm_ctr[0] % 2]
        fm_ctr[0] += 1
        return b

    def rstd_from_var(NM_, bank):
        nbk = NM_ // 128
        t3 = tmpf[:, 0:NM_].rearrange("p (j t) -> p j t", j=nbk)
        P.op("dve", lambda e: e.tensor_tensor(out=t3, in0=t3, in1=identf.unsqueeze(1).broadcast_to([128, nbk, 128]), op=ALU.mult),
             reads=[Rtmpf, Ridentf], writes=[Rtmpf])
        P.op("dve", lambda e: e.tensor_reduce(out=rdt[:, 0:nbk], in_=t3, axis=mybir.AxisListType.X, op=ALU.add),
             reads=[Rtmpf], writes=[Rrdt])
        P.op("pool", lambda e: e.tensor_tensor(out=rdt[:, 4:4 + nbk], in0=rdt[:, 0:nbk], in1=mhalf[:, 0:nbk], op=ALU.pow),
             reads=[Rrdt, Rmhalf], writes=[Rrdt])
        P.op("dve", lambda e: e.tensor_tensor(out=t3, in0=identf.unsqueeze(1).broadcast_to([128, nbk, 128]),
                                              in1=rdt[:, 4:4 + nbk].unsqueeze(2).broadcast_to([128, nbk, 128]), op=ALU.mult),
             reads=[Ridentf, Rrdt], writes=[Rtmpf])
        for j_ in range(nbk):
            P.op("pe", lambda e, j_=j_: e.matmul(psum[:, bank, j_ * 128:(j_ + 1) * 128], lhsT=ones, rhs=tmpf[:, j_ * 128:(j_ + 1) * 128],
                                                 start=True, stop=True),
                 reads=[Rones, Rtmpf], writes=[Rps[bank]], inc=(j_ == nbk - 1))
        P.op("act", lambda e: e.activation(out=rstd[:, 0:NM_], in_=psum[:, bank, 0:NM_], func=AF.Identity),
             reads=[Rps[bank]], writes=[Rrstd])

    tiles = ([(0, 2)] + [(2 + 4 * i, 4) for i in range(4)])[:KA]

    def load_xT(ti_):
        b0_, nb_ = tiles[ti_]
        for k in range(8):
            P.dma(wq, "ld_x", xT[ti_ % 2][:, k, 0:nb_ * 128], xT_d[k * 128:(k + 1) * 128, b0_ * 128:(b0_ + nb_) * 128],
                  writes=[RxT[ti_ % 2]], par=True)

    carry = [None]

    def issue_xt(bi_):
        if 1 <= bi_ < NBLK:
            P.dma("sp", "ld_xt", xt[bi_ % 3], xh_d[bi_ * 128:(bi_ + 1) * 128, :], writes=[Rxt[bi_ % 3]])
    for bi_ in (1, 2, 3):
        issue_xt(bi_)
    def dense_ag(ti_, chunks):
        N_ = tiles[ti_][1] * 128
        xb_ = xT[ti_ % 2]; Rxb_ = RxT[ti_ % 2]
        hb_ = hT[ti_ % 2]; Rhb_ = RhT[ti_ % 2]

        def pc(ch):
            b = fm_bank()
            for k in range(8):
                P.op("pe", lambda e, b=b, k=k, ch=ch: e.matmul(psum[:, b, 0:N_], lhsT=w_in[:, k, ch * 128:(ch + 1) * 128],
                                                               rhs=xb_[:, k, 0:N_], start=(k == 0), stop=(k == 7)),
                     reads=[Rw_ag, Rxb_], writes=[Rps[b]], inc=(k == 7))
            return b
        if 0 in chunks and ti_ > 0:
            pb = hT[(ti_ - 1) % 2]; Npv = tiles[ti_ - 1][1] * 128
            P.op("pool", lambda e: e.tensor_copy(out=hb_[:, :, 0:30], in_=pb[:, :, Npv:Npv + 30]),
                 reads=[RhT[(ti_ - 1) % 2]], writes=[Rhb_])
        for c in chunks:
            b = pc(6 + c)
            P.op("act", lambda e, b=b, c=c: e.activation(out=ab[:, 0:N_], in_=psum[:, b, 0:N_], func=AF.Identity,
                                                         scale=0.5, bias=halfb[:, c:c + 1]),
                 reads=[Rps[b], Rhalfb], writes=[Rab])
            b2 = pc(10 + c)
            P.op("act", lambda e, b2=b2, c=c: e.activation(out=Tb[:, 0:N_], in_=psum[:, b2, 0:N_], func=AF.Tanh,
                                                           scale=0.5, bias=halfb[:, 4 + c:5 + c]),
                 reads=[Rps[b2], Rhalfb], writes=[RTb])
            P.op("dve", lambda e, c=c: e.scalar_tensor_tensor(out=hb_[:, c, 30:30 + N_], in0=Tb[:, 0:N_], scalar=1.0, in1=ab[:, 0:N_],
                                                              op0=ALU.add, op1=ALU.mult),
                 reads=[RTb, Rab], writes=[Rhb_])
            if ti_ == 0:
                build_diag(c)
        if 3 in chunks and ti_ == 0:
            P.op("dve", lambda e: e.tensor_scalar(out=hb_[:, :, 30:30 + N_], in0=hb_[:, :, 30:30 + N_], scalar1=flag[:, 0:1],
                                                  scalar2=None, op0=ALU.mult),
                 reads=[Rhb_, Rflag], writes=[Rhb_])

    def dense_conv(ti_, chunks):
        N_ = tiles[ti_][1] * 128
        hb_ = hT[ti_ % 2]; Rhb_ = RhT[ti_ % 2]
        c0_ = 128 if ti_ == 0 else 0
        NM_ = N_ - c0_
        for c in chunks:
            b = fm_bank()
            for j in range(31):
                P.op("pe", lambda e, b=b, c=c, j=j: e.matmul(psum[:, b, 0:NM_], lhsT=diag[:, c * 31 + j, :],
                                                             rhs=hb_[:, c, c0_ + j:c0_ + j + NM_], start=(j == 0), stop=(j == 30)),
                     reads=[Rdiag[c], Rhb_], writes=[Rps[b]], inc=(j == 30))
            P.op("act", lambda e, b=b, c=c: e.activation(out=cf[:, c, 0:NM_], in_=psum[:, b, 0:NM_], func=AF.Identity,
                                                         bias=pcol[:, PC_CONV + c:PC_CONV + c + 1]),
                 reads=[Rps[b], Rpcol], writes=[Rcf[c]])

    for ti, (b0, nb) in enumerate(tiles):
        N = nb * 128
        t0 = b0 * 128
        xb = xT[ti % 2]; Rxb = RxT[ti % 2]
        hb = hT[ti % 2]; Rhb = RhT[ti % 2]
        if ti == 0:
            if len(tiles) > 1:
                load_xT(1)

        def proj_chunk(ch):
            b = fm_bank()
            for k in range(8):
                P.op("pe", lambda e, b=b, k=k, ch=ch: e.matmul(psum[:, b, 0:N], lhsT=w_in[:, k, ch * 128:(ch + 1) * 128],
                                                               rhs=xb[:, k, 0:N], start=(k == 0), stop=(k == 7)),
                     reads=[Rw_ag if ch >= 6 else Rw_in, Rxb], writes=[Rps[b]], inc=(k == 7))
            return b
        def emit_qk():
            for c in range(4):
                b = proj_chunk(c)
                P.op("act", lambda e, b=b, c=c: e.activation(out=qT[:, c, 0:N], in_=psum[:, b, 0:N], func=AF.Identity,
                                                             bias=pcol[:, PC_IN + c:PC_IN + c + 1]),
                     reads=[Rps[b], Rpcol], writes=[RqT])
            for kv in range(2):
                b = proj_chunk(4 + kv)
                for hh in range(2):
                    pr = slice(hh * 64, (hh + 1) * 64)
                    P.op("act", lambda e, b=b, kv=kv, hh=hh, pr=pr: e.activation(
                        out=kT[pr, kv, hh, t0:t0 + N], in_=psum[pr, b, 0:N], func=AF.Identity,
                        bias=pcol[pr, PC_IN + 4 + kv:PC_IN + 5 + kv]),
                        reads=[Rps[b], Rpcol], writes=[RkT])

        if ti == 0:
            dense_ag(0, [0, 1, 2, 3])
        def emit_v():
            for j in range(nb):
                bi = b0 + j
                b = fm_bank()
                for k in range(8):
                    P.op("pe", lambda e, b=b, k=k, j=j: e.matmul(psum[:, b, 0:128], lhsT=xb[:, k, j * 128:(j + 1) * 128],
                                                                 rhs=w_in[:, k, 1792:1920], start=(k == 0), stop=(k == 7)),
                         reads=[Rw_in, Rxb], writes=[Rps[b]], inc=(k == 7))
                P.op("dve", lambda e, b=b, bi=bi: e.tensor_tensor(out=vaug[:, bi, :, 0:64],
                                                                  in0=psum[:, b, 0:128].rearrange("p (a d) -> p a d", a=2),
                                                                  in1=bv_bc.rearrange("p (a d) -> p a d", a=2), op=ALU.add),
                     reads=[Rps[b], Rbv], writes=[Rvaug[bi]])
                if ti == 0:
                    P.op("dve", lambda e, bi=bi: e.tensor_scalar(out=vaug[:, bi], in0=vaug[:, bi], scalar1=flag[:, 0:1],
                                                                 scalar2=None, op0=ALU.mult),
                         reads=[Rvaug[bi], Rflag], writes=[Rvaug[bi]])

        if carry[0] is not None:
            carry[0]()
            carry[0] = None
        c0 = 128 if ti == 0 else 0
        NM = N - c0
        if ti == 0:
            dense_conv(0, [0, 1, 2, 3])
        emit_qk()
        emit_v()
        def stats_gen():
            bS1 = fm_bank()
            for c in range(4):
                P.op("pe", lambda e, c=c: e.matmul(psum[:, bS1, 0:NM], lhsT=ones, rhs=cf[:, c, 0:NM], start=(c == 0), stop=(c == 3)),
                     reads=[Rones, Rcf[c]], writes=[Rps[bS1]], inc=(c == 3))
            bS2 = fm_bank()
            for c in range(4):
                s = sq[c % 2]; Rs = Rsq[c % 2]
                P.op("act", lambda e, c=c, s=s: e.activation(out=s[:, 0:NM], in_=cf[:, c, 0:NM], func=AF.Square),
                     reads=[Rcf[c]], writes=[Rs])
                P.op("pe", lambda e, c=c, s=s: e.matmul(psum[:, bS2, 0:NM], lhsT=ones, rhs=s[:, 0:NM], start=(c == 0), stop=(c == 3)),
                     reads=[Rones, Rs], writes=[Rps[bS2]], inc=True)
            P.op("act", lambda e: e.activation(out=mean[:, 0:NM], in_=psum[:, bS1, 0:NM], func=AF.Identity, scale=1.0 / 512),
                 reads=[Rps[bS1]], writes=[Rmean])
            P.op("dve", lambda e: e.tensor_tensor(out=tmpf[:, 0:NM], in0=mean[:, 0:NM], in1=mean[:, 0:NM], op=ALU.mult),
                 reads=[Rmean], writes=[Rtmpf])
            P.op("dve", lambda e: e.scalar_tensor_tensor(out=tmpf[:, 0:NM], in0=psum[:, bS2, 0:NM], scalar=1.0 / 512, in1=tmpf[:, 0:NM],
                                                         op0=ALU.mult, op1=ALU.subtract),
                 reads=[Rps[bS2], Rtmpf], writes=[Rtmpf])
            P.op("dve", lambda e: e.tensor_scalar(out=tmpf[:, 0:NM], in0=tmpf[:, 0:NM], scalar1=EPS, scalar2=None, op0=ALU.add),
                 reads=[Rtmpf], writes=[Rtmpf])
            yield
            rstd_from_var(NM, bS1)
            bS3 = fm_bank()
            for c in range(4):
                P.op("dve", lambda e, c=c: e.tensor_tensor(out=cf[:, c, 0:NM], in0=cf[:, c, 0:NM], in1=mean[:, 0:NM], op=ALU.subtract),
                     reads=[Rcf[c], Rmean], writes=[Rcf[c]])
                P.op("dve", lambda e, c=c: e.tensor_tensor(out=cf[:, c, 0:NM], in0=cf[:, c, 0:NM], in1=rstd[:, 0:NM], op=ALU.mult),
                     reads=[Rcf[c], Rrstd], writes=[Rcf[c]])
                P.op("act", lambda e, c=c: e.activation(out=cf[:, c, 0:NM], in_=cf[:, c, 0:NM], func=AF.Identity,
                                                        scale=pcol[:, PC_CONV + 4 + c:PC_CONV + 5 + c],
                                                        bias=pcol[:, PC_CONV + 8 + c:PC_CONV + 9 + c]),
                     reads=[Rcf[c], Rpcol], writes=[Rcf[c]])
                P.op("act", lambda e, c=c: e.activation(out=Tb[:, 0:NM], in_=cf[:, c, 0:NM], func=AF.Tanh, scale=0.5),
                     reads=[Rcf[c]], writes=[RTb])
                P.op("dve", lambda e, c=c: e.scalar_tensor_tensor(out=cf[:, c, 0:NM], in0=Tb[:, 0:NM], scalar=1.0, in1=cf[:, c, 0:NM],
                                                                  op0=ALU.add, op1=ALU.mult),
                     reads=[RTb, Rcf[c]], writes=[Rcf[c]])
                s = sq[c % 2]; Rs = Rsq[c % 2]
                P.op("act", lambda e, c=c, s=s: e.activation(out=s[:, 0:NM], in_=cf[:, c, 0:NM], func=AF.Square),
                     reads=[Rcf[c]], writes=[Rs])
                P.op("pe", lambda e, c=c, s=s: e.matmul(psum[:, bS3, 0:NM], lhsT=ones, rhs=s[:, 0:NM], start=(c == 0), stop=(c == 3)),
                     reads=[Rones, Rs], writes=[Rps[bS3]], inc=True)
            yield
            P.op("dve", lambda e: e.tensor_scalar(out=tmpf[:, 0:NM], in0=psum[:, bS3, 0:NM], scalar1=1.0 / 512, scalar2=4 * EPS,
                                                  op0=ALU.mult, op1=ALU.add),
                 reads=[Rps[bS3]], writes=[Rtmpf])
            rstd_from_var(NM, bS2)
            RyT_all = RyT[0:nb]
            for c in range(4):
                P.op("dve", lambda e, c=c: e.scalar_tensor_tensor(out=yT[:, 4 + c, c0:c0 + NM], in0=cf[:, c, 0:NM],
                                                                  scalar=pcol[:, PC_CONV + 12 + c:PC_CONV + 13 + c],
                                                                  in1=rstd[:, 0:NM], op0=ALU.mult, op1=ALU.mult),
                     reads=[Rcf[c], Rpcol, Rrstd], writes=RyT_all)

            yield

        if ti + 2 < len(tiles):
            load_xT(ti + 2)
        pre_list = list(range(N_EARLY)) if (ti == len(tiles) - 1 and KB > 0) else []
        def S0(j):
            bi = b0 + j
            qc0 = j * 128
            Pb = Pt[bi % 2]; RPb = RPt[bi % 2]
            for kvh in range(2):
                for kb in range(2):
                    kbi = bi - 1 + kb
                    sb_ = ps_sc[(kvh * 2 + kb) % 2]
                    for g in range(4):
                        h = kvh * 4 + g
                        half = h % 2
                        P.op("pe", lambda e, sb_=sb_, g=g, kvh=kvh, kbi=kbi, half=half, h=h: e.matmul(
                            psum[:, sb_, g * 128:(g + 1) * 128],
                            lhsT=kT[:, kvh, half, kbi * 128:(kbi + 1) * 128],
                            rhs=qT[:, h // 2, qc0:qc0 + 128], start=True, stop=True),
                            reads=[RkT, RqT], writes=[Rps[sb_]], inc=(g == 3))
                    es = Esb[(kvh * 2 + kb) % 2]; Res_ = REsb[(kvh * 2 + kb) % 2]
                    P.op("act", lambda e, sb_=sb_, es=es: e.activation(out=es, in_=psum[:, sb_, :], func=AF.Exp, scale=0.125),
                         reads=[Rps[sb_]], writes=[Res_])
                    P.op("dve", lambda e, es=es, kvh=kvh, kb=kb: e.tensor_tensor(
                        out=Pb[:, kvh, kb, :].rearrange("p (g q) -> p g q", g=4),
                        in0=es.rearrange("p (g q) -> p g q", g=4),
                        in1=EB[:, kb, kvh * 4:(kvh + 1) * 4, :], op=ALU.mult),
                        reads=[Res_, REB], writes=[RPb[kvh][kb]])

        def S1(j, part=None):
            bi = b0 + j
            qc0 = j * 128
            if part in (None, 0):
                S1a(j)
            if part in (None, 1):
                S1b(j)

        def S1a(j):
            bi = b0 + j
            qc0 = j * 128
            Pb = Pt[bi % 2]; RPb = RPt[bi % 2]
            sm = att[:, bi % 2, :]; Rsm = Ratt[bi % 2]
            for h in range(8):
                kvh, g = h // 4, h % 4
                pb_ = ps_pv[h // 4]
                for kb in range(2):
                    kbi = bi - 1 + kb
                    P.op("pe", lambda e, pb_=pb_, g=g, kvh=kvh, kb=kb, kbi=kbi: e.matmul(
                        psum[:, pb_, g * 65:(g + 1) * 65], lhsT=Pb[:, kvh, kb, g * 128:(g + 1) * 128],
                        rhs=vaug[:, kbi, kvh, :], start=(kb == 0), stop=(kb == 1)),
                        reads=[RPb[kvh][kb], Rvaug[kbi]], writes=[Rps[pb_]], inc=(kb == 1 and g == 3))
            pvv = psum[:, 4:6, 0:260].rearrange("p a (g e) -> p a g e", e=65)
            den = sm[:, 0:8]
            P.op("dve", lambda e: e.tensor_tensor(out=den.rearrange("p (a g) -> p a g", a=2), in0=pvv[:, :, :, 64],
                                                  in1=esink.rearrange("p (a g) -> p a g", a=2), op=ALU.add),
                 reads=[Rps[4], Rps[5], Resink], writes=[Rsm])
            P.op("dve", lambda e: e.reciprocal(out=den, in_=den), reads=[Rsm], writes=[Rsm])
            P.op("dve", lambda e: e.tensor_tensor(out=yat.rearrange("p (a g d) -> p a g d", a=2, g=4), in0=pvv[:, :, :, 0:64],
                                                  in1=den.rearrange("p (a g) -> p a g", a=2).unsqueeze(3).broadcast_to([128, 2, 4, 64]),
                                                  op=ALU.mult),
                 reads=[Rps[4], Rps[5], Rsm], writes=[Ryat])
            ss = sm[:, 8:9]
            P.op("act", lambda e: e.activation(out=junk, in_=yat, func=AF.Square, accum_out=ss),
                 reads=[Ryat], writes=[Rjunk, Rsm])
            P.op("dve", lambda e: e.tensor_scalar(out=sm[:, 9:10], in0=ss, scalar1=1.0 / 512, scalar2=EPS, op0=ALU.mult, op1=ALU.add),
                 reads=[Rsm], writes=[Rsm])
            P.op("pool", lambda e: e.tensor_tensor(out=sm[:, 10:11], in0=sm[:, 9:10], in1=mhalf[:, 0:1], op=ALU.pow),
                 reads=[Rsm, Rmhalf], writes=[Rsm])
            P.op("dve", lambda e: e.scalar_tensor_tensor(out=ya, in0=yat, scalar=sm[:, 10:11], in1=again_bc, op0=ALU.mult, op1=ALU.mult),
                 reads=[Ryat, Rsm, Ragain], writes=[Rya])

        def S1b(j):
            bi = b0 + j
            qc0 = j * 128
            for c in range(4):
                pb_ = ps_pv[c // 2]
                tv = psum[:, pb_, 384:512].bitcast(BF16)
                P.op("pe", lambda e, tv=tv, c=c: e.transpose(tv[:, (c % 2) * 128:(c % 2 + 1) * 128], ya[:, c * 128:(c + 1) * 128], ident),
                     reads=[Rya, Rident], writes=[Rps[pb_]], inc=True)
            for a in range(2):
                tv = psum[:, ps_pv[a], 384:512].bitcast(BF16)
                P.op("act", lambda e, tv=tv, a=a: e.activation(out=yT[:, 2 * a:2 * a + 2, qc0:qc0 + 128],
                                                               in_=tv.rearrange("p (c t) -> p c t", c=2), func=AF.Identity),
                     reads=[Rps[ps_pv[a]]], writes=[RyT[j]])

        def S2a(j):
            bi = b0 + j
            qc0 = j * 128
            xx = xt[bi % 3]; Rxx = Rxt[bi % 3]
            P.op("dve", lambda e: e.scalar_tensor_tensor(out=xx, in0=xx, scalar=ALPHA, in1=bout_bc, op0=ALU.mult, op1=ALU.add),
                 reads=[Rxx, Rbout], writes=[Rxx])
            for hf in range(2):
                ob = ps_op[hf]
                for kc in range(8):
                    P.op("pe", lambda e, ob=ob, kc=kc, hf=hf: e.matmul(psum[:, ob, :], lhsT=yT[:, kc, qc0:qc0 + 128],
                                                                       rhs=w_out[:, kc, hf * 512:(hf + 1) * 512],
                                                                       start=(kc == 0), stop=(kc == 7)),
                         reads=[RyT[j], Rw_out], writes=[Rps[ob]], inc=(kc == 7))
                P.op("dve", lambda e, ob=ob, hf=hf: e.tensor_tensor(out=xx[:, hf * 512:(hf + 1) * 512], in0=xx[:, hf * 512:(hf + 1) * 512],
                                                                    in1=psum[:, ob, :], op=ALU.add),
                     reads=[Rxx, Rps[ob]], writes=[Rxx])

            def tail():
                ln_tail(P, nc, xx, Rxx, lnst1[:, bi % 3, :], Rlnst1[bi % 3], mhalf, Rmhalf, ln1g_bc, Rln1g, ln1b_bc, Rln1b, 0, gb_eng="dve")
                P.dma("pool", "st_x1", x1s_d[(bi - 1) * 128:bi * 128, :], xx, reads=[Rxx])
                issue_xt(bi + 3)
            return tail

        blocks = [j for j in range(nb) if b0 + j != 0]
        pending = None
        sg_ = stats_gen()
        next(sg_)
        S0(blocks[0])
        next(sg_)
        if len(blocks) > 1:
            S0(blocks[1])
        S1(blocks[0])
        for _ in sg_:
            pass
        for idx, j in enumerate(blocks):
            if idx + 2 < len(blocks):
                S0(blocks[idx + 2])
            if idx + 1 < len(blocks):
                S1(blocks[idx + 1], 0)
            tl = S2a(j)
            if pending is not None:
                pending()
            pending = tl
            if ti + 1 < len(tiles):
                nxt = [("ag", [0, 1]), ("ag", [2, 3]), ("conv", [0, 1]), ("conv", [2, 3])]
                per = (len(nxt) + len(blocks) - 1) // len(blocks)
                for kind_, chs_ in nxt[idx * per:(idx + 1) * per]:
                    (dense_ag if kind_ == "ag" else dense_conv)(ti + 1, chs_)
            if idx + 1 < len(blocks):
                S1(blocks[idx + 1], 1)
            npre = (len(pre_list) + (len(blocks) - idx) - 1) // (len(blocks) - idx)
            for p_ in pre_list[:npre]:
                load_wup(p_, EARLY_RES + [Rw_ag])
            pre_list = pre_list[npre:]
        if ti == len(tiles) - 1:
            if pending is not None:
                pending()
        else:
            carry[0] = pending

    P.barrier()
    A.off = keep_mark
    A.bf16(NFF, 8, 256)
    w_dn = A.bf16(NFF, D); Rw_dn = R()
    ln2g_bc = A.f32(D); Rln2g = R()
    ln2b_bc = A.f32(D); Rln2b = R()
    x1b = A.bf16(3, D); Rx1b = R()
    x1T = [A.bf16(8, 386) for _ in range(2)]; Rx1T = [R(), R()]
    tg = [A.f32(386) for _ in range(3)]; Rtg = [R(), R(), R()]
    tu = [A.f32(386) for _ in range(3)]; Rtu = [R(), R(), R()]
    sg = [A.f32(384) for _ in range(3)]; Rsg = [R(), R(), R()]
    actb = A.bf16(NFF, 384); Ract = [R() for _ in range(3)]
    xr = [A.f32(D) for _ in range(3)]; Rxr = [R(), R(), R()]
    lnst = A.f32(3, 16); Rlnst = [R(), R(), R()]

    LATE_WUP = list(range(N_EARLY, NFF))
    P.dma("sp", "ld_c2", ln2g_bc, ln2g_d.partition_broadcast(128), writes=[Rln2g])
    P.dma("sp", "ld_c2", ln2b_bc, ln2b_d.partition_broadcast(128), writes=[Rln2b])

    ps_tr = 6
    ps_up = [[0, 1], [2, 3], [4, 5]]
    ps_dn = [5, 6]

    def load_transpose(row0, nblk, dstT, RdstT, col0):
        for j in range(nblk):
            P.dma(wq, "ld_x1b", x1b[:, j, :], x1s_d[row0 + j * 128:row0 + (j + 1) * 128, :], writes=[Rx1b], par=True)
        for j in range(nblk):
            tv = psum[:, ps_tr, :].bitcast(BF16)
            for k in range(8):
                P.op("pe", lambda e, j=j, k=k, tv=tv: e.transpose(tv[:, k * 128:(k + 1) * 128], x1b[:, j, k * 128:(k + 1) * 128], ident),
                     reads=[Rx1b, Rident], writes=[Rps[ps_tr]], inc=(k == 7))
            P.op("act", lambda e, j=j, tv=tv: e.activation(out=dstT[:, :, col0 + j * 128:col0 + (j + 1) * 128],
                                                           in_=tv.rearrange("p (k t) -> p k t", k=8), func=AF.Identity),
                 reads=[Rps[ps_tr]], writes=[RdstT])

    ftiles = [(384 * i, 384) for i in range(5)] + [(1920, 128)]
    ftiles = ftiles[:KB]
    deferred = []
    dn_banks = [6, 7]
    dn_ctr = [0]

    def prep_tile(fi):
        tk0, T = ftiles[fi]
        cur = x1T[fi % 2]; Rcur = Rx1T[fi % 2]
        if fi == 0:
            load_transpose(0, 1, cur, Rcur, 2)
            P.op("dve", lambda e: e.tensor_scalar(out=cur[:, :, 0:2], in0=cur[:, :, 128:130], scalar1=flag[:, 0:1],
                                                  scalar2=None, op0=ALU.mult),
                 reads=[Rcur, Rflag], writes=[Rcur])
        else:
            pT = x1T[(fi - 1) % 2]; Tp = ftiles[fi - 1][1]
            P.op("dve", lambda e: e.tensor_copy(out=cur[:, :, 0:2], in_=pT[:, :, Tp:Tp + 2]),
                 reads=[Rx1T[(fi - 1) % 2]], writes=[Rcur])
        load_transpose(128 + tk0, T // 128, cur, Rcur, 2)

    def issue_xr(fi_, j_):
        if fi_ < len(ftiles) and j_ < ftiles[fi_][1] // 128:
            gb_ = ftiles[fi_][0] // 128 + j_
            P.dma("sp", "ld_xr", xr[j_], x1s_d[128 + gb_ * 128:256 + gb_ * 128, :], writes=[Rxr[j_]])

    if ftiles:
        prep_tile(0)
        for j_ in range(3):
            issue_xr(0, j_)
    for p_ in LATE_WUP:
        load_wup(p_, [])
    for p_ in range(NFF):
        P.dma(wq, "ld_wd", w_dn[:, p_, :], w_down_d[p_ * 128:(p_ + 1) * 128, :], writes=[Rw_dn], par=True)
    def pair(fi_, p_, part):
        T_ = ftiles[fi_][1]
        nblk_ = T_ // 128
        NC_ = T_ + 2
        cur_ = x1T[fi_ % 2]; Rcur_ = Rx1T[fi_ % 2]
        bg, bu = ps_up[p_ % 3]
        tgb, Rtgb = tg[p_ % 3], Rtg[p_ % 3]
        tub, Rtub = tu[p_ % 3], Rtu[p_ % 3]
        sgb, Rsgb = sg[p_ % 3], Rsg[p_ % 3]
        if part == "A":
            for gi, bb in ((0, bg), (1, bu)):
                for k in range(8):
                    P.op("pe", lambda e, bb=bb, k=k, gi=gi: e.matmul(psum[:, bb, 0:NC_], lhsT=w_up[:, p_, k, gi * 128:(gi + 1) * 128],
                                                                     rhs=cur_[:, k, 0:NC_], start=(k == 0), stop=(k == 7)),
                         reads=[Rw_up[p_], Rcur_], writes=[Rps[bb]], inc=(k == 7))
            for (tb, Rtb, bb, ch) in ((tgb, Rtgb, bg, p_), (tub, Rtub, bu, NFF + p_)):
                wc = PC_FW + ch * 3
                P.op("act", lambda e, tb=tb, bb=bb, wc=wc, ch=ch: e.activation(
                    out=tb[:, 0:T_], in_=psum[:, bb, 2:NC_], func=AF.Identity,
                    scale=pcol[:, wc + 2:wc + 3], bias=pcol[:, PC_FB + ch:PC_FB + ch + 1]),
                    reads=[Rps[bb], Rpcol], writes=[Rtb])
                P.op("dve", lambda e, tb=tb, bb=bb, wc=wc: e.scalar_tensor_tensor(
                    out=tb[:, 0:T_], in0=psum[:, bb, 1:NC_ - 1], scalar=pcol[:, wc + 1:wc + 2], in1=tb[:, 0:T_],
                    op0=ALU.mult, op1=ALU.add), reads=[Rps[bb], Rpcol, Rtb], writes=[Rtb])
                P.op("dve", lambda e, tb=tb, bb=bb, wc=wc: e.scalar_tensor_tensor(
                    out=tb[:, 0:T_], in0=psum[:, bb, 0:NC_ - 2], scalar=pcol[:, wc:wc + 1], in1=tb[:, 0:T_],
                    op0=ALU.mult, op1=ALU.add), reads=[Rps[bb], Rpcol, Rtb], writes=[Rtb])
            P.op("act", lambda e: e.activation(out=sgb[:, 0:T_], in_=tgb[:, 0:T_], func=AF.Silu),
                 reads=[Rtgb], writes=[Rsgb])
        else:
            P.op("pool", lambda e: e.tensor_tensor(out=actb[:, p_, 0:T_], in0=sgb[:, 0:T_], in1=tub[:, 0:T_], op=ALU.mult),
                 reads=[Rsgb, Rtub], writes=Ract[0:nblk_])

    NH = 2
    for fi, (tk0, T) in enumerate(ftiles):
        nblk = T // 128
        cur = x1T[fi % 2]; Rcur = Rx1T[fi % 2]
        NC = T + 2
        for p_ in range(NH if fi > 0 else 0, NFF):
            if p_ == 3 and fi + 1 < len(ftiles):
                prep_tile(fi + 1)
            pair(fi, p_, "A")
            pair(fi, p_, "B")
            if p_ in (6, 11, 16) and deferred:
                deferred.pop(0)()
        if fi + 1 < len(ftiles):
            for p_ in range(NH):
                pair(fi + 1, p_, "A")
        for j in range(nblk):
            gb = (tk0 // 128) + j
            xx = xr[j]; Rxx = Rxr[j]
            for hf in range(2):
                ob = dn_banks[dn_ctr[0] % 2]
                dn_ctr[0] += 1
                for p_ in range(NFF):
                    P.op("pe", lambda e, ob=ob, p_=p_, hf=hf, j=j: e.matmul(psum[:, ob, :], lhsT=actb[:, p_, j * 128:(j + 1) * 128],
                                                                            rhs=w_dn[:, p_, hf * 512:(hf + 1) * 512],
                                                                            start=(p_ == 0), stop=(p_ == NFF - 1)),
                         reads=[Ract[j], Rw_dn], writes=[Rps[ob]], inc=(p_ == NFF - 1))
                P.op("dve", lambda e, ob=ob, hf=hf: e.scalar_tensor_tensor(out=xx[:, hf * 512:(hf + 1) * 512],
                                                                           in0=xx[:, hf * 512:(hf + 1) * 512], scalar=ALPHA,
                                                                           in1=psum[:, ob, :], op0=ALU.mult, op1=ALU.add),
                     reads=[Rxx, Rps[ob]], writes=[Rxx])

            def tail(xx=xx, Rxx=Rxx, j=j, gb=gb, fi=fi):
                ln_tail(P, nc, xx, Rxx, lnst[:, j, :], Rlnst[j], mhalf, Rmhalf, ln2g_bc, Rln2g, ln2b_bc, Rln2b, 0)
                P.dma("pool", "st_out", out_d[gb * 128:(gb + 1) * 128, :], xx, reads=[Rxx])
                issue_xr(fi + 1, j)
            deferred.append(tail)
        if fi + 1 < len(ftiles):
            for p_ in range(NH):
                pair(fi + 1, p_, "B")
    while deferred:
        deferred.pop(0)()

    P.wait_all("sp")
    P.wait_all("pool")
    P.emit()
    return nc


def ln_tail(P, nc, xx, Rxx, sc, Rsc, mhalf, Rmhalf, g_bc, Rg, b_bc, Rb, so, gb_eng="pool"):
    st = sc[:, so:so + 12]
    mv = sc[:, so + 12:so + 14]
    P.op("dve", lambda e: e.bn_stats(out=st[:, 0:6], in_=xx[:, 0:512]), reads=[Rxx], writes=[Rsc])
    P.op("dve", lambda e: e.bn_stats(out=st[:, 6:12], in_=xx[:, 512:1024]), reads=[Rxx], writes=[Rsc])
    P.op("dve", lambda e: e.bn_aggr(out=mv, in_=st), reads=[Rsc], writes=[Rsc])
    P.op("dve", lambda e: e.tensor_scalar(out=sc[:, so + 14:so + 15], in0=mv[:, 1:2], scalar1=EPS, scalar2=None, op0=ALU.add),
         reads=[Rsc], writes=[Rsc])
    P.op("pool", lambda e: e.tensor_tensor(out=sc[:, so + 15:so + 16], in0=sc[:, so + 14:so + 15], in1=mhalf[:, 0:1], op=ALU.pow),
         reads=[Rsc, Rmhalf], writes=[Rsc])
    P.op("dve", lambda e: e.scalar_tensor_tensor(out=sc[:, so + 14:so + 15], in0=mv[:, 0:1], scalar=-1.0, in1=sc[:, so + 15:so + 16],
                                                 op0=ALU.mult, op1=ALU.mult),
         reads=[Rsc], writes=[Rsc])
    P.op("act", lambda e: e.activation(out=xx, in_=xx, func=AF.Identity, scale=sc[:, so + 15:so + 16], bias=sc[:, so + 14:so + 15]),
         reads=[Rxx, Rsc], writes=[Rxx])
    P.op(gb_eng, lambda e: e.tensor_tensor(out=xx, in0=xx, in1=g_bc, op=ALU.mult), reads=[Rxx, Rg], writes=[Rxx])
    P.op(gb_eng, lambda e: e.tensor_tensor(out=xx, in0=xx, in1=b_bc, op=ALU.add), reads=[Rxx, Rb], writes=[Rxx])


def _bucket_idx():
    q = np.arange(128)[:, None]
    kc = np.arange(256)[None, :]
    dist = q + 128 - kc
    n = np.maximum(dist, 0)
    nf = np.maximum(n, 16).astype(np.float32)
    large = 16 + (np.log(nf / np.float32(16)) / np.float32(math.log(128 / 16)) * np.float32(16)).astype(np.int32)
    large = np.minimum(large, 31)
    bucket = np.where(n < 16, n, large)
    ok = (dist >= 0) & (dist < 128)
    return bucket, ok


def _cols(v, nchunk):
    return np.ascontiguousarray(np.asarray(v, np.float32).reshape(nchunk, 128).T)


_NC_CACHE = {}


def kernel(x, w_in, b_in, attn_sinks, rel_bias_table, conv_dw_w, conv_dw_b, conv_ln_g, conv_ln_b,
           attn_out_gain, conv_out_gain, w_out, b_out, ln1_g, ln1_b, w_up, ffn_dw_w, ffn_dw_b, w_down,
           ln2_g, ln2_b):
    f = lambda a: np.asarray(a, dtype=np.float32)
    x = f(x)
    w_in2, b_in1 = f(w_in)[0], f(b_in)[0]
    bq, bk, bvv, ba, bg = b_in1[0:512], b_in1[512:640], b_in1[640:768], b_in1[768:1280], b_in1[1280:1792]
    pcol = np.zeros((128, PC_N), np.float32)
    pcol[:, 0:4] = _cols(bq, 4)
    pcol[:, 4] = np.concatenate([bk[0:64], bk[0:64]])
    pcol[:, 5] = np.concatenate([bk[64:128], bk[64:128]])
    pcol[:, 6:10] = _cols(ba, 4)
    pcol[:, 10:14] = _cols(bg, 4)
    pcol[:, 14:18] = _cols(f(conv_dw_b)[0], 4)
    pcol[:, 18:22] = _cols(f(conv_ln_g)[0], 4)
    pcol[:, 22:26] = _cols(f(conv_ln_b)[0], 4)
    pcol[:, 26:30] = _cols(f(conv_out_gain)[0], 4)
    cw = f(conv_dw_w)[0]
    pcol[:, PC_CW:PC_CW + 124] = cw.reshape(31, 4, 128).transpose(2, 1, 0).reshape(128, 124)
    fw = f(ffn_dw_w)[0]
    pcol[:, PC_FW:PC_FW + 132] = fw.reshape(3, 44, 128).transpose(2, 1, 0).reshape(128, 132)
    pcol[:, PC_FB:PC_FB + 44] = _cols(f(ffn_dw_b)[0], 44)
    bucket, ok = _bucket_idx()
    tab = f(rel_bias_table)
    bias_full = tab[bucket]
    biasg = np.ascontiguousarray(bias_full.reshape(128, 2, 128, 8).transpose(2, 1, 3, 0)).reshape(128, 2 * 8 * 128)
    maskc = np.ascontiguousarray(ok.astype(np.float32).reshape(128, 2, 128).transpose(2, 1, 0)).reshape(128, 256)
    shared = {
        "w_in": w_in2, "w_out": f(w_out)[0], "w_up": f(w_up)[0], "w_down": f(w_down)[0], "pcol": pcol,
        "bv": bvv.reshape(1, 128), "again": f(attn_out_gain), "bout": f(b_out), "ln1g": f(ln1_g), "ln1b": f(ln1_b),
        "ln2g": f(ln2_g), "ln2b": f(ln2_b), "sinks": f(attn_sinks), "biasg": biasg, "maskc": maskc,
    }
    in_maps = []
    for core in range(8):
        b, c = core // 4, core % 4
        s0 = c * OWN
        xh = np.zeros((NTOK, D), np.float32)
        lo = s0 - 256
        if lo >= 0:
            xh[:] = x[b, lo:s0 + OWN]
        else:
            xh[256:] = x[b, 0:OWN]
        m = dict(shared)
        m["xh"] = xh
        m["xT"] = np.ascontiguousarray(xh.T)
        m["flag"] = np.full((128, 1), 1.0 if c > 0 else 0.0, np.float32)
        in_maps.append(m)
    if "nc" not in _NC_CACHE:
        _NC_CACHE["nc"] = build()
    res = run_bass_kernel_spmd(_NC_CACHE["nc"], in_maps, core_ids=list(range(8)))
    out = np.empty((2, 8192, D), np.float32)
    for core in range(8):
        b, c = core // 4, core % 4
        out[b, c * OWN:(c + 1) * OWN] = res.results[core]["out"]
    if DEBUG:
        kernel.dbg = [r for r in res.results]
    return out
```
